# Optimizing a Trainium2 kernel written in Bass

```python
import math
import jax, jax.numpy as jnp
from jax import lax
import numpy as np

D_MODEL = 1024
BATCH = 8
SEQ = 2048
DEPTH = 2
DEC_BATCH = 128
DEC_SEQ = 4
PAST_LEN = 16384
PAGE_SIZE = 128

N_MIXERS = 2
N_A_LAYERS = (DEPTH + N_MIXERS - 1) // N_MIXERS
N_B_LAYERS = DEPTH // N_MIXERS
BRANCH = 3 * D_MODEL // 2
N_XHEADS = 4
XHEAD_DIM = 128
XATT = N_XHEADS * XHEAD_DIM
MIX_WIDTH = BRANCH + XATT
N_MEM = 256
CHUNK = 128
A_GROUPS = 8
A_GDIM = BRANCH // A_GROUPS
SSM_GCH = 16
SSM_GROUPS = BRANCH // SSM_GCH
SSM_STATE = 64
DT_MIN = 1e-3
DT_MAX = 1e-1
EPS = 1e-6

kernel_name = "hybrid_gmlp_s5_memxattn_step"


def rms_norm(x, g):
    xf = x.astype(jnp.float32)
    y = xf * lax.rsqrt(jnp.mean(xf * xf, axis=-1, keepdims=True) + EPS)
    return (y * g.astype(jnp.float32)).astype(x.dtype)


def layer_norm(x, g, b):
    xf = x.astype(jnp.float32)
    mu = jnp.mean(xf, axis=-1, keepdims=True)
    var = jnp.mean(jnp.square(xf - mu), axis=-1, keepdims=True)
    y = (xf - mu) * lax.rsqrt(var + EPS) * g.astype(jnp.float32) + b.astype(jnp.float32)
    return y.astype(x.dtype)


def memory_kv(mem, g, w_k, w_v):
    m = rms_norm(mem, g)
    k = (m @ w_k).reshape(mem.shape[0], mem.shape[1], N_XHEADS, XHEAD_DIM)
    v = (m @ w_v).reshape(mem.shape[0], mem.shape[1], N_XHEADS, XHEAD_DIM)
    return k, v


def cross_attend(q, k, v):
    s = jnp.einsum("blhd,bmhd->bhlm", q.astype(jnp.float32), k.astype(jnp.float32)) * (XHEAD_DIM ** -0.5)
    p = jax.nn.softmax(s, axis=-1)
    o = jnp.einsum("bhlm,bmhd->blhd", p, v.astype(jnp.float32))
    return o.reshape(q.shape[0], q.shape[1], XATT).astype(q.dtype)


def chunk_gating(z_uv, ln_g, ln_b, w_s, b_s):
    uv = jax.nn.gelu(z_uv)
    u, v = uv[..., :BRANCH], uv[..., BRANCH:]
    v = layer_norm(v, ln_g, ln_b)
    bsz, l, _ = v.shape
    n = min(l, CHUNK)
    nc = l // n
    mask = jnp.tril(jnp.ones((n, n), dtype=bool))
    w = jnp.where(mask, w_s[:, :n, :n], jnp.zeros((), w_s.dtype))
    vc = v.reshape(bsz, nc, n, A_GROUPS, A_GDIM)
    mixed = jnp.einsum("gts,bcsgd->bctgd", w, vc) + jnp.transpose(b_s[:, :n])[:, :, None]
    return u * mixed.reshape(bsz, l, BRANCH), v


def _combine(e1, e2):
    a1, b1 = e1
    a2, b2 = e2
    return a1 * a2, a2 * b1 + b2


def s5_branch(u, h0, lam_re, lam_im, log_dt, b_re, b_im, c_re, c_im, d, w_glu, b_glu):
    f32 = jnp.float32
    lam = lax.complex(lam_re.astype(f32), lam_im.astype(f32))
    dt = jnp.exp(log_dt.astype(f32))[:, None]
    lam_bar = jnp.exp(lam * dt)
    b_bar = ((lam_bar - 1.0) / lam)[..., None] * lax.complex(b_re.astype(f32), b_im.astype(f32))
    cr, ci = c_re.astype(f32), c_im.astype(f32)
    dg = d.astype(f32).reshape(SSM_GROUPS, SSM_GCH)
    bsz, l, _ = u.shape
    n = min(l, CHUNK)
    nc = l // n
    u_blocks = u.astype(f32).reshape(bsz, nc, n, SSM_GROUPS, SSM_GCH).transpose(1, 2, 0, 3, 4)
    a = jnp.broadcast_to(lam_bar, (n, 1, SSM_GROUPS, SSM_STATE))

    def step(h, uc):
        bu = jnp.einsum("tbgc,gpc->tbgp", uc.astype(jnp.complex64), b_bar)
        a_cum, hs = lax.associative_scan(_combine, (a, bu), axis=0)
        hs = hs + a_cum * h[None]
        yc = (jnp.einsum("tbgp,gcp->tbgc", hs.real, cr)
              - jnp.einsum("tbgp,gcp->tbgc", hs.imag, ci) + dg * uc)
        return hs[-1], yc

    h_last, ys = lax.scan(step, h0, u_blocks)
    y = ys.transpose(2, 0, 1, 3, 4).reshape(bsz, l, BRANCH)
    y = jax.nn.gelu(y)
    y = y * jax.nn.sigmoid(y @ w_glu.astype(f32) + b_glu.astype(f32))
    return y.astype(u.dtype), h_last


def setup_inputs(seed: int = 0) -> dict:
    key = jax.random.key(seed)
    ks = jax.random.split(key, 32)
    f32 = jnp.float32
    nrm = lambda k, s, sc: jax.random.normal(k, s, f32) * sc
    in_a = 2 * BRANCH + XATT + MIX_WIDTH
    in_b = BRANCH + XATT + MIX_WIDTH
    lam_im = jnp.broadcast_to(math.pi * jnp.arange(SSM_STATE, dtype=f32), (N_B_LAYERS, SSM_GROUPS, SSM_STATE))
    return {
        "x_prompt": nrm(ks[0], (BATCH, SEQ, D_MODEL), 1.0),
        "x_sample": nrm(ks[1], (DEC_BATCH, DEC_SEQ, D_MODEL), 1.0),
        "cache_mem_k": nrm(ks[2], (DEPTH, DEC_BATCH, N_MEM, N_XHEADS, XHEAD_DIM), 1.0),
        "cache_mem_v": nrm(ks[3], (DEPTH, DEC_BATCH, N_MEM, N_XHEADS, XHEAD_DIM), 1.0),
        "state_ssm_re": nrm(ks[4], (N_B_LAYERS, DEC_BATCH, SSM_GROUPS, SSM_STATE), 0.5),
        "state_ssm_im": nrm(ks[5], (N_B_LAYERS, DEC_BATCH, SSM_GROUPS, SSM_STATE), 0.5),
        "mem_prompt": nrm(ks[6], (BATCH, N_MEM, D_MODEL), 1.0),
        "w_in_a": nrm(ks[7], (N_A_LAYERS, D_MODEL, in_a), D_MODEL ** -0.5),
        "ln_v_g": 1.0 + nrm(ks[8], (N_A_LAYERS, BRANCH), 0.02),
        "ln_v_b": nrm(ks[9], (N_A_LAYERS, BRANCH), 0.02),
        "w_spatial": nrm(ks[10], (N_A_LAYERS, A_GROUPS, CHUNK, CHUNK), CHUNK ** -0.5),
        "b_spatial": 1.0 + nrm(ks[11], (N_A_LAYERS, A_GROUPS, CHUNK), 0.02),
        "w_in_b": nrm(ks[12], (N_B_LAYERS, D_MODEL, in_b), D_MODEL ** -0.5),
        "ssm_lambda_re": -0.5 * jnp.exp(nrm(ks[13], (N_B_LAYERS, SSM_GROUPS, SSM_STATE), 0.05)),
        "ssm_lambda_im": lam_im + nrm(ks[14], (N_B_LAYERS, SSM_GROUPS, SSM_STATE), 0.01),
        "ssm_log_dt": jax.random.uniform(ks[15], (N_B_LAYERS, SSM_GROUPS), f32, math.log(DT_MIN), math.log(DT_MAX)),
        "ssm_b_re": nrm(ks[16], (N_B_LAYERS, SSM_GROUPS, SSM_STATE, SSM_GCH), (2 * SSM_GCH) ** -0.5),
        "ssm_b_im": nrm(ks[17], (N_B_LAYERS, SSM_GROUPS, SSM_STATE, SSM_GCH), (2 * SSM_GCH) ** -0.5),
        "ssm_c_re": nrm(ks[18], (N_B_LAYERS, SSM_GROUPS, SSM_GCH, SSM_STATE), (2 * SSM_STATE) ** -0.5),
        "ssm_c_im": nrm(ks[19], (N_B_LAYERS, SSM_GROUPS, SSM_GCH, SSM_STATE), (2 * SSM_STATE) ** -0.5),
        "ssm_d": nrm(ks[20], (N_B_LAYERS, BRANCH), 1.0),
        "w_glu": nrm(ks[21], (N_B_LAYERS, BRANCH, BRANCH), BRANCH ** -0.5),
        "b_glu": nrm(ks[22], (N_B_LAYERS, BRANCH), 0.01),
        "mem_norm_g": 1.0 + nrm(ks[23], (DEPTH, D_MODEL), 0.02),
        "w_mem_k": nrm(ks[24], (DEPTH, D_MODEL, XATT), D_MODEL ** -0.5),
        "w_mem_v": nrm(ks[25], (DEPTH, D_MODEL, XATT), D_MODEL ** -0.5),
        "w_out": nrm(ks[26], (DEPTH, MIX_WIDTH, D_MODEL), MIX_WIDTH ** -0.5),
        "pre_norm_g": 1.0 + nrm(ks[27], (DEPTH, D_MODEL), 0.02),
        "post_norm_g": 1.0 + nrm(ks[28], (DEPTH, D_MODEL), 0.02),
    }


def reference(x_prompt, x_sample, cache_mem_k, cache_mem_v, state_ssm_re, state_ssm_im, mem_prompt,
              w_in_a, ln_v_g, ln_v_b, w_spatial, b_spatial,
              w_in_b, ssm_lambda_re, ssm_lambda_im, ssm_log_dt, ssm_b_re, ssm_b_im, ssm_c_re, ssm_c_im,
              ssm_d, w_glu, b_glu,
              mem_norm_g, w_mem_k, w_mem_v, w_out, pre_norm_g, post_norm_g):
    f32 = jnp.float32

    def layer(i, x, k_mem, v_mem, h0):
        j = i // N_MIXERS
        h = rms_norm(x, pre_norm_g[i])
        if i % N_MIXERS == 0:
            z = h @ w_in_a[j]
            branch, extra = chunk_gating(z[..., :2 * BRANCH], ln_v_g[j], ln_v_b[j], w_spatial[j], b_spatial[j])
            off = 2 * BRANCH
        else:
            z = h @ w_in_b[j]
            branch, extra = s5_branch(z[..., :BRANCH], h0, ssm_lambda_re[j], ssm_lambda_im[j], ssm_log_dt[j],
                                      ssm_b_re[j], ssm_b_im[j], ssm_c_re[j], ssm_c_im[j], ssm_d[j],
                                      w_glu[j], b_glu[j])
            off = BRANCH
        q = z[..., off:off + XATT].reshape(x.shape[0], x.shape[1], N_XHEADS, XHEAD_DIM)
        att = cross_attend(q, k_mem, v_mem)
        mixed = jnp.concatenate([branch, att], axis=-1) * jax.nn.silu(z[..., off + XATT:])
        return x + rms_norm(mixed @ w_out[i], post_norm_g[i]), extra

    yp, ys = x_prompt, x_sample
    mk_p, mv_p, hp_re, hp_im, hs_re, hs_im, v_s = [], [], [], [], [], [], []
    for i in range(DEPTH):
        j = i // N_MIXERS
        kp, vp = memory_kv(mem_prompt, mem_norm_g[i], w_mem_k[i], w_mem_v[i])
        mk_p.append(kp)
        mv_p.append(vp)
        if i % N_MIXERS == 0:
            yp, _ = layer(i, yp, kp, vp, None)
            ys, v_rows = layer(i, ys, cache_mem_k[i], cache_mem_v[i], None)
            v_s.append(v_rows)
        else:
            h0p = jnp.zeros((x_prompt.shape[0], SSM_GROUPS, SSM_STATE), jnp.complex64)
            h0s = lax.complex(state_ssm_re[j].astype(f32), state_ssm_im[j].astype(f32))
            yp, hlp = layer(i, yp, kp, vp, h0p)
            ys, hls = layer(i, ys, cache_mem_k[i], cache_mem_v[i], h0s)
            hp_re.append(hlp.real)
            hp_im.append(hlp.imag)
            hs_re.append(hls.real)
            hs_im.append(hls.imag)

    return (yp, ys, jnp.stack(mk_p), jnp.stack(mv_p), jnp.stack(hp_re), jnp.stack(hp_im),
            jnp.stack(hs_re), jnp.stack(hs_im), jnp.stack(v_s))
```

```python
import numpy as np
import concourse.bass as bass
import concourse.mybir as mybir
from concourse.bass_utils import run_bass_kernel_spmd

F32 = mybir.dt.float32
BF16 = mybir.dt.bfloat16
AF = mybir.ActivationFunctionType
ALU = mybir.AluOpType

D = 1024
SEQ = 2048
NT = SEQ // 128
NS = 64
NSEQ = 16
BR = 1536
XATT = 512
MIXW = 2048
NMEM = 256
INA = 2 * BR + XATT + MIXW
INB = BR + XATT + MIXW
EPS = 1e-6
NCORES = 8

SB_BASE = 16512
SB_TOP = 229344
NDS = 40


def _dsize(dt):
    return 4 if dt == F32 else 2


class Buf:
    __slots__ = ("w", "r")

    def __init__(self):
        self.w = {}
        self.r = {}


def _merge(d, key, val):
    if d.get(key, 0) < val:
        d[key] = val


class T:
    def __init__(self, handle, off, nbytes, nb):
        self.t = handle
        self.off = off
        self.nbytes = nbytes
        self.b = [Buf() for _ in range(nb)]

    def __getitem__(self, idx):
        return self.t[idx]


class KB:
    def __init__(self, nc):
        self.nc = nc
        self.eng = {"pe": nc.tensor, "act": nc.scalar, "dve": nc.vector, "pool": nc.gpsimd, "sp": nc.sync}
        self.esem = {e: nc.alloc_semaphore("sem_" + e) for e in ("pe", "act", "dve", "pool")}
        self.ecnt = {e: 0 for e in self.esem}
        self.seen = {e: {} for e in self.eng}
        self.pending = {e: [] for e in self.esem}
        self.dsem = [nc.alloc_semaphore("dsem%d" % i) for i in range(NDS)]
        self.dcnt = [0] * NDS
        self.dnext = 0
        self.free = [(SB_BASE, SB_TOP)]
        self.dead = []
        self.nalloc = 0
        self.banks = []
        for i in range(8):
            h = nc.alloc_psum_tensor("bank%d" % i, [128, 512], F32)
            self.banks.append(T(h, 0, 0, 1))
        self.bnext = 0

    def alloc(self, name, shape, dtype, nb=1):
        n = 1
        for s in shape[1:]:
            n *= s
        nbytes = (n * _dsize(dtype) + 63) // 64 * 64
        for i, (s, e) in enumerate(self.free):
            if e - s >= nbytes:
                off = s
                if e - s == nbytes:
                    self.free.pop(i)
                else:
                    self.free[i] = (s + nbytes, e)
                break
        else:
            raise RuntimeError("SBUF arena full allocating %s (%d B); free=%s" % (name, nbytes, self.free))
        self.nalloc += 1
        h = self.nc.alloc_sbuf_tensor_at("%s_%d" % (name, self.nalloc), list(shape), dtype, offset=off)
        t = T(h, off, nbytes, nb)
        keep = []
        for (s, e, bufs) in self.dead:
            if s < off + nbytes and off < e:
                for ob in bufs:
                    for nbuf in t.b:
                        for k, v in ob.w.items():
                            _merge(nbuf.r, k, v)
                        for k, v in ob.r.items():
                            _merge(nbuf.r, k, v)
                if not (s >= off and e <= off + nbytes):
                    keep.append((s, e, bufs))
            else:
                keep.append((s, e, bufs))
        self.dead = keep
        return t

    def release(self, t, force=False):
        if getattr(t, "persistent", False) and not force:
            return
        self.dead.append((t.off, t.off + t.nbytes, t.b))
        fr = self.free + [(t.off, t.off + t.nbytes)]
        fr.sort()
        out = []
        for s, e in fr:
            if out and out[-1][1] == s:
                out[-1] = (out[-1][0], e)
            else:
                out.append((s, e))
        self.free = out

    def bank(self, hold=False):
        while True:
            b = self.banks[self.bnext]
            self.bnext = (self.bnext + 1) % 8
            if not getattr(b, "held", False):
                break
        if hold:
            b.held = True
        return b

    def unhold(self, b):
        b.held = False

    def _handle(self, key):
        return self.esem[key[1]] if key[0] == "e" else self.dsem[key[1]]

    def _waits(self, E, R, W, extra=None):
        deps = {}
        for b in R:
            for k, v in b.w.items():
                _merge(deps, k, v)
        for b in W:
            for k, v in b.w.items():
                _merge(deps, k, v)
            for k, v in b.r.items():
                _merge(deps, k, v)
        if extra:
            for k, v in extra.items():
                _merge(deps, k, v)
        eng = self.eng[E]
        seen = self.seen[E]
        for k, v in deps.items():
            if E == "pe" and k == ("e", "pe"):
                continue
            if v <= 0 or seen.get(k, 0) >= v:
                continue
            eng.wait_ge(self._handle(k), v)
            seen[k] = v

    def _record(self, tok, R, W):
        for b in R:
            _merge(b.r, tok[0], tok[1])
        for b in W:
            _merge(b.w, tok[0], tok[1])

    def op(self, E, fn, R=(), W=(), inc=True):
        self._waits(E, R, W)
        ins = fn(self.eng[E])
        if inc:
            self.ecnt[E] += 1
            ins.then_inc(self.esem[E], 1)
            tok = (("e", E), self.ecnt[E])
            self._record(tok, R, W)
            for (r2, w2) in self.pending[E]:
                self._record(tok, r2, w2)
            self.pending[E] = []
        else:
            assert E == "pe"
            self.pending[E].append((list(R), list(W)))
        return ins

    def pe(self, fn, R=(), W=(), inc=True):
        return self.op("pe", fn, R, W, inc)

    def act(self, fn, R=(), W=()):
        return self.op("act", fn, R, W)

    def dve(self, fn, R=(), W=()):
        return self.op("dve", fn, R, W)

    def pool(self, fn, R=(), W=()):
        return self.op("pool", fn, R, W)

    def dma(self, q, out, in_, R=(), W=(), **kw):
        i = self.dnext
        self.dnext = (i + 1) % NDS
        self._waits(q, R, W, extra={("d", i): self.dcnt[i]})
        self.dcnt[i] += 16
        self.eng[q].dma_start(out=out, in_=in_, **kw).then_inc(self.dsem[i], 16)
        tok = (("d", i), self.dcnt[i])
        self._record(tok, R, W)

    def finish(self):
        sp = self.eng["sp"]
        for i in range(NDS):
            if self.dcnt[i] > 0:
                sp.wait_ge(self.dsem[i], self.dcnt[i])
        for e in self.esem:
            if self.ecnt[e] > 0:
                sp.wait_ge(self.esem[e], self.ecnt[e])


def build_program(parts=("l0", "setup", "l1"), debug=False, stop=99):
    nc = bass.Bass("TRN2", target_bir_lowering=False)
    k = KB(nc)

    def din(name, shape, dt=F32):
        return nc.dram_tensor(name, list(shape), dt, kind="ExternalInput").ap()

    def dout(name, shape, dt=F32):
        return nc.dram_tensor(name, list(shape), dt, kind="ExternalOutput").ap()

    xp = din("xp", [SEQ, D])
    xs = din("xs", [NS, D])
    mem = din("mem", [NMEM, D])
    w_mem_k = din("w_mem_k", [2, D, XATT])
    w_mem_v = din("w_mem_v", [2, D, XATT])
    gvec = din("gvec", [6, D])
    ident_f = din("ident_f", [128, 128])
    ident_b = din("ident_b", [128, 128], BF16)

    w_in_a = din("w_in_a", [D, INA])
    w_out = din("w_out", [2, MIXW, D])
    lnv = din("lnv", [2, BR])
    wsT_d = din("wsT", [128, 8, 128])
    trilT_d = din("trilT", [128, 128])
    bsT_d = din("bsT", [128, 8])
    wsamp_d = din("wsamp", [64, 8, 64])
    mask_s_d = din("mask_s", [64, 64])
    bs_s_d = din("bs_s", [64, 8])
    kc_d = din("kc", [2, NSEQ, NMEM, XATT])
    vc_d = din("vc", [2, NSEQ, NMEM, XATT])
    x1d = nc.dram_tensor("x1d", [SEQ + NS, D], F32, kind="Internal").ap()
    lam2_d = din("lam2", [128, 2, 48])
    ldt2_d = din("ldt2", [128, 48])
    B2_d = din("B2", [128, 2, 48, 16])
    C2_d = din("C2", [128, 2, 48, 16])
    dcol_d = din("dcol", [128, 96])
    maskJ_d = din("maskJ", [128, 128])
    dk = "ExternalOutput" if debug else "Internal"
    TT_d = nc.dram_tensor("TT_d", [96, 128, 128], BF16, kind=dk).ap()
    BX_d = nc.dram_tensor("BX_d", [96, 128, 2, 64], BF16, kind=dk).ap()
    CX_d = nc.dram_tensor("CX_d", [48, 128, 2, 128], BF16, kind=dk).ap()
    SC_d = nc.dram_tensor("SC_d", [128, 2, 48, 26], F32, kind=dk).ap()
    P8_d = nc.dram_tensor("P8_d", [128, 2, 48, 32], F32, kind="Internal").ap()

    o_yp = dout("o_yp", [SEQ, D])
    o_ys = dout("o_ys", [NS, D])
    o_mk = dout("o_mk", [2, NMEM, XATT])
    o_mv = dout("o_mv", [2, NMEM, XATT])
    o_vs = dout("o_vs", [NS, BR])
    o_hp = dout("o_hp", [2, 48, 128])
    o_hs = dout("o_hs", [NSEQ, 2, 6144])
    w_in_b = din("w_in_b", [D, INB])
    w_glu = din("w_glu", [BR, BR])
    bglu_d = din("bglu", [1, BR])
    st_d = din("st", [NSEQ, 2, 6144])

    idf = k.alloc("idf", [128, 128], F32)
    idb = k.alloc("idb", [128, 128], BF16)
    k.dma("sp", idf[:], ident_f, W=[idf.b[0]])
    k.dma("sp", idb[:], ident_b, W=[idb.b[0]])
    cneg = k.alloc("cneg", [128, 1], F32)
    k.pool(lambda e: e.memset(cneg[:], -0.5), W=[cneg.b[0]])

    def load_gb(row):
        t = k.alloc("gb", [128, D], F32)
        k.dma("sp", t[:], gvec[row:row + 1, :].broadcast_to([128, D]), W=[t.b[0]])
        return t

    def load_w(name, src, rows, cols, c0=0, c1=None, q="pool"):
        c1 = cols if c1 is None else c1
        kc = rows // 128
        t = k.alloc(name, [128, kc, c1 - c0], BF16)
        v = src.rearrange("(kc p) n -> p kc n", p=128)
        step = 512
        for cb in range(c0, c1, step):
            ce = min(c1, cb + step)
            k.dma(q, t[:, :, cb - c0:ce - c0], v[:, :, cb:ce], W=[t.b[0]])
        return t

    def rmsnorm_bf(xt, xbuf, P, gb, bufs=None):
        if bufs is not None:
            junk, ss, h_pre = bufs
        else:
            junk = k.alloc("junk", [128, D], BF16)
            ss = k.alloc("ss", [128, 4], F32)
            h_pre = None
        k.act(lambda e: e.activation(out=junk[0:P, :], in_=xt, func=AF.Square, accum_out=ss[0:P, 0:1]),
              R=[xbuf], W=[junk.b[0], ss.b[0]])
        k.dve(lambda e: e.tensor_scalar(out=ss[0:P, 1:2], in0=ss[0:P, 0:1], scalar1=1.0 / D, scalar2=EPS,
                                        op0=ALU.mult, op1=ALU.add), R=[ss.b[0]], W=[ss.b[0]])
        k.pool(lambda e: e.tensor_tensor(out=ss[0:P, 2:3], in0=ss[0:P, 1:2], in1=cneg[0:P, :], op=ALU.pow),
               R=[ss.b[0], cneg.b[0]], W=[ss.b[0]])
        h = h_pre if h_pre is not None else k.alloc("h", [128, D], BF16)
        k.dve(lambda e: e.scalar_tensor_tensor(out=h[0:P, :], in0=xt, scalar=ss[0:P, 2:3], in1=gb[0:P, :],
                                               op0=ALU.mult, op1=ALU.mult),
              R=[xbuf, ss.b[0], gb.b[0]], W=[h.b[0]])
        k.release(junk)
        k.release(ss)
        return h

    def transpose_to(dst, dbuf, src, sbuf, P, nchunk, col0=0):
        done = 0
        while done < nchunk:
            n = min(8, nchunk - done)
            bk = k.bank()
            pv = bk.t.bitcast(BF16)
            for c in range(n):
                cc = done + c
                k.pe(lambda e, c=c, cc=cc: e.transpose(out=pv[:, c * 128:c * 128 + P],
                                                       in_=src[0:P, cc * 128:(cc + 1) * 128],
                                                       identity=idb[0:P, 0:P]),
                     R=[sbuf, idb.b[0]], W=[bk.b[0]], inc=(c == n - 1))
            pview = pv.rearrange("p (c t) -> p c t", t=128)
            k.dve(lambda e, n=n, d0=done: e.tensor_copy(out=dst[:, d0:d0 + n, col0:col0 + P],
                                                         in_=pview[:, 0:n, 0:P]),
                  R=[bk.b[0]], W=[dbuf])
            done += n

    class FrontSets:
        def __init__(self, n=2):
            self.sets = []
            for _ in range(n):
                d = dict(xt=k.alloc("fxt", [128, D], F32), junk=k.alloc("fjunk", [128, D], BF16),
                         ss=k.alloc("fss", [128, 4], F32), h=k.alloc("fh", [128, D], BF16),
                         hT=k.alloc("fhT", [128, 8, 128], BF16))
                for t_ in d.values():
                    t_.persistent = True
                self.sets.append(d)
            self.i = 0

        def front(self, xsrc, P, gb, xdeps=()):
            d = self.sets[self.i % len(self.sets)]
            self.i += 1
            xt, hT = d["xt"], d["hT"]
            k.dma("sp", xt[0:P, :], xsrc, R=list(xdeps), W=[xt.b[0]])
            h = rmsnorm_bf(xt[0:P, :], xt.b[0], P, gb, bufs=(d["junk"], d["ss"], d["h"]))
            transpose_to(hT, hT.b[0], h, h.b[0], P, 8)
            return xt, hT

        def free(self):
            for d in self.sets:
                for t_ in d.values():
                    k.release(t_, force=True)

    x1buf = Buf()
    scrbuf = Buf()

    def mem_kv(layer, gb_mem):
        wk = load_w("wk", w_mem_k[layer], D, XATT)
        wv = load_w("wv", w_mem_v[layer], D, XATT)
        mT = k.alloc("mT", [128, 8, NMEM], BF16)
        for mt in range(2):
            xt = k.alloc("memx", [128, D], F32)
            k.dma("sp", xt[:], mem[mt * 128:(mt + 1) * 128, :], W=[xt.b[0]])
            h = rmsnorm_bf(xt[:, :], xt.b[0], 128, gb_mem)
            transpose_to(mT, mT.b[0], h, h.b[0], 128, 8, col0=mt * 128)
            k.release(h)
            k.release(xt)
        KT = k.alloc("KT", [128, 4, NMEM], BF16)
        Vb = k.alloc("Vb", [128, 2, 4, 132], BF16)
        k.pool(lambda e: e.memset(Vb[:, :, :, 128:132], 1.0), W=[Vb.b[0]])
        for mt in range(2):
            for which, w, odram in ((0, wk, o_mk), (1, wv, o_mv)):
                bk = k.bank()
                for kc in range(8):
                    k.pe(lambda e, kc=kc: e.matmul(bk.t[:, :], lhsT=mT[:, kc, mt * 128:(mt + 1) * 128],
                                                   rhs=w[:, kc, :], start=(kc == 0), stop=(kc == 7)),
                         R=[mT.b[0], w.b[0]], W=[bk.b[0]], inc=(kc == 7))
                st = k.alloc("kvst", [128, XATT], F32)
                k.act(lambda e: e.copy(out=st[:, :], in_=bk.t[:, :]), R=[bk.b[0]], W=[st.b[0]])
                if which == 1:
                    k.dve(lambda e: e.tensor_copy(out=Vb[:, mt, :, 0:128], in_=st[:, :].rearrange("p (h d) -> p h d", h=4)),
                          R=[st.b[0]], W=[Vb.b[0]])
                k.dma("pool", odram[layer, mt * 128:(mt + 1) * 128, :], st[:, :], R=[st.b[0]])
                k.release(st)
        for hp in range(2):
            bk = k.bank()
            for hh in range(2):
                hd = hp * 2 + hh
                for kc in range(8):
                    k.pe(lambda e, kc=kc, hd=hd, hh=hh: e.matmul(bk.t[:, hh * 256:(hh + 1) * 256],
                                                                 lhsT=wk[:, kc, hd * 128:(hd + 1) * 128],
                                                                 rhs=mT[:, kc, :], start=(kc == 0), stop=(kc == 7)),
                         R=[mT.b[0], wk.b[0]], W=[bk.b[0]], inc=(kc == 7 and hh == 1))
            k.act(lambda e, hp=hp: e.copy(out=KT[:, hp * 2:hp * 2 + 2, :],
                                          in_=bk.t.rearrange("p (h m) -> p h m", h=2)),
                  R=[bk.b[0]], W=[KT.b[0]])
        k.release(mT)
        k.release(wk)
        k.release(wv)
        return KT, Vb

    ones_b = k.alloc("ones_b", [128, 128], BF16)
    k.pool(lambda e: e.memset(ones_b[:], 1.0), W=[ones_b.b[0]])

    def rstd_from_ms(st, P, src_col, dst_col):
        k.pool(lambda e: e.tensor_tensor(out=st[0:P, dst_col:dst_col + 1], in0=st[0:P, src_col:src_col + 1],
                                         in1=cneg[0:P, :], op=ALU.pow),
               R=[st.b[0], cneg.b[0]], W=[st.b[0]])

    def mm_group(outap, bk, pairs, R, last_inc=True):
        n = len(pairs)
        for i, (l, r) in enumerate(pairs):
            k.pe(lambda e, l=l, r=r, i=i: e.matmul(outap, lhsT=l, rhs=r, start=(i == 0), stop=(i == n - 1)),
                 R=R, W=[bk.b[0]], inc=(last_inc and i == n - 1))

    def attn_sample(layer, qT, gT, mixT):
        P = NS
        bkS = k.bank(hold=True)
        sview = bkS.t.rearrange("p (s g j) -> p s g j", s=NSEQ, g=8)
        NBUF = 4
        Kbs = [k.alloc("Kb", [128, 2, XATT], BF16) for _ in range(NBUF)]

        def load_k(b):
            k.dma("pool", Kbs[b % NBUF][:], kc_d[layer, b].rearrange("(mc m) f -> m mc f", mc=2), W=[Kbs[b % NBUF].b[0]])

        for b in range(NBUF):
            load_k(b)
        for b in range(NSEQ):
            Kb = Kbs[b % NBUF]
            bk = k.bank()
            pv = bk.t.bitcast(BF16)
            for hd in range(4):
                for mc in range(2):
                    c = hd * 2 + mc
                    k.pe(lambda e, c=c, hd=hd, mc=mc: e.transpose(out=pv[:, c * 128:(c + 1) * 128],
                                                                   in_=Kb[:, mc, hd * 128:(hd + 1) * 128],
                                                                   identity=idb[:, :]),
                         R=[Kb.b[0], idb.b[0]], W=[bk.b[0]], inc=(c == 7))
            if b + NBUF < NSEQ:
                load_k(b + NBUF)
            KTs = k.alloc("KTs", [128, 1024], BF16)
            k.dve(lambda e: e.tensor_copy(out=KTs[:, :], in_=pv[:, :]), R=[bk.b[0]], W=[KTs.b[0]])
            for hd in range(4):
                for mc in range(2):
                    c = hd * 2 + mc
                    k.pe(lambda e, c=c, hd=hd, b=b: e.matmul(sview[:, b, c, :], lhsT=KTs[:, c * 128:(c + 1) * 128],
                                                             rhs=qT[:, hd, b:NS:NSEQ], start=True, stop=True),
                         R=[KTs.b[0], qT.b[0]], W=[bkS.b[0]], inc=(c == 7))
            k.release(KTs)
        for t_ in Kbs:
            k.release(t_)
        PTs = k.alloc("PTs", [128, NSEQ, 8, 4], BF16)
        k.act(lambda e: e.activation(out=PTs[:].rearrange("p s g j -> p (s g j)"), in_=bkS.t[:, :], func=AF.Exp,
                                     scale=float(128 ** -0.5)),
              R=[bkS.b[0]], W=[PTs.b[0]])
        k.unhold(bkS)
        bkO = k.bank(hold=True)
        bkR = k.bank(hold=True)
        Vss = [k.alloc("Vs", [128, 2, XATT], BF16) for _ in range(NBUF)]

        def load_v(b):
            k.dma("pool", Vss[b % NBUF][:], vc_d[layer, b].rearrange("(mc m) f -> m mc f", mc=2), W=[Vss[b % NBUF].b[0]])

        for b in range(NBUF):
            load_v(b)
        for b in range(NSEQ):
            Vs = Vss[b % NBUF]
            for hd in range(4):
                oc = bkO.t[:, hd * 128 + b:hd * 128 + NS:NSEQ]
                rc = bkR.t[:, hd * 128 + b:hd * 128 + NS:NSEQ]
                mm_group(oc, bkO, [(Vs[:, mc, hd * 128:(hd + 1) * 128], PTs[:, b, hd * 2 + mc, :]) for mc in range(2)],
                         R=[Vs.b[0], PTs.b[0]], last_inc=False)
                mm_group(rc, bkR, [(ones_b[:, :], PTs[:, b, hd * 2 + mc, :]) for mc in range(2)],
                         R=[ones_b.b[0], PTs.b[0]], last_inc=(hd == 3))
            if b + NBUF < NSEQ:
                load_v(b + NBUF)
        for t_ in Vss:
            k.release(t_)
        k.release(PTs)
        finish_attn(P, bkO, bkR, gT, mixT)
        k.unhold(bkO)
        k.unhold(bkR)

    def finish_attn(P, bkO, bkR, gT, mixT):
        ov = bkO.t.rearrange("p (h t) -> p h t", h=4)
        rv = bkR.t.rearrange("p (h t) -> p h t", h=4)
        rrec = k.alloc("rrec", [128, 4, 128], F32)
        k.dve(lambda e: e.reciprocal(out=rrec[:, :, 0:P], in_=rv[:, :, 0:P]), R=[bkR.b[0]], W=[rrec.b[0]])
        k.dve(lambda e: e.tensor_tensor(out=rrec[:, :, 0:P], in0=rrec[:, :, 0:P], in1=gT[:, :, 0:P], op=ALU.mult),
              R=[rrec.b[0], gT.b[0]], W=[rrec.b[0]])
        k.dve(lambda e: e.tensor_tensor(out=mixT[:, 12:16, 0:P], in0=ov[:, :, 0:P], in1=rrec[:, :, 0:P], op=ALU.mult),
              R=[bkO.b[0], rrec.b[0]], W=[mixT.b[0]])
        k.release(rrec)

    def attn_prompt(P, KT, Vb, qT, gate, goff, mix):
        bkA = k.bank()
        bkB = k.bank()
        for hd in range(4):
            bk = bkA if hd < 2 else bkB
            for mc in range(2):
                c = (hd % 2) * 2 + mc
                k.pe(lambda e, bk=bk, c=c, hd=hd, mc=mc: e.matmul(bk.t[:, c * 128:c * 128 + P],
                                                                  lhsT=KT[:, hd, mc * 128:(mc + 1) * 128],
                                                                  rhs=qT[:, hd, 0:P], start=True, stop=True),
                     R=[KT.b[0], qT.b[0]], W=[bk.b[0]], inc=(c == 3))
        PT = k.alloc("PT", [128, 8, 128], BF16)
        for i, bk in enumerate((bkA, bkB)):
            k.act(lambda e, i=i, bk=bk: e.activation(out=PT[:, i * 4:(i + 1) * 4, :],
                                                     in_=bk.t.rearrange("p (c t) -> p c t", c=4),
                                                     func=AF.Exp, scale=float(128 ** -0.5)),
                  R=[bk.b[0]], W=[PT.b[0]])
        rr = k.alloc("rr", [128, 4], F32)
        obk = [k.bank(), k.bank()]
        for hp in range(2):
            bk = obk[hp]
            for hh in range(2):
                hd = hp * 2 + hh
                mm_group(bk.t[0:P, hh * 129:(hh + 1) * 129], bk,
                         [(PT[:, hd * 2 + mc, 0:P], Vb[:, mc, hd, 0:129]) for mc in range(2)],
                         R=[Vb.b[0], PT.b[0]], last_inc=(hh == 1))
            k.dve(lambda e, hp=hp, bk=bk: e.reciprocal(out=rr[0:P, hp * 2:hp * 2 + 2], in_=bk.t[0:P, 128:258:129]),
                  R=[bk.b[0]], W=[rr.b[0]])
            for hh in range(2):
                hd = hp * 2 + hh
                k.dve(lambda e, hh=hh, hd=hd, bk=bk: e.scalar_tensor_tensor(
                    out=mix[0:P, BR + hd * 128:BR + (hd + 1) * 128], in0=bk.t[0:P, hh * 129:hh * 129 + 128],
                    scalar=rr[0:P, hd:hd + 1], in1=gate[0:P, goff + hd * 128:goff + (hd + 1) * 128],
                    op0=ALU.mult, op1=ALU.mult), R=[bk.b[0], rr.b[0], gate.b[0]], W=[mix.b[0]])
        k.release(PT)
        k.release(rr)

    def post_and_store(P, bks, xt, gb_post, dst, dst2=None, wbuf=None):
        st = k.alloc("pst", [128, 4], F32)
        junk = k.alloc("pjunk", [128, 512], BF16)
        for i, bk in enumerate(bks):
            k.act(lambda e, i=i, bk=bk: e.activation(out=junk[0:P, :], in_=bk.t[0:P, :], func=AF.Square,
                                                     accum_out=st[0:P, i:i + 1]),
                  R=[bk.b[0]], W=[junk.b[0], st.b[0]])
        k.dve(lambda e: e.tensor_tensor(out=st[0:P, 2:3], in0=st[0:P, 0:1], in1=st[0:P, 1:2], op=ALU.add),
              R=[st.b[0]], W=[st.b[0]])
        k.dve(lambda e: e.tensor_scalar(out=st[0:P, 3:4], in0=st[0:P, 2:3], scalar1=1.0 / D, scalar2=EPS,
                                        op0=ALU.mult, op1=ALU.add), R=[st.b[0]], W=[st.b[0]])
        rstd_from_ms(st, P, 3, 0)
        x1 = k.alloc("x1", [128, D], F32)
        for i, bk in enumerate(bks):
            k.dve(lambda e, i=i, bk=bk: e.scalar_tensor_tensor(out=x1[0:P, i * 512:(i + 1) * 512], in0=bk.t[0:P, :],
                                                               scalar=st[0:P, 0:1],
                                                               in1=gb_post[0:P, i * 512:(i + 1) * 512],
                                                               op0=ALU.mult, op1=ALU.mult),
                  R=[bk.b[0], st.b[0], gb_post.b[0]], W=[x1.b[0]])
        k.pool(lambda e: e.tensor_tensor(out=x1[0:P, :], in0=x1[0:P, :], in1=xt[0:P, :], op=ALU.add),
               R=[x1.b[0], xt.b[0]], W=[x1.b[0]])
        k.dma("pool", dst, x1[0:P, :], R=[x1.b[0]], W=([wbuf] if wbuf is not None else []))
        if dst2 is not None:
            k.dma("pool", dst2, x1[0:P, :], R=[x1.b[0]])
        k.release(st)
        k.release(junk)
        k.release(x1)


    def ssm_setup(stop=99, after_tables=None):
        TWO_PI = 6.283185307179586
        MAGIC = 12582912.0
        L = k.alloc("sL", [128, 2, 48], F32)
        LD = k.alloc("sLD", [128, 48], F32)
        B2 = k.alloc("sB2", [128, 2, 48, 16], F32)
        C2 = k.alloc("sC2", [128, 2, 48, 16], F32)
        dcol = k.alloc("sdcol", [128, 96], F32)
        mJ = k.alloc("smJ", [128, 128], F32)
        for t_, d_ in ((L, lam2_d), (LD, ldt2_d), (B2, B2_d), (C2, C2_d), (dcol, dcol_d), (mJ, maskJ_d)):
            k.dma("sp", t_[:], d_, W=[t_.b[0]])
        W1 = k.alloc("sW1", [128, 12, 48], F32)
        wb = W1.b[0]

        def tt(out, a, b, op, R=()):
            k.dve(lambda e: e.tensor_tensor(out=out, in0=a, in1=b, op=op), R=[wb] + list(R), W=[wb])

        k.act(lambda e: e.activation(out=W1[:, 0, :], in_=LD[:, :], func=AF.Exp), R=[LD.b[0]], W=[wb])
        tt(W1[:, 1, :], L[:, 0, :], W1[:, 0, :], ALU.mult, R=[L.b[0]])
        tt(W1[:, 2, :], L[:, 1, :], W1[:, 0, :], ALU.mult, R=[L.b[0]])
        k.act(lambda e: e.activation(out=W1[:, 3, :], in_=W1[:, 1, :], func=AF.Exp), R=[wb], W=[wb])

        def sin_of(dst, src, shift):
            k.dve(lambda e: e.tensor_scalar(out=W1[:, 8, :], in0=src, scalar1=shift, scalar2=None, op0=ALU.add),
                  R=[wb], W=[wb])
            k.dve(lambda e: e.tensor_scalar(out=W1[:, 9, :], in0=W1[:, 8, :], scalar1=1.0 / TWO_PI, scalar2=MAGIC,
                                            op0=ALU.mult, op1=ALU.add), R=[wb], W=[wb])
            k.dve(lambda e: e.tensor_scalar(out=W1[:, 10, :], in0=W1[:, 9, :], scalar1=-MAGIC, scalar2=None,
                                            op0=ALU.add), R=[wb], W=[wb])
            k.dve(lambda e: e.scalar_tensor_tensor(out=W1[:, 11, :], in0=W1[:, 10, :], scalar=-TWO_PI,
                                                   in1=W1[:, 8, :], op0=ALU.mult, op1=ALU.add), R=[wb], W=[wb])
            k.dve(lambda e: e.tensor_scalar(out=W1[:, 11, :], in0=W1[:, 11, :], scalar1=3.1415925, scalar2=-3.1415925,
                                            op0=ALU.min, op1=ALU.max), R=[wb], W=[wb])
            k.act(lambda e: e.activation(out=dst, in_=W1[:, 11, :], func=AF.Sin), R=[wb], W=[wb])

        sin_of(W1[:, 4, :], W1[:, 2, :], 0.0)
        sin_of(W1[:, 5, :], W1[:, 2, :], 1.5707963267948966)
        SC = k.alloc("SC", [128, 2, 48, 26], F32)
        P8 = k.alloc("P8", [128, 2, 48, 32], F32)
        scb = SC.b[0]

        def stt(out, a, b, op):
            k.dve(lambda e: e.tensor_tensor(out=out, in0=a, in1=b, op=op), R=[wb, scb, P8.b[0]], W=[wb, scb, P8.b[0]])

        k.dve(lambda e: e.memset(SC[:, 0, :, 0], 1.0), W=[scb])
        k.dve(lambda e: e.memset(SC[:, 1, :, 0], 0.0), W=[scb])
        stt(SC[:, 0, :, 1], W1[:, 3, :], W1[:, 5, :], ALU.mult)
        stt(SC[:, 1, :, 1], W1[:, 3, :], W1[:, 4, :], ALU.mult)

        def cmul(dr, di, xr, xi, yr, yi):
            stt(W1[:, 8, :], xr, yr, ALU.mult)
            stt(W1[:, 9, :], xi, yi, ALU.mult)
            stt(W1[:, 10, :], xr, yi, ALU.mult)
            stt(W1[:, 11, :], xi, yr, ALU.mult)
            stt(dr, W1[:, 8, :], W1[:, 9, :], ALU.subtract)
            stt(di, W1[:, 10, :], W1[:, 11, :], ALU.add)

        def sl(n):
            return SC[:, 0, :, n], SC[:, 1, :, n]

        W2 = k.alloc("sW2", [128, 4, 48, 8], F32)

        def vcmul(Td, d0, Ts, s0, m, Tm_, mi):
            Rr = [wb, scb, P8.b[0], W2.b[0]]
            xr, xi = Ts[:, 0, :, s0:s0 + m], Ts[:, 1, :, s0:s0 + m]
            yr = Tm_[:, 0, :, mi].unsqueeze(2).broadcast_to([128, 48, m])
            yi = Tm_[:, 1, :, mi].unsqueeze(2).broadcast_to([128, 48, m])
            tv = [W2[:, q, :, 0:m] for q in range(4)]
            for q, (a_, b_) in enumerate(((xr, yr), (xi, yi), (xr, yi), (xi, yr))):
                k.dve(lambda e, q=q, a_=a_, b_=b_: e.tensor_tensor(out=tv[q], in0=a_, in1=b_, op=ALU.mult),
                      R=Rr, W=[W2.b[0]])
            k.dve(lambda e: e.tensor_tensor(out=Td[:, 0, :, d0:d0 + m], in0=tv[0], in1=tv[1], op=ALU.subtract),
                  R=Rr, W=[wb, scb, P8.b[0]])
            k.dve(lambda e: e.tensor_tensor(out=Td[:, 1, :, d0:d0 + m], in0=tv[2], in1=tv[3], op=ALU.add),
                  R=Rr, W=[wb, scb, P8.b[0]])

        vcmul(SC, 2, SC, 1, 1, SC, 1)
        vcmul(SC, 3, SC, 1, 2, SC, 2)
        vcmul(SC, 5, SC, 1, 4, SC, 4)
        vcmul(SC, 9, SC, 1, 8, SC, 8)
        cmul(*sl(19), *sl(16), *sl(16))
        cmul(*sl(20), *sl(19), *sl(19))
        cmul(*sl(21), *sl(20), *sl(20))
        cmul(*sl(22), *sl(21), *sl(21))
        cmul(*sl(23), *sl(22), *sl(22))
        cmul(*sl(24), *sl(23), *sl(23))
        stt(W1[:, 8, :], SC[:, 0, :, 8], SC[:, 0, :, 8], ALU.mult)
        stt(W1[:, 9, :], SC[:, 1, :, 8], SC[:, 1, :, 8], ALU.mult)
        stt(W1[:, 8, :], W1[:, 8, :], W1[:, 9, :], ALU.add)
        k.pool(lambda e: e.tensor_tensor(out=W1[:, 9, :], in0=W1[:, 8, :], in1=cneg[:, 0:1].broadcast_to([128, 48]),
                                         op=ALU.pow), R=[wb, cneg.b[0]], W=[wb])
        stt(SC[:, 0, :, 25], W1[:, 8, :], W1[:, 9, :], ALU.mult)
        k.dve(lambda e: e.memset(SC[:, 1, :, 25], 0.0), W=[scb])
        k.dve(lambda e: e.memset(P8[:, 0, :, 0], 1.0), W=[P8.b[0], scb])
        k.dve(lambda e: e.memset(P8[:, 1, :, 0], 0.0), W=[P8.b[0], scb])
        k.dve(lambda e: e.memset(P8[:, 0, :, 16], 1.0), W=[P8.b[0], scb])
        k.dve(lambda e: e.memset(P8[:, 1, :, 16], 0.0), W=[P8.b[0], scb])
        stt(P8[:, 0, :, 1], SC[:, 0, :, 8], W1[:, 9, :], ALU.mult)
        stt(W1[:, 10, :], SC[:, 1, :, 8], W1[:, 9, :], ALU.mult)
        k.dve(lambda e: e.tensor_scalar(out=P8[:, 1, :, 1], in0=W1[:, 10, :], scalar1=-1.0, scalar2=None, op0=ALU.mult),
              R=[wb], W=[P8.b[0], scb])
        vcmul(P8, 2, P8, 1, 1, P8, 1)
        vcmul(P8, 3, P8, 1, 2, P8, 2)
        vcmul(P8, 5, P8, 1, 4, P8, 4)
        vcmul(P8, 9, P8, 1, 7, P8, 8)
        vcmul(P8, 17, P8, 8, 1, P8, 8)
        vcmul(P8, 18, P8, 17, 1, P8, 17)
        vcmul(P8, 19, P8, 17, 2, P8, 18)
        vcmul(P8, 21, P8, 17, 4, P8, 20)
        vcmul(P8, 25, P8, 17, 7, P8, 24)
        k.release(W2)

        def cinv(dn, sn):
            xr, xi = sl(sn)
            stt(W1[:, 8, :], xr, xr, ALU.mult)
            stt(W1[:, 9, :], xi, xi, ALU.mult)
            stt(W1[:, 8, :], W1[:, 8, :], W1[:, 9, :], ALU.add)
            k.dve(lambda e: e.reciprocal(out=W1[:, 9, :], in_=W1[:, 8, :]), R=[wb], W=[wb])
            stt(SC[:, 0, :, dn], xr, W1[:, 9, :], ALU.mult)
            stt(W1[:, 10, :], xi, W1[:, 9, :], ALU.mult)
            k.dve(lambda e: e.tensor_scalar(out=SC[:, 1, :, dn], in0=W1[:, 10, :], scalar1=-1.0, scalar2=None,
                                            op0=ALU.mult), R=[wb], W=[scb])

        cinv(17, 4)
        cinv(18, 7)
        PR = k.alloc("sPR", [128, 2, 48, 8], F32)
        for jp in range(8):
            for ri in range(2):
                k.dve(lambda e, jp=jp, ri=ri: e.tensor_copy(out=PR[:, ri, :, jp], in_=SC[:, ri, :, 7 - jp]),
                      R=[scb], W=[PR.b[0]])
        k.dve(lambda e: e.tensor_scalar(out=W1[:, 0, :], in0=SC[:, 0, :, 1], scalar1=-1.0, scalar2=None, op0=ALU.add),
              R=[scb], W=[wb])
        stt(W1[:, 8, :], L[:, 0, :], L[:, 0, :], ALU.mult)
        stt(W1[:, 9, :], L[:, 1, :], L[:, 1, :], ALU.mult)
        stt(W1[:, 8, :], W1[:, 8, :], W1[:, 9, :], ALU.add)
        k.dve(lambda e: e.reciprocal(out=W1[:, 1, :], in_=W1[:, 8, :]), R=[wb], W=[wb])
        stt(W1[:, 8, :], W1[:, 0, :], L[:, 0, :], ALU.mult)
        stt(W1[:, 9, :], SC[:, 1, :, 1], L[:, 1, :], ALU.mult)
        stt(W1[:, 8, :], W1[:, 8, :], W1[:, 9, :], ALU.add)
        stt(W1[:, 6, :], W1[:, 8, :], W1[:, 1, :], ALU.mult)
        stt(W1[:, 8, :], SC[:, 1, :, 1], L[:, 0, :], ALU.mult)
        stt(W1[:, 9, :], W1[:, 0, :], L[:, 1, :], ALU.mult)
        stt(W1[:, 8, :], W1[:, 8, :], W1[:, 9, :], ALU.subtract)
        stt(W1[:, 7, :], W1[:, 8, :], W1[:, 1, :], ALU.mult)
        if stop <= 1:
            return SC, P8

        Bb = k.alloc("sBb", [128, 2, 48, 16], F32)
        Cm = k.alloc("sCm", [128, 2, 48, 16], F32)
        T2 = k.alloc("sT2", [128, 4, 48, 16], F32)

        def bc16(ap):
            return ap.unsqueeze(2).broadcast_to([128, 48, 16])

        def cmul16(dst, X, yr, yi, neg_im=False):
            Rr = [wb, scb, X.b[0], T2.b[0], dst.b[0]]
            ops = ((0, X[:, 0], yr), (1, X[:, 1], yi), (2, X[:, 0], yi), (3, X[:, 1], yr))
            for i_, a_, b_ in ops:
                k.dve(lambda e, i_=i_, a_=a_, b_=b_: e.tensor_tensor(out=T2[:, i_], in0=a_, in1=bc16(b_), op=ALU.mult),
                      R=Rr, W=[T2.b[0]])
            k.dve(lambda e: e.tensor_tensor(out=dst[:, 0], in0=T2[:, 0], in1=T2[:, 1], op=ALU.subtract),
                  R=Rr, W=[dst.b[0]])
            k.dve(lambda e: e.tensor_tensor(out=dst[:, 1], in0=T2[:, 2], in1=T2[:, 3], op=ALU.add),
                  R=Rr, W=[dst.b[0]])

        cmul16(Bb, B2, W1[:, 6, :], W1[:, 7, :])
        cmul16(Cm, C2, SC[:, 0, :, 18], SC[:, 1, :, 18])
        k.release(T2)
        if stop <= 2:
            return SC, P8

        if after_tables is not None:
            after_tables()
        SPB = 4
        SGB = 2 * SPB
        NSB = 48 // SPB

        class SB_:
            pass

        def stage_a(bl):
            Bk = SB_()
            Bk.bl = bl
            ps_ = slice(bl * SPB, (bl + 1) * SPB)
            X = k.alloc("sX", [128, 6, SPB, 8, 16], F32, nb=3)
            T4 = k.alloc("sT4", [128, 4, SPB, 8, 16], F32)
            T4p = k.alloc("sT4p", [128, 4, SPB, 8, 16], F32)
            xb = X.b[0]
            xbs = [X.b[0], X.b[0], X.b[2], X.b[2]]
            shp = [128, SPB, 8, 16]

            def expand(dr, di, M, pw_lo, reverse, neg_im, eng=None, TT_=None, xb=None):
                eng = eng or k.dve
                TT_ = TT_ or T4
                xb = xb or X.b[0]
                if reverse:
                    pr = PR[:, 0, ps_, :]
                    pi = PR[:, 1, ps_, :]
                else:
                    pr = SC[:, 0, ps_, pw_lo:pw_lo + 8]
                    pi = SC[:, 1, ps_, pw_lo:pw_lo + 8]
                pr = pr.unsqueeze(3).broadcast_to(shp)
                pi = pi.unsqueeze(3).broadcast_to(shp)
                mr = M[:, 0, ps_, :].unsqueeze(2).broadcast_to(shp)
                mi = M[:, 1, ps_, :].unsqueeze(2).broadcast_to(shp)
                Rr = [scb, M.b[0], TT_.b[0], xb, PR.b[0]]
                for i_, a_, b_ in ((0, mr, pr), (1, mi, pi), (2, mr, pi), (3, mi, pr)):
                    eng(lambda e, i_=i_, a_=a_, b_=b_: e.tensor_tensor(out=TT_[:, i_], in0=a_, in1=b_, op=ALU.mult),
                        R=Rr, W=[TT_.b[0]])
                eng(lambda e: e.tensor_tensor(out=X[:, dr], in0=TT_[:, 0], in1=TT_[:, 1], op=ALU.subtract),
                    R=Rr, W=[xb])
                if neg_im:
                    k.dve(lambda e: e.scalar_tensor_tensor(out=X[:, di].rearrange("p a b c -> p (a b c)"),
                                                           in0=TT_[:, 2].rearrange("p a b c -> p (a b c)"), scalar=-1.0,
                                                           in1=TT_[:, 3].rearrange("p a b c -> p (a b c)"),
                                                           op0=ALU.mult, op1=ALU.subtract), R=Rr, W=[xb])
                else:
                    eng(lambda e: e.tensor_tensor(out=X[:, di], in0=TT_[:, 2], in1=TT_[:, 3], op=ALU.add),
                        R=Rr, W=[xb])

            expand(4, 5, C2, 1, False, False, eng=k.pool, TT_=T4p, xb=X.b[1])
            expand(0, 1, Bb, 0, True, False)
            expand(2, 3, Cm, 0, False, False, xb=X.b[2])
            k.release(T4p)
            cxb = k.alloc("scxb", [128, SPB, 2, 128], BF16)
            for ri in range(2):
                k.act(lambda e, ri=ri: e.activation(out=cxb[:, :, ri, :],
                                                    in_=X[:, 4 + ri].rearrange("p a b c -> p a (b c)"),
                                                    func=AF.Copy, scale=(1.0 if ri == 0 else -1.0)),
                      R=[X.b[1]], W=[cxb.b[0]])
            k.dma("pool", CX_d[bl * SPB:(bl + 1) * SPB].rearrange("q r x c -> r q x c"), cxb[:], R=[cxb.b[0]], W=[scrbuf])
            k.release(cxb)
            Xf = [X[:, i_].rearrange("p a b c -> p a (b c)") for i_ in range(6)]
            XH = k.alloc("sXH", [128, 4, SPB, 128], BF16)
            XL = k.alloc("sXL", [128, 4, SPB, 128], BF16)
            for i_ in range(4):
                sc_ = -1.0 if i_ == 3 else 1.0
                k.act(lambda e, i_=i_, sc_=sc_: e.activation(out=XH[:, i_], in_=Xf[i_], func=AF.Copy, scale=sc_),
                      R=[xbs[i_]], W=[XH.b[0]])
                k.dve(lambda e, i_=i_: e.tensor_tensor(out=T4[:, i_].rearrange("p a b c -> p a (b c)"), in0=Xf[i_],
                                                       in1=XH[:, i_], op=(ALU.add if i_ == 3 else ALU.subtract)),
                      R=[xbs[i_], XH.b[0]], W=[T4.b[0]])
                k.act(lambda e, i_=i_, sc_=sc_: e.activation(out=XL[:, i_], in_=T4[:, i_].rearrange("p a b c -> p a (b c)"),
                                                             func=AF.Copy, scale=sc_),
                      R=[T4.b[0]], W=[XL.b[0]])
            k.release(X)
            k.release(T4)
            Bk.XH, Bk.XL = XH, XL
            return Bk

        def stage_b(Bk):
            bl, XH, XL = Bk.bl, Bk.XH, Bk.XL
            ttb = k.alloc("sttb", [128, SGB, 128], BF16)
            bxb = k.alloc("sbxb", [128, SGB, 2, 64], BF16)
            tmps = []
            for pp in range(SPB):
                for g2 in range(2):
                    gl = pp * 2 + g2
                    g = bl * SGB + gl
                    rows = slice(64 * g2, 64 * g2 + 64)
                    bk = k.bank()
                    combos = ((XH, 0, XH, 2), (XH, 0, XL, 2), (XL, 0, XH, 2), (XH, 1, XH, 3), (XH, 1, XL, 3), (XL, 1, XH, 3))
                    mm_group(bk.t[:, 0:128], bk, [(A_[rows, ia, pp, :], B_[rows, ib, pp, :]) for (A_, ia, B_, ib) in combos],
                             R=[XH.b[0], XL.b[0]])
                    tmp = k.alloc("sttmp", [128, 128], F32)
                    tmps.append(tmp)
                    k.dve(lambda e, bk=bk, tmp=tmp: e.tensor_tensor(out=tmp[:, :], in0=bk.t[:, 0:128], in1=mJ[:, :], op=ALU.mult),
                          R=[bk.b[0], mJ.b[0]], W=[tmp.b[0]])
                    k.dve(lambda e, g=g, gl=gl, tmp=tmp: e.scalar_tensor_tensor(out=ttb[:, gl, :], in0=idf[:, :],
                                                                                scalar=dcol[:, g:g + 1], in1=tmp[:, :],
                                                                                op0=ALU.mult, op1=ALU.add),
                          R=[tmp.b[0], idf.b[0], dcol.b[0]], W=[ttb.b[0]])
                    bkt = k.bank()
                    pvb = bkt.t.bitcast(BF16)
                    for ri in range(2):
                        k.pe(lambda e, pvb=pvb, pp=pp, rows=rows, ri=ri: e.transpose(
                            out=pvb[:, ri * 64:(ri + 1) * 64], in_=XH[rows, ri, pp, :],
                            identity=idb[rows, rows]), R=[XH.b[0], idb.b[0]], W=[bkt.b[0]], inc=(ri == 1))
                    k.act(lambda e, pvb=pvb, bkt=bkt, gl=gl: e.copy(out=bxb[:, gl, :, :].rearrange("p x q -> p (x q)"),
                                                                    in_=pvb[:, 0:128]), R=[bkt.b[0]], W=[bxb.b[0]])
                    if len(tmps) > 2:
                        k.release(tmps.pop(0))
            for t_ in tmps:
                k.release(t_)
            k.dma("pool", TT_d[bl * SGB:(bl + 1) * SGB].rearrange("g r c -> r g c"), ttb[:], R=[ttb.b[0]], W=[scrbuf])
            k.dma("pool", BX_d[bl * SGB:(bl + 1) * SGB].rearrange("g r x p -> r g x p"), bxb[:], R=[bxb.b[0]], W=[scrbuf])
            k.release(ttb)
            k.release(bxb)
            k.release(XH)
            k.release(XL)

        if stop > 3:
            cur = stage_a(0)
            for bl in range(NSB):
                nxt = stage_a(bl + 1) if bl + 1 < NSB else None
                stage_b(cur)
                cur = nxt
        k.dma("pool", SC_d, SC[:], R=[scb], W=[scrbuf])
        k.dma("pool", P8_d, P8[:], R=[P8.b[0]], W=[scrbuf])
        for t_ in (L, LD, B2, C2, dcol, mJ, W1, Bb, Cm, PR, SC, P8):
            k.release(t_)
        return None, None

    def layer0(Wa_pre=None, kv_pre=None):
        if kv_pre:
            KT, Vb = kv_pre
        else:
            gb_mem0 = load_gb(4)
            KT, Vb = mem_kv(0, gb_mem0)
            k.release(gb_mem0)
        Wa = Wa_pre if Wa_pre is not None else load_w("Wa", w_in_a, D, INA)
        Wo = load_w("Wo", w_out[0], MIXW, D)
        gb_pre = load_gb(0)
        gb_post = load_gb(2)
        lng = k.alloc("lng", [128, BR], F32)
        lnb = k.alloc("lnb", [128, BR], F32)
        k.dma("sp", lng[:], lnv[0:1, :].broadcast_to([128, BR]), W=[lng.b[0]])
        k.dma("sp", lnb[:], lnv[1:2, :].broadcast_to([128, BR]), W=[lnb.b[0]])
        wtmp = k.alloc("wtmp", [128, 8, 128], F32)
        msk = k.alloc("msk", [128, 128], F32)
        k.dma("sp", wtmp[:], wsT_d, W=[wtmp.b[0]])
        k.dma("sp", msk[:], trilT_d, W=[msk.b[0]])
        wsT = k.alloc("wsT", [128, 8, 128], BF16)
        for g in range(8):
            k.dve(lambda e, g=g: e.tensor_tensor(out=wsT[:, g, :], in0=wtmp[:, g, :], in1=msk[:, :], op=ALU.mult),
                  R=[wtmp.b[0], msk.b[0]], W=[wsT.b[0]])
        k.release(wtmp)
        k.release(msk)
        wtmp2 = k.alloc("wtmp2", [64, 8, 64], F32)
        msk2 = k.alloc("msk2", [64, 64], F32)
        k.dma("sp", wtmp2[:], wsamp_d, W=[wtmp2.b[0]])
        k.dma("sp", msk2[:], mask_s_d, W=[msk2.b[0]])
        wsS = k.alloc("wsS", [64, 8, 64], BF16)
        for g in range(8):
            k.dve(lambda e, g=g: e.tensor_tensor(out=wsS[:, g, :], in0=wtmp2[:, g, :], in1=msk2[:, :], op=ALU.mult),
                  R=[wtmp2.b[0], msk2.b[0]], W=[wsS.b[0]])
        k.release(wtmp2)
        k.release(msk2)
        bsT = k.alloc("bsT", [128, 8], F32)
        bsS = k.alloc("bsS", [64, 8], F32)
        k.dma("sp", bsT[:], bsT_d, W=[bsT.b[0]])
        k.dma("sp", bsS[:], bs_s_d, W=[bsS.b[0]])

        fsets = FrontSets(2)

        def front(xsrc, P):
            return fsets.front(xsrc, P, gb_pre)

        def tile(fr, P, sample, dst, next_front):
            xt, hT = fr
            RW = [hT.b[0], Wa.b[0]]
            u = k.alloc("u", [128, BR], F32)
            v = k.alloc("v", [128, BR], F32)
            gate = k.alloc("gate", [128, MIXW if not sample else BR], F32)
            qT = k.alloc("qT", [128, 4, 128], BF16)
            gT = k.alloc("gT", [128, 4, 128], F32) if sample else None

            def tm_blocks(dstt, c0, fn, nblk=3):
                for cb in range(nblk):
                    bk = k.bank()
                    mm_group(bk.t[0:P, :], bk, [(hT[:, kc, 0:P], Wa[:, kc, c0 + cb * 512:c0 + (cb + 1) * 512])
                                                for kc in range(8)], R=RW)
                    k.act(lambda e, bk=bk, cb=cb: e.activation(out=dstt[0:P, cb * 512:(cb + 1) * 512],
                                                               in_=bk.t[0:P, :], func=fn),
                          R=[bk.b[0]], W=[dstt.b[0]])

            def fm_block(dstt, c0, fn):
                bk = k.bank()
                for hd in range(4):
                    mm_group(bk.t[:, hd * 128:hd * 128 + P], bk,
                             [(Wa[:, kc, c0 + hd * 128:c0 + (hd + 1) * 128], hT[:, kc, 0:P]) for kc in range(8)],
                             R=RW, last_inc=(hd == 3))
                k.act(lambda e, bk=bk: e.activation(out=dstt[:, :, 0:P],
                                                    in_=bk.t.rearrange("p (h t) -> p h t", h=4)[:, :, 0:P], func=fn),
                      R=[bk.b[0]], W=[dstt.b[0]])

            tm_blocks(v, BR, AF.Gelu_apprx_tanh)
            st = k.alloc("lnst", [128, 32], F32)
            for cb in range(3):
                k.dve(lambda e, cb=cb: e.bn_stats(out=st[0:P, cb * 6:(cb + 1) * 6], in_=v[0:P, cb * 512:(cb + 1) * 512]),
                      R=[v.b[0]], W=[st.b[0]])
            k.dve(lambda e: e.bn_aggr(out=st[0:P, 18:20], in_=st[0:P, 0:18]), R=[st.b[0]], W=[st.b[0]])
            k.dve(lambda e: e.tensor_scalar(out=st[0:P, 20:21], in0=st[0:P, 19:20], scalar1=EPS, scalar2=None,
                                            op0=ALU.add), R=[st.b[0]], W=[st.b[0]])
            rstd_from_ms(st, P, 20, 21)
            k.dve(lambda e: e.scalar_tensor_tensor(out=st[0:P, 22:23], in0=st[0:P, 18:19], scalar=-1.0,
                                                   in1=st[0:P, 21:22], op0=ALU.mult, op1=ALU.mult),
                  R=[st.b[0]], W=[st.b[0]])
            tm_blocks(u, 0, AF.Gelu_apprx_tanh)
            q_tm = k.alloc("q_tm", [128, XATT], BF16)
            bkq = k.bank()
            mm_group(bkq.t[0:P, :], bkq, [(hT[:, kc, 0:P], Wa[:, kc, 2 * BR:2 * BR + XATT]) for kc in range(8)], R=RW)
            k.act(lambda e: e.copy(out=q_tm[0:P, :], in_=bkq.t[0:P, :]), R=[bkq.b[0]], W=[q_tm.b[0]])
            transpose_to(qT, qT.b[0], q_tm, q_tm.b[0], P, 4)
            k.release(q_tm)
            k.act(lambda e: e.activation(out=v[0:P, :], in_=v[0:P, :], func=AF.Identity, scale=st[0:P, 21:22],
                                         bias=st[0:P, 22:23]), R=[v.b[0], st.b[0]], W=[v.b[0]])
            k.dve(lambda e: e.tensor_tensor(out=v[0:P, :], in0=v[0:P, :], in1=lng[0:P, :], op=ALU.mult),
                  R=[v.b[0], lng.b[0]], W=[v.b[0]])
            vnb = k.alloc("vnb", [128, BR], BF16)
            if sample:
                k.dve(lambda e: e.tensor_tensor(out=v[0:P, :], in0=v[0:P, :], in1=lnb[0:P, :], op=ALU.add),
                      R=[v.b[0], lnb.b[0]], W=[v.b[0]])
                k.dma("pool", o_vs, v[0:P, :], R=[v.b[0]])
                k.dve(lambda e: e.tensor_copy(out=vnb[0:P, :], in_=v[0:P, :]), R=[v.b[0]], W=[vnb.b[0]])
            else:
                k.dve(lambda e: e.tensor_tensor(out=vnb[0:P, :], in0=v[0:P, :], in1=lnb[0:P, :], op=ALU.add),
                      R=[v.b[0], lnb.b[0]], W=[vnb.b[0]])
            k.release(st)
            k.release(v)
            mixT = k.alloc("mixT", [128, 16, 128], BF16)
            mix = k.alloc("mix", [128, BR if sample else MIXW], BF16)
            if sample:
                tm_blocks(gate, 2 * BR + XATT, AF.Silu)
                fm_block(gT, 2 * BR + XATT + BR, AF.Silu)
                k.release(hT)
                attn_sample(0, qT, gT, mixT)
                k.release(gT)
            else:
                tm_blocks(gate, 2 * BR + XATT, AF.Silu, nblk=4)
                k.release(hT)
                attn_prompt(P, KT, Vb, qT, gate, BR, mix)
            k.release(qT)
            k.dve(lambda e: e.tensor_tensor(out=u[0:P, :], in0=u[0:P, :], in1=gate[0:P, 0:BR], op=ALU.mult),
                  R=[u.b[0], gate.b[0]], W=[u.b[0]])
            k.release(gate)
            ws = wsS if sample else wsT
            bs = bsS if sample else bsT
            for gp in range(4):
                bk = k.bank()
                for gg in range(2):
                    g = gp * 2 + gg
                    k.pe(lambda e, bk=bk, g=g, gg=gg: e.matmul(bk.t[0:P, gg * 192:(gg + 1) * 192], lhsT=ws[0:P, g, 0:P],
                                                               rhs=vnb[0:P, g * 192:(g + 1) * 192], start=True, stop=True),
                         R=[ws.b[0], vnb.b[0]], W=[bk.b[0]], inc=(gg == 1))
                for gg in range(2):
                    g = gp * 2 + gg
                    k.dve(lambda e, bk=bk, g=g, gg=gg: e.scalar_tensor_tensor(out=mix[0:P, g * 192:(g + 1) * 192],
                                                                              in0=bk.t[0:P, gg * 192:(gg + 1) * 192],
                                                                              scalar=bs[0:P, g:g + 1],
                                                                              in1=u[0:P, g * 192:(g + 1) * 192],
                                                                              op0=ALU.add, op1=ALU.mult),
                          R=[bk.b[0], bs.b[0], u.b[0]], W=[mix.b[0]])
            k.release(vnb)
            k.release(u)
            nf = next_front() if next_front is not None else None
            transpose_to(mixT, mixT.b[0], mix, mix.b[0], P, 12 if sample else 16)
            k.release(mix)
            bks = [k.bank(), k.bank()]
            for cb in range(2):
                mm_group(bks[cb].t[0:P, :], bks[cb], [(mixT[:, kc, 0:P], Wo[:, kc, cb * 512:(cb + 1) * 512])
                                                     for kc in range(16)], R=[mixT.b[0], Wo.b[0]])
            k.release(mixT)
            post_and_store(P, bks, xt, gb_post, dst[0], dst[1], wbuf=x1buf)
            k.release(xt)
            return nf

        srcs = [(xp[t * 128:(t + 1) * 128, :], 128, False, (x1d[t * 128:(t + 1) * 128, :], None)) for t in range(NT)]
        srcs.append((xs[:, :], NS, True, (x1d[SEQ:SEQ + NS, :], None)))
        fr = front(srcs[0][0], srcs[0][1])
        for i, (src, P, smp, dst) in enumerate(srcs):
            nxt = (lambda i=i: front(srcs[i + 1][0], srcs[i + 1][1])) if i + 1 < len(srcs) else None
            fr = tile(fr, P, smp, dst, nxt)
        fsets.free()
        for t_ in (KT, Vb, Wa, Wo, gb_pre, gb_post, lng, lnb, wsT, wsS, bsT, bsS):
            k.release(t_)


    def layer1(SC, P8):
        SC = k.alloc("SC", [128, 2, 48, 26], F32)
        P8 = k.alloc("P8", [128, 2, 48, 32], F32)
        k.dma("sp", SC[:], SC_d, R=[scrbuf], W=[SC.b[0]])
        k.dma("sp", P8[:], P8_d, R=[scrbuf], W=[P8.b[0]])
        gb_mem1 = load_gb(5)
        KT, Vb = mem_kv(1, gb_mem1)
        k.release(gb_mem1)
        gb_pre = load_gb(1)
        gb_post = load_gb(3)
        UG = k.alloc("UG", [128, 2, 96, 128], BF16, nb=2)
        USG = k.alloc("USG", [NSEQ, 96, 64], BF16)
        HL = k.alloc("HL", [128, 2, 48], F32)
        scb = SC.b[0]

        def x1_rows(kt, j):
            return x1d[kt * 1024 + j:(kt + 1) * 1024:8, :]

        fsets = FrontSets(2)

        def load_norm_T(src, P):
            return fsets.front(src, P, gb_pre, xdeps=[x1buf])

        Wu = load_w("Wu", w_in_b, D, INB, c0=0, c1=BR)

        def a0_tile(fr, P, out_fn, in_fn, dbuf, next_front):
            xt, hT = fr
            k.release(xt)
            nf = next_front() if next_front is not None else None
            for cb in range(3):
                bk = k.bank()
                mm_group(bk.t[0:P, :], bk, [(hT[:, kc, 0:P], Wu[:, kc, cb * 512:(cb + 1) * 512]) for kc in range(8)],
                         R=[hT.b[0], Wu.b[0]])
                k.dve(lambda e, bk=bk, cb=cb: e.tensor_copy(out=out_fn(cb), in_=in_fn(bk)), R=[bk.b[0]], W=[dbuf])
            k.release(hT)
            return nf

        UG5 = UG.t.rearrange("p t g (j c) -> p t g j c", c=16)
        us = k.alloc("us", [NS, BR], BF16)
        a0src = []
        for kt in range(2):
            for j in range(8):
                a0src.append((x1_rows(kt, j), 128,
                              (lambda cb, kt=kt, j=j: UG5[:, kt, cb * 32:(cb + 1) * 32, j, :]),
                              (lambda bk: bk.t[:, :].rearrange("p (g c) -> p g c", c=16)), UG.b[kt]))
        a0src.append((x1d[SEQ:SEQ + NS, :], NS, (lambda cb: us[0:NS, cb * 512:(cb + 1) * 512]),
                      (lambda bk: bk.t[0:NS, :]), us.b[0]))
        fr = load_norm_T(a0src[0][0], a0src[0][1])
        for i, (src, P, ofn, ifn, dbuf) in enumerate(a0src):
            nxt = (lambda i=i: load_norm_T(a0src[i + 1][0], a0src[i + 1][1])) if i + 1 < len(a0src) else None
            fr = a0_tile(fr, P, ofn, ifn, dbuf, nxt)
        uyS = k.alloc("uyS", [NSEQ, 4, BR], BF16)
        for j in range(4):
            k.dma("sp", uyS[0:NSEQ, j, :], us[16 * j:16 * j + 16, :], R=[us.b[0]], W=[uyS.b[0]])
        k.release(us)
        USG4 = USG.t.rearrange("p g (j c) -> p g j c", c=16)
        for j in range(4):
            k.pool(lambda e, j=j: e.tensor_copy(out=USG4[:, :, j, :], in_=uyS[0:NSEQ, j, :].rearrange("p (g c) -> p g c", c=16)),
                   R=[uyS.b[0]], W=[USG.b[0]])
        k.release(uyS)
        k.release(Wu)

        NPc = 256
        NPB = 4
        NGB = 2 * NPB
        NBLK = 48 // NPB

        def bc(ap, shape, axis):
            return ap.unsqueeze(axis).broadcast_to(shape)

        def gen_E(bl):
            gps_ = slice(bl * NPB, (bl + 1) * NPB)
            Eb_ = k.alloc("Eb", [128, 2, NPB, 256], F32)
            TP = k.alloc("TP", [128, 2, NPB, 256], F32)
            Rp = [Eb_.b[0], TP.b[0], P8.b[0]]
            E4 = [Eb_[:, ri].rearrange("p a (c i) -> p a c i", i=16) for ri in range(2)]
            Tv = [TP[:, q].rearrange("p a (c i) -> p a c i", i=16) for q in range(2)]
            Gr = bc(P8[:, 0, gps_, 16:32], [128, NPB, 16, 16], 3)
            Gi = bc(P8[:, 1, gps_, 16:32], [128, NPB, 16, 16], 3)
            Fr = bc(P8[:, 0, gps_, 0:16], [128, NPB, 16, 16], 2)
            Fi = bc(P8[:, 1, gps_, 0:16], [128, NPB, 16, 16], 2)

            def pv(out, a_, b_, op, W):
                k.pool(lambda e: e.tensor_tensor(out=out, in0=a_, in1=b_, op=op), R=Rp, W=W)

            pv(Tv[0], Gr, Fr, ALU.mult, [TP.b[0]])
            pv(Tv[1], Gi, Fi, ALU.mult, [TP.b[0]])
            pv(E4[0], Tv[0], Tv[1], ALU.subtract, [Eb_.b[0]])
            pv(Tv[0], Gr, Fi, ALU.mult, [TP.b[0]])
            pv(Tv[1], Gi, Fr, ALU.mult, [TP.b[0]])
            pv(E4[1], Tv[0], Tv[1], ALU.add, [Eb_.b[0]])
            k.release(TP)
            return Eb_

        class Blk:
            pass

        def s1(bl):
            B = Blk()
            B.bl = bl
            B.TTb = k.alloc("TTb", [128, NGB, 128], BF16)
            B.BXb = k.alloc("BXb", [128, NGB, 2, 64], BF16)
            B.CXb = k.alloc("CXb", [128, NPB, 2, 128], BF16)
            k.dma("sp", B.TTb[:], TT_d[bl * NGB:(bl + 1) * NGB].rearrange("g r c -> r g c"), R=[scrbuf], W=[B.TTb.b[0]])
            k.dma("sp", B.BXb[:], BX_d[bl * NGB:(bl + 1) * NGB].rearrange("g r x p -> r g x p"), R=[scrbuf], W=[B.BXb.b[0]])
            k.dma("sp", B.CXb[:], CX_d[bl * NPB:(bl + 1) * NPB].rearrange("q r x c -> r q x c"), R=[scrbuf], W=[B.CXb.b[0]])
            stS = k.alloc("stS", [NSEQ, 2, NPB * 128], F32)
            k.dma("sp", stS[:], st_d[:, :, bl * NPB * 128:(bl + 1) * NPB * 128], W=[stS.b[0]])
            B.Ut = k.alloc("Ut", [128, NGB, 272], BF16)
            Ut = B.Ut
            k.pool(lambda e: e.memset(Ut[64:128, :, 256:272], 0.0), W=[Ut.b[0]])
            for g4 in range(NGB // 4):
                bk = k.bank()
                pv = bk.t.bitcast(BF16)
                for gg in range(4):
                    g = bl * NGB + g4 * 4 + gg
                    for kt in range(2):
                        c = gg * 2 + kt
                        k.pe(lambda e, c=c, g=g, kt=kt, pv=pv: e.transpose(out=pv[:, c * 128:(c + 1) * 128],
                                                                           in_=UG[:, kt, g, :], identity=idb[:, :]),
                             R=[UG.b[kt], idb.b[0]], W=[bk.b[0]], inc=(c == 7))
                k.act(lambda e, g4=g4, pv=pv: e.copy(out=Ut[:, g4 * 4:(g4 + 1) * 4, 0:NPc],
                                                     in_=pv.rearrange("p (g t) -> p g t", g=4)),
                      R=[bk.b[0]], W=[Ut.b[0]])
            bk = k.bank()
            pv = bk.t.bitcast(BF16)
            for gl in range(NGB):
                g = bl * NGB + gl
                k.pe(lambda e, gl=gl, g=g, pv=pv: e.transpose(out=pv[0:64, gl * 16:(gl + 1) * 16],
                                                              in_=USG[0:NSEQ, g, :], identity=idb[0:NSEQ, 0:NSEQ]),
                     R=[USG.b[0], idb.b[0]], W=[bk.b[0]], inc=(gl == NGB - 1))
            k.dve(lambda e, pv=pv: e.tensor_copy(out=Ut[0:64, :, NPc:NPc + 16],
                                                 in_=pv[0:64, 0:NGB * 16].rearrange("p (g t) -> p g t", g=NGB)),
                  R=[bk.b[0]], W=[Ut.b[0]])
            bk = k.bank()
            for ri in range(2):
                for pp in range(NPB):
                    c = ri * NPB + pp
                    k.pe(lambda e, c=c, ri=ri, pp=pp, bk=bk: e.transpose(out=bk.t[:, c * 16:(c + 1) * 16],
                                                                         in_=stS[0:NSEQ, ri, pp * 128:(pp + 1) * 128],
                                                                         identity=idf[0:NSEQ, 0:NSEQ]),
                         R=[stS.b[0], idf.b[0]], W=[bk.b[0]], inc=(c == 2 * NPB - 1))
            B.H0 = k.alloc("H0", [128, 2, NPB, 16], F32)
            H0 = B.H0
            k.dve(lambda e, bk=bk: e.tensor_copy(out=H0[:].rearrange("p a b c -> p (a b c)"), in_=bk.t[:, 0:2 * NPB * 16]),
                  R=[bk.b[0]], W=[H0.b[0]])
            k.release(stS)
            B.S = k.alloc("S", [128, 2, NPB, 272], F32)
            S = B.S
            for pp in range(NPB):
                bks = [k.bank(), k.bank()]
                for ri in range(2):
                    for g2 in range(2):
                        gl = pp * 2 + g2
                        k.pe(lambda e, ri=ri, g2=g2, gl=gl, bks=bks: e.matmul(bks[ri].t[64 * g2:64 * g2 + 64, 0:272],
                                                                              lhsT=B.BXb[:, gl, ri, :], rhs=Ut[:, gl, :],
                                                                              start=True, stop=True),
                             R=[B.BXb.b[0], Ut.b[0]], W=[bks[ri].b[0]], inc=(g2 == 1))
                    k.act(lambda e, ri=ri, pp=pp, bks=bks: e.copy(out=S[:, ri, pp, :], in_=bks[ri].t[:, 0:272]),
                          R=[bks[ri].b[0]], W=[S.b[0]])
            return B

        def s2(B, Eb):
            S = B.S
            sb_ = S.b[0]
            gps = slice(B.bl * NPB, (B.bl + 1) * NPB)
            Tq = k.alloc("Tq", [128, 3, NPB, 256], F32)
            eb, tq = Eb.b[0], Tq.b[0]
            Rr = [sb_, eb, tq, scb, P8.b[0]]

            def dv(out, a_, b_, op, W):
                k.dve(lambda e: e.tensor_tensor(out=out, in0=a_, in1=b_, op=op), R=Rr, W=W)

            Sr = S[:, 0, :, 0:NPc]
            Si = S[:, 1, :, 0:NPc]
            Er, Ei = Eb[:, 0], Eb[:, 1]
            t0, t1, t2 = Tq[:, 0], Tq[:, 1], Tq[:, 2]
            dv(t0, Sr, Er, ALU.mult, [tq])
            dv(t1, Si, Ei, ALU.mult, [tq])
            dv(t2, Sr, Ei, ALU.mult, [tq])
            dv(Sr, t0, t1, ALU.subtract, [sb_])
            dv(t0, Si, Er, ALU.mult, [tq])
            dv(Si, t2, t0, ALU.add, [sb_])
            k.dve(lambda e: e.tensor_copy(out=t1, in_=bc(SC[:, 0, gps, 25], [128, NPB, 256], 2)), R=Rr, W=[tq])
            for ri in range(2):
                for q in range(NPB):
                    k.dve(lambda e, ri=ri, q=q: e.tensor_tensor_scan(out=Tq[:, 2 * ri, q, :], data0=t1[:, q, :],
                                                                      data1=S[:, ri, q, 0:NPc],
                                                                      initial=0.0, op0=ALU.mult, op1=ALU.add),
                          R=Rr, W=[tq])
            Wr, Wi = Tq[:, 0], Tq[:, 2]
            dv(t1, Wr, Er, ALU.mult, [tq])
            dv(Sr, Wi, Ei, ALU.mult, [sb_])
            dv(Sr, Sr, t1, ALU.add, [sb_])
            dv(t1, Wi, Er, ALU.mult, [tq])
            dv(Si, Wr, Ei, ALU.mult, [sb_])
            dv(Si, t1, Si, ALU.subtract, [sb_])
            k.release(Eb)
            k.release(Tq)

        def s3prep(B):
            S, H0 = B.S, B.H0
            sb_ = S.b[0]
            bl = B.bl
            ps_ = slice(bl * NPB, (bl + 1) * NPB)
            B.Hin = k.alloc("Hin", [128, 2, NPB, 272], BF16)
            Hin = B.Hin
            for ri in range(2):
                k.pool(lambda e, ri=ri: e.memset(Hin[:, ri, :, 0:1], 0.0), W=[Hin.b[0]])
                k.act(lambda e, ri=ri: e.copy(out=Hin[:, ri, :, 1:NPc], in_=S[:, ri, :, 0:NPc - 1]),
                      R=[sb_], W=[Hin.b[0]])
                k.dve(lambda e, ri=ri: e.tensor_copy(out=Hin[:, ri, :, NPc:NPc + 16], in_=H0[:, ri]),
                      R=[H0.b[0]], W=[Hin.b[0]])
                k.dve(lambda e, ri=ri: e.tensor_copy(out=HL[:, ri, ps_], in_=S[:, ri, :, NPc - 1]),
                      R=[sb_], W=[HL.b[0]])
            Tm = k.alloc("Tm", [128, 4, NPB, 16], F32)
            tb = Tm.b[0]

            def cmac(dr, di, ar, ai, xr, xi):
                tv = [Tm[:, q] for q in range(4)]
                Rr = [sb_, tb, scb, H0.b[0]]
                for q, (a_, x_) in enumerate(((ar, xr), (ai, xi), (ai, xr), (ar, xi))):
                    k.dve(lambda e, q=q, a_=a_, x_=x_: e.tensor_tensor(out=tv[q], in0=a_, in1=x_, op=ALU.mult),
                          R=Rr, W=[tb])
                k.dve(lambda e: e.tensor_tensor(out=tv[0], in0=tv[0], in1=tv[1], op=ALU.subtract), R=Rr, W=[tb])
                k.dve(lambda e: e.tensor_tensor(out=tv[2], in0=tv[2], in1=tv[3], op=ALU.add), R=Rr, W=[tb])
                k.dve(lambda e: e.tensor_tensor(out=dr, in0=dr, in1=tv[0], op=ALU.add), R=Rr, W=[sb_, tb])
                k.dve(lambda e: e.tensor_tensor(out=di, in0=di, in1=tv[2], op=ALU.add), R=Rr, W=[sb_, tb])

            A8r = bc(SC[:, 0, ps_, 8], [128, NPB, 16], 2)
            A8i = bc(SC[:, 1, ps_, 8], [128, NPB, 16], 2)
            Ss = [S[:, ri, :, NPc:NPc + 16] for ri in range(2)]
            cmac(Ss[0], Ss[1], A8r, A8i, H0[:, 0], H0[:, 1])
            HSo = k.alloc("HSo", [128, 2, NPB, 16], F32)
            k.dve(lambda e: e.memset(HSo[:], 0.0), W=[HSo.b[0], sb_])
            cmac(HSo[:, 0], HSo[:, 1], bc(SC[:, 0, ps_, 17], [128, NPB, 16], 2), bc(SC[:, 1, ps_, 17], [128, NPB, 16], 2),
                 Ss[0], Ss[1])
            hso = k.alloc("hso", [NSEQ, 2, NPB * 128], F32)
            for ri in range(2):
                bk = k.bank()
                for q in range(NPB):
                    k.pe(lambda e, q=q, ri=ri, bk=bk: e.transpose(out=bk.t[0:NSEQ, q * 128:(q + 1) * 128],
                                                                  in_=HSo[:, ri, q, :], identity=idf[:, :]),
                         R=[sb_, HSo.b[0], idf.b[0]], W=[bk.b[0]], inc=(q == NPB - 1))
                k.act(lambda e, ri=ri, bk=bk: e.copy(out=hso[0:NSEQ, ri, :], in_=bk.t[0:NSEQ, 0:NPB * 128]),
                      R=[bk.b[0]], W=[hso.b[0]])
            k.dma("pool", o_hs[:, :, bl * NPB * 128:(bl + 1) * NPB * 128], hso[:], R=[hso.b[0]])
            k.release(hso)
            k.release(HSo)
            k.release(Tm)

        def s3y(B):
            bl = B.bl

            def y_stage1(gl):
                pp, g2 = gl // 2, gl % 2
                rows = slice(64 * g2, 64 * g2 + 64)
                bk = k.bank()
                mm_group(bk.t[:, 0:272], bk,
                         [(B.TTb[:, gl, :], B.Ut[:, gl, :]),
                          (B.CXb[rows, pp, 0, :], B.Hin[rows, 0, pp, :]),
                          (B.CXb[rows, pp, 1, :], B.Hin[rows, 1, pp, :])],
                         R=[B.TTb.b[0], B.Ut.b[0], B.CXb.b[0], B.Hin.b[0]])
                Ysb = k.alloc("Ysb", [128, 272], F32)
                k.act(lambda e: e.copy(out=Ysb[:, :], in_=bk.t[:, 0:272]), R=[bk.b[0]], W=[Ysb.b[0]])
                return Ysb

            def y_stage2(gl, Ysb):
                g = bl * NGB + gl
                bk2 = k.bank()
                for kt in range(2):
                    k.pe(lambda e, kt=kt: e.transpose(out=bk2.t[:, kt * 128:(kt + 1) * 128],
                                                      in_=Ysb[:, kt * 128:(kt + 1) * 128], identity=idf[:, :]),
                         R=[Ysb.b[0], idf.b[0]], W=[bk2.b[0]], inc=False)
                k.pe(lambda e: e.transpose(out=bk2.t[0:NSEQ, 256:384], in_=Ysb[:, 256:272], identity=idf[:, :]),
                     R=[Ysb.b[0], idf.b[0]], W=[bk2.b[0]], inc=True)
                k.release(Ysb)
                k.act(lambda e: e.activation(out=UG[:, :, g, :], in_=bk2.t[:, 0:256].rearrange("p (t c) -> p t c", t=2),
                                             func=AF.Gelu_apprx_tanh), R=[bk2.b[0]], W=[UG.b[0], UG.b[1]])
                k.act(lambda e: e.activation(out=USG[0:NSEQ, g, :], in_=bk2.t[0:NSEQ, 256:320], func=AF.Gelu_apprx_tanh),
                      R=[bk2.b[0]], W=[USG.b[0]])

            pend = [y_stage1(0), y_stage1(1)]
            for gl in range(NGB):
                if gl + 2 < NGB:
                    pend.append(y_stage1(gl + 2))
                y_stage2(gl, pend.pop(0))
            for t_ in (B.TTb, B.BXb, B.CXb, B.S, B.Hin, B.Ut, B.H0):
                k.release(t_)

        cur = s1(0)
        Ecur = gen_E(0)
        Enext = gen_E(1)
        s2(cur, Ecur)
        for bl in range(NBLK):
            nxt = s1(bl + 1) if bl + 1 < NBLK else None
            s3prep(cur)
            if nxt is not None:
                En2 = gen_E(bl + 2) if bl + 2 < NBLK else None
                s2(nxt, Enext)
                Enext = En2
            s3y(cur)
            cur = nxt
        bk = k.bank()
        for ri in range(2):
            k.pe(lambda e, ri=ri: e.transpose(out=bk.t[0:48, ri * 128:(ri + 1) * 128], in_=HL[:, ri, :], identity=idf[:, :]),
                 R=[HL.b[0], idf.b[0]], W=[bk.b[0]], inc=(ri == 1))
        hlo = k.alloc("hlo", [48, 2, 128], F32)
        k.act(lambda e: e.copy(out=hlo[:].rearrange("p a b -> p (a b)"), in_=bk.t[0:48, 0:256]), R=[bk.b[0]], W=[hlo.b[0]])
        k.dma("pool", o_hp.rearrange("r q c -> q r c"), hlo[:], R=[hlo.b[0]])
        k.release(hlo)
        k.release(HL)
        k.release(SC)
        k.release(P8)
        Wg = load_w("Wg", w_glu, BR, BR)
        uyP = k.alloc("uyP", [128, 2, 8, BR], BF16, nb=16)
        uyS = k.alloc("uyS", [NSEQ, 4, BR], BF16)
        cnt = 0
        for kt in range(2):
            for j in range(8):
                sel = cnt % 5
                cnt += 1
                if sel in (0, 2):
                    k.act(lambda e, kt=kt, j=j: e.copy(out=uyP[:, kt, j, :].rearrange("p (g c) -> p g c", c=16),
                                                       in_=UG5[:, kt, :, j, :]),
                          R=[UG.b[kt]], W=[uyP.b[kt * 8 + j]])
                else:
                    eng = k.pool if sel == 4 else k.dve
                    eng(lambda e, kt=kt, j=j: e.tensor_copy(out=uyP[:, kt, j, :].rearrange("p (g c) -> p g c", c=16),
                                                            in_=UG5[:, kt, :, j, :]),
                        R=[UG.b[kt]], W=[uyP.b[kt * 8 + j]])
        for j in range(4):
            k.pool(lambda e, j=j: e.tensor_copy(out=uyS[0:NSEQ, j, :].rearrange("p (g c) -> p g c", c=16),
                                                in_=USG4[:, :, j, :]), R=[USG.b[0]], W=[uyS.b[0]])
        k.release(UG)
        k.release(USG)

        Wqg = load_w("Wqg", w_in_b, D, INB, c0=BR, c1=INB)
        bgl = k.alloc("bgl", [1, BR], BF16)
        k.dma("pool", bgl[:], bglu_d, W=[bgl.b[0]])
        ys = k.alloc("ys", [NS, BR], BF16)
        for j in range(4):
            k.dma("sp", ys[16 * j:16 * j + 16, :], uyS[0:NSEQ, j, :], R=[uyS.b[0]], W=[ys.b[0]])

        def glu_front(y_ap, ybuf, P):
            yT = k.alloc("yT", [128, 12, 128], BF16)
            transpose_to(yT, yT.b[0], y_ap, ybuf, P, 12)
            return yT

        def glu_tile(yT, y_ap, ybuf, P, next_front):
            nf = next_front() if next_front is not None else None
            for cb in range(3):
                bk = k.bank()
                pairs = [(yT[:, kc, 0:P], Wg[:, kc, cb * 512:(cb + 1) * 512]) for kc in range(12)]
                pairs.append((ones_b[0:1, 0:P], bgl[0:1, cb * 512:(cb + 1) * 512]))
                mm_group(bk.t[0:P, :], bk, pairs, R=[yT.b[0], Wg.b[0], ones_b.b[0], bgl.b[0]])
                sig = k.alloc("sig", [128, 512], F32)
                k.act(lambda e, bk=bk, sig=sig: e.activation(out=sig[0:P, :], in_=bk.t[0:P, :], func=AF.Sigmoid),
                      R=[bk.b[0]], W=[sig.b[0]])
                k.dve(lambda e, cb=cb, sig=sig: e.tensor_tensor(out=y_ap[0:P, cb * 512:(cb + 1) * 512],
                                                                in0=y_ap[0:P, cb * 512:(cb + 1) * 512],
                                                                in1=sig[0:P, :], op=ALU.mult),
                      R=[sig.b[0], ybuf], W=[ybuf])
                k.release(sig)
            k.release(yT)
            return nf

        gsrc = [(uyP[:, kt, j, :], uyP.b[kt * 8 + j], 128) for kt in range(2) for j in range(8)]
        gsrc.append((ys, ys.b[0], NS))
        yT = glu_front(*gsrc[0])
        for i, (y_ap, ybuf, P) in enumerate(gsrc):
            nxt = (lambda i=i: glu_front(*gsrc[i + 1])) if i + 1 < len(gsrc) else None
            yT = glu_tile(yT, y_ap, ybuf, P, nxt)
        k.release(Wg)
        k.release(bgl)

        Wo = load_w("Wo1", w_out[1], MIXW, D)

        def b_tile(fr, P, sample, br_ap, brbuf, dst, next_front):
            xt, hT = fr
            RW = [hT.b[0], Wqg.b[0]]
            qT = k.alloc("qT", [128, 4, 128], BF16)
            gT = k.alloc("gT", [128, 4, 128], F32) if sample else None
            gate = k.alloc("gate", [128, BR if sample else MIXW], F32)

            def fm_block(dstt, c0, fn):
                bk = k.bank()
                for hd in range(4):
                    mm_group(bk.t[:, hd * 128:hd * 128 + P], bk,
                             [(Wqg[:, kc, c0 + hd * 128:c0 + (hd + 1) * 128], hT[:, kc, 0:P]) for kc in range(8)],
                             R=RW, last_inc=(hd == 3))
                k.act(lambda e, bk=bk: e.activation(out=dstt[:, :, 0:P],
                                                    in_=bk.t.rearrange("p (h t) -> p h t", h=4)[:, :, 0:P], func=fn),
                      R=[bk.b[0]], W=[dstt.b[0]])

            q_tm = k.alloc("q_tm", [128, XATT], BF16)
            bkq = k.bank()
            mm_group(bkq.t[0:P, :], bkq, [(hT[:, kc, 0:P], Wqg[:, kc, 0:XATT]) for kc in range(8)], R=RW)
            k.act(lambda e: e.copy(out=q_tm[0:P, :], in_=bkq.t[0:P, :]), R=[bkq.b[0]], W=[q_tm.b[0]])
            transpose_to(qT, qT.b[0], q_tm, q_tm.b[0], P, 4)
            k.release(q_tm)
            if sample:
                fm_block(gT, XATT + BR, AF.Silu)
            for cb in range(3 if sample else 4):
                bk = k.bank()
                mm_group(bk.t[0:P, :], bk, [(hT[:, kc, 0:P], Wqg[:, kc, XATT + cb * 512:XATT + (cb + 1) * 512])
                                            for kc in range(8)], R=RW)
                k.act(lambda e, bk=bk, cb=cb: e.activation(out=gate[0:P, cb * 512:(cb + 1) * 512], in_=bk.t[0:P, :],
                                                          func=AF.Silu), R=[bk.b[0]], W=[gate.b[0]])
            k.release(hT)
            mixT = k.alloc("mixT", [128, 16, 128], BF16)
            mix = k.alloc("mix", [128, BR if sample else MIXW], BF16)
            k.dve(lambda e: e.tensor_tensor(out=mix[0:P, 0:BR], in0=gate[0:P, 0:BR], in1=br_ap, op=ALU.mult),
                  R=[gate.b[0], brbuf], W=[mix.b[0]])
            if sample:
                attn_sample(1, qT, gT, mixT)
                k.release(gT)
            else:
                attn_prompt(P, KT, Vb, qT, gate, BR, mix)
            k.release(gate)
            k.release(qT)
            transpose_to(mixT, mixT.b[0], mix, mix.b[0], P, 12 if sample else 16)
            k.release(mix)
            nf = next_front() if next_front is not None else None
            bks = [k.bank(), k.bank()]
            for cb in range(2):
                mm_group(bks[cb].t[0:P, :], bks[cb], [(mixT[:, kc, 0:P], Wo[:, kc, cb * 512:(cb + 1) * 512])
                                                     for kc in range(16)], R=[mixT.b[0], Wo.b[0]])
            k.release(mixT)
            post_and_store(P, bks, xt, gb_post, dst)
            k.release(xt)
            return nf

        bsrc = []
        for kt in range(2):
            for j in range(8):
                bsrc.append((x1_rows(kt, j), 128, False, uyP[:, kt, j, :], uyP.b[kt * 8 + j],
                             o_yp[kt * 1024 + j:(kt + 1) * 1024:8, :]))
        bsrc.append((x1d[SEQ:SEQ + NS, :], NS, True, ys[0:NS, :], ys.b[0], o_ys[:, :]))
        fr = load_norm_T(bsrc[0][0], bsrc[0][1])
        for i, (src, P, smp, br_ap, brbuf, dst) in enumerate(bsrc):
            nxt = (lambda i=i: load_norm_T(bsrc[i + 1][0], bsrc[i + 1][1])) if i + 1 < len(bsrc) else None
            fr = b_tile(fr, P, smp, br_ap, brbuf, dst, nxt)
        fsets.free()
        for t_ in (KT, Vb, Wqg, Wo, gb_pre, gb_post, uyP, uyS, ys):
            k.release(t_)

    SC = P8 = None
    Wa_pre = None
    if "l0" in parts and "setup" in parts:
        Wa_pre = load_w("Wa", w_in_a, D, INA)
    kv_pre = []

    def early_kv():
        gb_mem0 = load_gb(4)
        kv_pre.extend(mem_kv(0, gb_mem0))
        k.release(gb_mem0)

    if "setup" in parts:
        SC, P8 = ssm_setup(stop, after_tables=(early_kv if "l0" in parts else None))
    if "l0" in parts:
        layer0(Wa_pre, kv_pre)
    if "l1" in parts:
        layer1(SC, P8)

    k.finish()
    return nc


_PROGRAM = None
_BUILD_ARGS = None
_LAST = None


def _get_program():
    global _PROGRAM
    if _PROGRAM is None:
        _PROGRAM = build_program()
    return _PROGRAM


def _l2(x):
    return np.ascontiguousarray(x.reshape(2, 48, 2, 64).transpose(2, 3, 0, 1).reshape(128, 2, 48))


def kernel(x_prompt, x_sample, cache_mem_k, cache_mem_v, state_ssm_re, state_ssm_im, mem_prompt,
           w_in_a, ln_v_g, ln_v_b, w_spatial, b_spatial,
           w_in_b, ssm_lambda_re, ssm_lambda_im, ssm_log_dt, ssm_b_re, ssm_b_im, ssm_c_re, ssm_c_im,
           ssm_d, w_glu, b_glu,
           mem_norm_g, w_mem_k, w_mem_v, w_out, pre_norm_g, post_norm_g):
    import ml_dtypes
    f32 = np.float32
    A = lambda a: np.ascontiguousarray(np.asarray(a), dtype=f32)
    gvec = np.concatenate([A(pre_norm_g), A(post_norm_g), A(mem_norm_g)], axis=0)
    shared = {
        "w_mem_k": A(w_mem_k), "w_mem_v": A(w_mem_v), "gvec": gvec,
        "w_in_a": A(w_in_a)[0], "w_out": A(w_out),
        "lnv": np.concatenate([A(ln_v_g), A(ln_v_b)], axis=0),
        "wsT": np.ascontiguousarray(A(w_spatial)[0].transpose(2, 0, 1)),
        "trilT": np.triu(np.ones((128, 128), f32)),
        "bsT": np.ascontiguousarray(A(b_spatial)[0].T),
        "wsamp": np.ascontiguousarray(np.broadcast_to(
            A(w_spatial)[0][:, :4, :4].transpose(2, 0, 1)[:, None, :, :, None], (4, 16, 8, 4, 16)).reshape(64, 8, 64)),
        "mask_s": np.ascontiguousarray((np.triu(np.ones((4, 4), f32))[:, None, :, None]
                                        * np.eye(16, dtype=f32)[None, :, None, :]).reshape(64, 64)),
        "bs_s": np.ascontiguousarray(np.repeat(A(b_spatial)[0][:, :4].T, 16, axis=0)),
        "w_in_b": A(w_in_b)[0], "w_glu": A(w_glu)[0], "bglu": A(b_glu),
        "lam2": _l2(np.stack([A(ssm_lambda_re)[0], A(ssm_lambda_im)[0]], 0)),
        "ldt2": np.ascontiguousarray(np.broadcast_to(A(ssm_log_dt)[0].reshape(48, 2).T[:, None, :], (2, 64, 48)).reshape(128, 48)),
        "B2": np.ascontiguousarray(np.stack([A(ssm_b_re)[0], A(ssm_b_im)[0]], 0).reshape(2, 48, 2, 64, 16)
                                   .transpose(2, 3, 0, 1, 4).reshape(128, 2, 48, 16)),
        "C2": np.ascontiguousarray(np.stack([A(ssm_c_re)[0], A(ssm_c_im)[0]], 0).reshape(2, 48, 2, 16, 64)
                                   .transpose(2, 4, 0, 1, 3).reshape(128, 2, 48, 16)),
        "dcol": np.ascontiguousarray(np.tile(A(ssm_d)[0].reshape(96, 16).T, (8, 1))),
        "maskJ": np.ascontiguousarray(np.kron(np.triu(np.ones((8, 8), f32)), np.ones((16, 16), f32))),
        "ident_f": np.eye(128, dtype=f32), "ident_b": np.eye(128, dtype=f32).astype(ml_dtypes.bfloat16),
    }
    x_prompt = A(x_prompt); x_sample = A(x_sample); mem_prompt = A(mem_prompt)
    cache_mem_k = A(cache_mem_k); cache_mem_v = A(cache_mem_v)
    state_ssm_re = A(state_ssm_re); state_ssm_im = A(state_ssm_im)
    in_maps = []
    for c in range(NCORES):
        m = dict(shared)
        m["xp"] = x_prompt[c]
        xs_c = x_sample[c * NSEQ:(c + 1) * NSEQ]
        m["xs"] = np.ascontiguousarray(xs_c.transpose(1, 0, 2).reshape(NS, D))
        m["mem"] = mem_prompt[c]
        m["st"] = np.ascontiguousarray(np.stack([state_ssm_re[0, c * NSEQ:(c + 1) * NSEQ].reshape(NSEQ, 6144),
                                                 state_ssm_im[0, c * NSEQ:(c + 1) * NSEQ].reshape(NSEQ, 6144)], axis=1))
        m["kc"] = np.ascontiguousarray(cache_mem_k[:, c * NSEQ:(c + 1) * NSEQ].reshape(2, NSEQ, NMEM, XATT))
        m["vc"] = np.ascontiguousarray(cache_mem_v[:, c * NSEQ:(c + 1) * NSEQ].reshape(2, NSEQ, NMEM, XATT))
        in_maps.append(m)
    if _BUILD_ARGS is not None:
        nc = build_program(**_BUILD_ARGS)
        res = run_bass_kernel_spmd(nc, in_maps[:1], core_ids=[0])
        global _LAST
        _LAST = res.results
        return None
    nc = _get_program()
    res = run_bass_kernel_spmd(nc, in_maps, core_ids=list(range(NCORES)))
    R = res.results
    y_prompt = np.stack([R[c]["o_yp"] for c in range(NCORES)], axis=0)
    y_sample = np.concatenate([R[c]["o_ys"].reshape(4, NSEQ, D).transpose(1, 0, 2) for c in range(NCORES)], axis=0)
    mk = np.stack([R[c]["o_mk"] for c in range(NCORES)], axis=1).reshape(2, NCORES, NMEM, 4, 128)
    mv = np.stack([R[c]["o_mv"] for c in range(NCORES)], axis=1).reshape(2, NCORES, NMEM, 4, 128)
    hp_re = np.stack([R[c]["o_hp"][0].reshape(96, 64) for c in range(NCORES)], axis=0)[None]
    hp_im = np.stack([R[c]["o_hp"][1].reshape(96, 64) for c in range(NCORES)], axis=0)[None]
    hs_re = np.concatenate([R[c]["o_hs"][:, 0].reshape(NSEQ, 96, 64) for c in range(NCORES)], axis=0)[None]
    hs_im = np.concatenate([R[c]["o_hs"][:, 1].reshape(NSEQ, 96, 64) for c in range(NCORES)], axis=0)[None]
    v_s = np.concatenate([R[c]["o_vs"].reshape(4, NSEQ, BR).transpose(1, 0, 2) for c in range(NCORES)], axis=0)[None]
    return (y_prompt, y_sample, mk, mv, hp_re, hp_im, hs_re, hs_im, v_s)
```

```python
import numpy as np
import concourse.bass as bass
import concourse.mybir as mybir
from concourse.bass_utils import run_bass_kernel_spmd

F32 = mybir.dt.float32
BF16 = mybir.dt.bfloat16
AF = mybir.ActivationFunctionType
ALU = mybir.AluOpType

D = 1024
SEQ = 2048
NT = SEQ // 128
NS = 64
NSEQ = 16
BR = 1536
XATT = 512
MIXW = 2048
NMEM = 256
INA = 2 * BR + XATT + MIXW
INB = BR + XATT + MIXW
EPS = 1e-6
NCORES = 8

SB_BASE = 16512
SB_TOP = 229344
NDS = 64


def _dsize(dt):
    return 4 if dt == F32 else 2


class Buf:
    __slots__ = ("w", "r")

    def __init__(self):
        self.w = {}
        self.r = {}


def _merge(d, key, val):
    if d.get(key, 0) < val:
        d[key] = val


class T:
    def __init__(self, handle, off, nbytes, nb):
        self.t = handle
        self.off = off
        self.nbytes = nbytes
        self.b = [Buf() for _ in range(nb)]

    def __getitem__(self, idx):
        return self.t[idx]


class KB:
    def __init__(self, nc):
        self.nc = nc
        self.eng = {"pe": nc.tensor, "act": nc.scalar, "dve": nc.vector, "pool": nc.gpsimd, "sp": nc.sync}
        self.esem = {e: nc.alloc_semaphore("sem_" + e) for e in ("pe", "act", "dve", "pool")}
        self.ecnt = {e: 0 for e in self.esem}
        self.seen = {e: {} for e in self.eng}
        self.pending = {e: [] for e in self.esem}
        self.dsem = [nc.alloc_semaphore("dsem%d" % i) for i in range(NDS)]
        self.dcnt = [0] * NDS
        self.dnext = 0
        self.free = [(SB_BASE, SB_TOP)]
        self.dead = []
        self.nalloc = 0
        self.banks = []
        for i in range(8):
            h = nc.alloc_psum_tensor("bank%d" % i, [128, 512], F32)
            self.banks.append(T(h, 0, 0, 1))
        self.bnext = 0

    def alloc(self, name, shape, dtype, nb=1):
        n = 1
        for s in shape[1:]:
            n *= s
        nbytes = (n * _dsize(dtype) + 63) // 64 * 64
        for i, (s, e) in enumerate(self.free):
            if e - s >= nbytes:
                off = s
                if e - s == nbytes:
                    self.free.pop(i)
                else:
                    self.free[i] = (s + nbytes, e)
                break
        else:
            raise RuntimeError("SBUF arena full allocating %s (%d B); free=%s" % (name, nbytes, self.free))
        self.nalloc += 1
        h = self.nc.alloc_sbuf_tensor_at("%s_%d" % (name, self.nalloc), list(shape), dtype, offset=off)
        t = T(h, off, nbytes, nb)
        keep = []
        for (s, e, bufs) in self.dead:
            if s < off + nbytes and off < e:
                for ob in bufs:
                    for nbuf in t.b:
                        for k, v in ob.w.items():
                            _merge(nbuf.r, k, v)
                        for k, v in ob.r.items():
                            _merge(nbuf.r, k, v)
                if not (s >= off and e <= off + nbytes):
                    keep.append((s, e, bufs))
            else:
                keep.append((s, e, bufs))
        self.dead = keep
        return t

    def release(self, t, force=False):
        if getattr(t, "persistent", False) and not force:
            return
        self.dead.append((t.off, t.off + t.nbytes, t.b))
        fr = self.free + [(t.off, t.off + t.nbytes)]
        fr.sort()
        out = []
        for s, e in fr:
            if out and out[-1][1] == s:
                out[-1] = (out[-1][0], e)
            else:
                out.append((s, e))
        self.free = out

    def bank(self, hold=False):
        while True:
            b = self.banks[self.bnext]
            self.bnext = (self.bnext + 1) % 8
            if not getattr(b, "held", False):
                break
        if hold:
            b.held = True
        return b

    def unhold(self, b):
        b.held = False

    def _handle(self, key):
        return self.esem[key[1]] if key[0] == "e" else self.dsem[key[1]]

    def _waits(self, E, R, W, extra=None):
        deps = {}
        for b in R:
            for k, v in b.w.items():
                _merge(deps, k, v)
        for b in W:
            for k, v in b.w.items():
                _merge(deps, k, v)
            for k, v in b.r.items():
                _merge(deps, k, v)
        if extra:
            for k, v in extra.items():
                _merge(deps, k, v)
        eng = self.eng[E]
        seen = self.seen[E]
        for k, v in deps.items():
            if E == "pe" and k == ("e", "pe"):
                continue
            if v <= 0 or seen.get(k, 0) >= v:
                continue
            eng.wait_ge(self._handle(k), v)
            seen[k] = v

    def _record(self, tok, R, W):
        for b in R:
            _merge(b.r, tok[0], tok[1])
        for b in W:
            _merge(b.w, tok[0], tok[1])

    def op(self, E, fn, R=(), W=(), inc=True):
        self._waits(E, R, W)
        ins = fn(self.eng[E])
        if inc:
            self.ecnt[E] += 1
            ins.then_inc(self.esem[E], 1)
            tok = (("e", E), self.ecnt[E])
            self._record(tok, R, W)
            for (r2, w2) in self.pending[E]:
                self._record(tok, r2, w2)
            self.pending[E] = []
        else:
            assert E == "pe"
            self.pending[E].append((list(R), list(W)))
        return ins

    def pe(self, fn, R=(), W=(), inc=True):
        return self.op("pe", fn, R, W, inc)

    def act(self, fn, R=(), W=()):
        return self.op("act", fn, R, W)

    def dve(self, fn, R=(), W=()):
        return self.op("dve", fn, R, W)

    def pool(self, fn, R=(), W=()):
        return self.op("pool", fn, R, W)

    def dma(self, q, out, in_, R=(), W=(), **kw):
        i = self.dnext
        self.dnext = (i + 1) % NDS
        self._waits(q, R, W, extra={("d", i): self.dcnt[i]})
        self.dcnt[i] += 16
        self.eng[q].dma_start(out=out, in_=in_, **kw).then_inc(self.dsem[i], 16)
        tok = (("d", i), self.dcnt[i])
        self._record(tok, R, W)

    def finish(self):
        sp = self.eng["sp"]
        for i in range(NDS):
            if self.dcnt[i] > 0:
                sp.wait_ge(self.dsem[i], self.dcnt[i])
        for e in self.esem:
            if self.ecnt[e] > 0:
                sp.wait_ge(self.esem[e], self.ecnt[e])


def build_program(parts=("l0", "setup", "l1"), debug=False, stop=99):
    nc = bass.Bass("TRN2", target_bir_lowering=False)
    k = KB(nc)

    def din(name, shape, dt=F32):
        return nc.dram_tensor(name, list(shape), dt, kind="ExternalInput").ap()

    def dout(name, shape, dt=F32):
        return nc.dram_tensor(name, list(shape), dt, kind="ExternalOutput").ap()

    xp = din("xp", [SEQ, D])
    xs = din("xs", [NS, D])
    mem = din("mem", [NMEM, D])
    w_mem_k = din("w_mem_k", [2, D, XATT])
    w_mem_v = din("w_mem_v", [2, D, XATT])
    gvec = din("gvec", [6, D])
    ident_f = din("ident_f", [128, 128])
    ident_b = din("ident_b", [128, 128], BF16)

    w_in_a = din("w_in_a", [D, INA])
    w_out = din("w_out", [2, MIXW, D])
    lnv = din("lnv", [2, BR])
    wsT_d = din("wsT", [128, 8, 128])
    trilT_d = din("trilT", [128, 128])
    bsT_d = din("bsT", [128, 8])
    wsamp_d = din("wsamp", [64, 8, 64])
    mask_s_d = din("mask_s", [64, 64])
    bs_s_d = din("bs_s", [64, 8])
    kc_d = din("kc", [2, NSEQ, NMEM, XATT])
    vc_d = din("vc", [2, NSEQ, NMEM, XATT])
    x1d = nc.dram_tensor("x1d", [SEQ + NS, D], F32, kind="Internal").ap()
    lam2_d = din("lam2", [128, 2, 48])
    ldt2_d = din("ldt2", [128, 48])
    B2_d = din("B2", [128, 2, 48, 16])
    C2_d = din("C2", [128, 2, 48, 16])
    dcol_d = din("dcol", [128, 96])
    maskJ_d = din("maskJ", [128, 128])
    dk = "ExternalOutput" if debug else "Internal"
    TT_d = nc.dram_tensor("TT_d", [96, 128, 128], BF16, kind=dk).ap()
    BX_d = nc.dram_tensor("BX_d", [96, 128, 2, 64], BF16, kind=dk).ap()
    CX_d = nc.dram_tensor("CX_d", [48, 128, 2, 128], BF16, kind=dk).ap()
    SC_d = nc.dram_tensor("SC_d", [128, 2, 48, 26], F32, kind=dk).ap()
    P8_d = nc.dram_tensor("P8_d", [128, 2, 48, 32], F32, kind="Internal").ap()

    o_yp = dout("o_yp", [SEQ, D])
    o_ys = dout("o_ys", [NS, D])
    o_mk = dout("o_mk", [2, NMEM, XATT])
    o_mv = dout("o_mv", [2, NMEM, XATT])
    o_vs = dout("o_vs", [NS, BR])
    o_hp = dout("o_hp", [2, 48, 128])
    o_hs = dout("o_hs", [NSEQ, 2, 6144])
    w_in_b = din("w_in_b", [D, INB])
    w_glu = din("w_glu", [BR, BR])
    bglu_d = din("bglu", [1, BR])
    st_d = din("st", [NSEQ, 2, 6144])

    idf = k.alloc("idf", [128, 128], F32)
    idb = k.alloc("idb", [128, 128], BF16)
    k.dma("sp", idf[:], ident_f, W=[idf.b[0]])
    k.dma("sp", idb[:], ident_b, W=[idb.b[0]])
    cneg = k.alloc("cneg", [128, 1], F32)
    k.pool(lambda e: e.memset(cneg[:], -0.5), W=[cneg.b[0]])

    def load_gb(row):
        t = k.alloc("gb", [128, D], F32)
        k.dma("sp", t[:], gvec[row:row + 1, :].broadcast_to([128, D]), W=[t.b[0]])
        return t

    def load_w(name, src, rows, cols, c0=0, c1=None, q="pool"):
        c1 = cols if c1 is None else c1
        kc = rows // 128
        t = k.alloc(name, [128, kc, c1 - c0], BF16)
        v = src.rearrange("(kc p) n -> p kc n", p=128)
        step = 512
        for cb in range(c0, c1, step):
            ce = min(c1, cb + step)
            k.dma(q, t[:, :, cb - c0:ce - c0], v[:, :, cb:ce], W=[t.b[0]])
        return t

    def rmsnorm_bf(xt, xbuf, P, gb, bufs=None):
        if bufs is not None:
            junk, ss, h_pre = bufs
        else:
            junk = k.alloc("junk", [128, D], BF16)
            ss = k.alloc("ss", [128, 4], F32)
            h_pre = None
        k.act(lambda e: e.activation(out=junk[0:P, :], in_=xt, func=AF.Square, accum_out=ss[0:P, 0:1]),
              R=[xbuf], W=[junk.b[0], ss.b[0]])
        k.dve(lambda e: e.tensor_scalar(out=ss[0:P, 1:2], in0=ss[0:P, 0:1], scalar1=1.0 / D, scalar2=EPS,
                                        op0=ALU.mult, op1=ALU.add), R=[ss.b[0]], W=[ss.b[0]])
        k.pool(lambda e: e.tensor_tensor(out=ss[0:P, 2:3], in0=ss[0:P, 1:2], in1=cneg[0:P, :], op=ALU.pow),
               R=[ss.b[0], cneg.b[0]], W=[ss.b[0]])
        h = h_pre if h_pre is not None else k.alloc("h", [128, D], BF16)
        k.dve(lambda e: e.scalar_tensor_tensor(out=h[0:P, :], in0=xt, scalar=ss[0:P, 2:3], in1=gb[0:P, :],
                                               op0=ALU.mult, op1=ALU.mult),
              R=[xbuf, ss.b[0], gb.b[0]], W=[h.b[0]])
        k.release(junk)
        k.release(ss)
        return h

    def transpose_to(dst, dbuf, src, sbuf, P, nchunk, col0=0):
        done = 0
        while done < nchunk:
            n = min(8, nchunk - done)
            bk = k.bank()
            pv = bk.t.bitcast(BF16)
            for c in range(n):
                cc = done + c
                k.pe(lambda e, c=c, cc=cc: e.transpose(out=pv[:, c * 128:c * 128 + P],
                                                       in_=src[0:P, cc * 128:(cc + 1) * 128],
                                                       identity=idb[0:P, 0:P]),
                     R=[sbuf, idb.b[0]], W=[bk.b[0]], inc=(c == n - 1))
            pview = pv.rearrange("p (c t) -> p c t", t=128)
            k.dve(lambda e, n=n, d0=done: e.tensor_copy(out=dst[:, d0:d0 + n, col0:col0 + P],
                                                         in_=pview[:, 0:n, 0:P]),
                  R=[bk.b[0]], W=[dbuf])
            done += n

    class FrontSets:
        def __init__(self, n=2):
            self.sets = []
            for _ in range(n):
                d = dict(xt=k.alloc("fxt", [128, D], F32), junk=k.alloc("fjunk", [128, D], BF16),
                         ss=k.alloc("fss", [128, 4], F32), h=k.alloc("fh", [128, D], BF16),
                         hT=k.alloc("fhT", [128, 8, 128], BF16))
                for t_ in d.values():
                    t_.persistent = True
                self.sets.append(d)
            self.i = 0

        def front(self, xsrc, P, gb, xdeps=()):
            d = self.sets[self.i % len(self.sets)]
            self.i += 1
            xt, hT = d["xt"], d["hT"]
            k.dma("sp", xt[0:P, :], xsrc, R=list(xdeps), W=[xt.b[0]])
            h = rmsnorm_bf(xt[0:P, :], xt.b[0], P, gb, bufs=(d["junk"], d["ss"], d["h"]))
            transpose_to(hT, hT.b[0], h, h.b[0], P, 8)
            return xt, hT

        def free(self):
            for d in self.sets:
                for t_ in d.values():
                    k.release(t_, force=True)

    x1buf = Buf()
    scrbuf = Buf()

    def mem_kv(layer, gb_mem):
        wk = load_w("wk", w_mem_k[layer], D, XATT)
        wv = load_w("wv", w_mem_v[layer], D, XATT)
        mT = k.alloc("mT", [128, 8, NMEM], BF16)
        for mt in range(2):
            xt = k.alloc("memx", [128, D], F32)
            k.dma("sp", xt[:], mem[mt * 128:(mt + 1) * 128, :], W=[xt.b[0]])
            h = rmsnorm_bf(xt[:, :], xt.b[0], 128, gb_mem)
            transpose_to(mT, mT.b[0], h, h.b[0], 128, 8, col0=mt * 128)
            k.release(h)
            k.release(xt)
        KT = k.alloc("KT", [128, 4, NMEM], BF16)
        Vb = k.alloc("Vb", [128, 2, 4, 132], BF16)
        k.pool(lambda e: e.memset(Vb[:, :, :, 128:132], 1.0), W=[Vb.b[0]])
        for mt in range(2):
            for which, w, odram in ((0, wk, o_mk), (1, wv, o_mv)):
                bk = k.bank()
                for kc in range(8):
                    k.pe(lambda e, kc=kc: e.matmul(bk.t[:, :], lhsT=mT[:, kc, mt * 128:(mt + 1) * 128],
                                                   rhs=w[:, kc, :], start=(kc == 0), stop=(kc == 7)),
                         R=[mT.b[0], w.b[0]], W=[bk.b[0]], inc=(kc == 7))
                st = k.alloc("kvst", [128, XATT], F32)
                k.act(lambda e: e.copy(out=st[:, :], in_=bk.t[:, :]), R=[bk.b[0]], W=[st.b[0]])
                if which == 1:
                    k.dve(lambda e: e.tensor_copy(out=Vb[:, mt, :, 0:128], in_=st[:, :].rearrange("p (h d) -> p h d", h=4)),
                          R=[st.b[0]], W=[Vb.b[0]])
                k.dma("pool", odram[layer, mt * 128:(mt + 1) * 128, :], st[:, :], R=[st.b[0]])
                k.release(st)
        for hp in range(2):
            bk = k.bank()
            for hh in range(2):
                hd = hp * 2 + hh
                for kc in range(8):
                    k.pe(lambda e, kc=kc, hd=hd, hh=hh: e.matmul(bk.t[:, hh * 256:(hh + 1) * 256],
                                                                 lhsT=wk[:, kc, hd * 128:(hd + 1) * 128],
                                                                 rhs=mT[:, kc, :], start=(kc == 0), stop=(kc == 7)),
                         R=[mT.b[0], wk.b[0]], W=[bk.b[0]], inc=(kc == 7 and hh == 1))
            k.act(lambda e, hp=hp: e.copy(out=KT[:, hp * 2:hp * 2 + 2, :],
                                          in_=bk.t.rearrange("p (h m) -> p h m", h=2)),
                  R=[bk.b[0]], W=[KT.b[0]])
        k.release(mT)
        k.release(wk)
        k.release(wv)
        return KT, Vb

    ones_b = k.alloc("ones_b", [128, 128], BF16)
    k.pool(lambda e: e.memset(ones_b[:], 1.0), W=[ones_b.b[0]])

    def rstd_from_ms(st, P, src_col, dst_col):
        k.pool(lambda e: e.tensor_tensor(out=st[0:P, dst_col:dst_col + 1], in0=st[0:P, src_col:src_col + 1],
                                         in1=cneg[0:P, :], op=ALU.pow),
               R=[st.b[0], cneg.b[0]], W=[st.b[0]])

    def mm_group(outap, bk, pairs, R, last_inc=True):
        n = len(pairs)
        for i, (l, r) in enumerate(pairs):
            k.pe(lambda e, l=l, r=r, i=i: e.matmul(outap, lhsT=l, rhs=r, start=(i == 0), stop=(i == n - 1)),
                 R=R, W=[bk.b[0]], inc=(last_inc and i == n - 1))

    def attn_sample(layer, qT, gT, mixT):
        P = NS
        bkS = k.bank(hold=True)
        sview = bkS.t.rearrange("p (s g j) -> p s g j", s=NSEQ, g=8)
        NBUF = 4
        Kbs = [k.alloc("Kb", [128, 2, XATT], BF16) for _ in range(NBUF)]

        def load_k(b):
            k.dma("pool", Kbs[b % NBUF][:], kc_d[layer, b].rearrange("(mc m) f -> m mc f", mc=2), W=[Kbs[b % NBUF].b[0]])

        for b in range(NBUF):
            load_k(b)
        for b in range(NSEQ):
            Kb = Kbs[b % NBUF]
            bk = k.bank()
            pv = bk.t.bitcast(BF16)
            for hd in range(4):
                for mc in range(2):
                    c = hd * 2 + mc
                    k.pe(lambda e, c=c, hd=hd, mc=mc: e.transpose(out=pv[:, c * 128:(c + 1) * 128],
                                                                   in_=Kb[:, mc, hd * 128:(hd + 1) * 128],
                                                                   identity=idb[:, :]),
                         R=[Kb.b[0], idb.b[0]], W=[bk.b[0]], inc=(c == 7))
            if b + NBUF < NSEQ:
                load_k(b + NBUF)
            KTs = k.alloc("KTs", [128, 1024], BF16)
            k.dve(lambda e: e.tensor_copy(out=KTs[:, :], in_=pv[:, :]), R=[bk.b[0]], W=[KTs.b[0]])
            for hd in range(4):
                for mc in range(2):
                    c = hd * 2 + mc
                    k.pe(lambda e, c=c, hd=hd, b=b: e.matmul(sview[:, b, c, :], lhsT=KTs[:, c * 128:(c + 1) * 128],
                                                             rhs=qT[:, hd, b:NS:NSEQ], start=True, stop=True),
                         R=[KTs.b[0], qT.b[0]], W=[bkS.b[0]], inc=(c == 7))
            k.release(KTs)
        for t_ in Kbs:
            k.release(t_)
        PTs = k.alloc("PTs", [128, NSEQ, 8, 4], BF16)
        k.act(lambda e: e.activation(out=PTs[:].rearrange("p s g j -> p (s g j)"), in_=bkS.t[:, :], func=AF.Exp,
                                     scale=float(128 ** -0.5)),
              R=[bkS.b[0]], W=[PTs.b[0]])
        k.unhold(bkS)
        bkO = k.bank(hold=True)
        bkR = k.bank(hold=True)
        Vss = [k.alloc("Vs", [128, 2, XATT], BF16) for _ in range(NBUF)]

        def load_v(b):
            k.dma("pool", Vss[b % NBUF][:], vc_d[layer, b].rearrange("(mc m) f -> m mc f", mc=2), W=[Vss[b % NBUF].b[0]])

        for b in range(NBUF):
            load_v(b)
        for b in range(NSEQ):
            Vs = Vss[b % NBUF]
            for hd in range(4):
                oc = bkO.t[:, hd * 128 + b:hd * 128 + NS:NSEQ]
                rc = bkR.t[:, hd * 128 + b:hd * 128 + NS:NSEQ]
                mm_group(oc, bkO, [(Vs[:, mc, hd * 128:(hd + 1) * 128], PTs[:, b, hd * 2 + mc, :]) for mc in range(2)],
                         R=[Vs.b[0], PTs.b[0]], last_inc=False)
                mm_group(rc, bkR, [(ones_b[:, :], PTs[:, b, hd * 2 + mc, :]) for mc in range(2)],
                         R=[ones_b.b[0], PTs.b[0]], last_inc=(hd == 3))
            if b + NBUF < NSEQ:
                load_v(b + NBUF)
        for t_ in Vss:
            k.release(t_)
        k.release(PTs)
        finish_attn(P, bkO, bkR, gT, mixT)
        k.unhold(bkO)
        k.unhold(bkR)

    def finish_attn(P, bkO, bkR, gT, mixT):
        ov = bkO.t.rearrange("p (h t) -> p h t", h=4)
        rv = bkR.t.rearrange("p (h t) -> p h t", h=4)
        rrec = k.alloc("rrec", [128, 4, 128], F32)
        k.dve(lambda e: e.reciprocal(out=rrec[:, :, 0:P], in_=rv[:, :, 0:P]), R=[bkR.b[0]], W=[rrec.b[0]])
        k.dve(lambda e: e.tensor_tensor(out=rrec[:, :, 0:P], in0=rrec[:, :, 0:P], in1=gT[:, :, 0:P], op=ALU.mult),
              R=[rrec.b[0], gT.b[0]], W=[rrec.b[0]])
        k.dve(lambda e: e.tensor_tensor(out=mixT[:, 12:16, 0:P], in0=ov[:, :, 0:P], in1=rrec[:, :, 0:P], op=ALU.mult),
              R=[bkO.b[0], rrec.b[0]], W=[mixT.b[0]])
        k.release(rrec)

    def attn_prompt(P, KT, Vb, qT, gate, goff, mix):
        bkA = k.bank()
        bkB = k.bank()
        for hd in range(4):
            bk = bkA if hd < 2 else bkB
            for mc in range(2):
                c = (hd % 2) * 2 + mc
                k.pe(lambda e, bk=bk, c=c, hd=hd, mc=mc: e.matmul(bk.t[:, c * 128:c * 128 + P],
                                                                  lhsT=KT[:, hd, mc * 128:(mc + 1) * 128],
                                                                  rhs=qT[:, hd, 0:P], start=True, stop=True),
                     R=[KT.b[0], qT.b[0]], W=[bk.b[0]], inc=(c == 3))
        PT = k.alloc("PT", [128, 8, 128], BF16)
        for i, bk in enumerate((bkA, bkB)):
            k.act(lambda e, i=i, bk=bk: e.activation(out=PT[:, i * 4:(i + 1) * 4, :],
                                                     in_=bk.t.rearrange("p (c t) -> p c t", c=4),
                                                     func=AF.Exp, scale=float(128 ** -0.5)),
                  R=[bk.b[0]], W=[PT.b[0]])
        rr = k.alloc("rr", [128, 4], F32)
        obk = [k.bank(), k.bank()]
        for hp in range(2):
            bk = obk[hp]
            for hh in range(2):
                hd = hp * 2 + hh
                mm_group(bk.t[0:P, hh * 129:(hh + 1) * 129], bk,
                         [(PT[:, hd * 2 + mc, 0:P], Vb[:, mc, hd, 0:129]) for mc in range(2)],
                         R=[Vb.b[0], PT.b[0]], last_inc=(hh == 1))
            k.dve(lambda e, hp=hp, bk=bk: e.reciprocal(out=rr[0:P, hp * 2:hp * 2 + 2], in_=bk.t[0:P, 128:258:129]),
                  R=[bk.b[0]], W=[rr.b[0]])
            for hh in range(2):
                hd = hp * 2 + hh
                k.dve(lambda e, hh=hh, hd=hd, bk=bk: e.scalar_tensor_tensor(
                    out=mix[0:P, BR + hd * 128:BR + (hd + 1) * 128], in0=bk.t[0:P, hh * 129:hh * 129 + 128],
                    scalar=rr[0:P, hd:hd + 1], in1=gate[0:P, goff + hd * 128:goff + (hd + 1) * 128],
                    op0=ALU.mult, op1=ALU.mult), R=[bk.b[0], rr.b[0], gate.b[0]], W=[mix.b[0]])
        k.release(PT)
        k.release(rr)

    def post_and_store(P, bks, xt, gb_post, dst, dst2=None, wbuf=None):
        st = k.alloc("pst", [128, 4], F32)
        junk = k.alloc("pjunk", [128, 512], BF16)
        for i, bk in enumerate(bks):
            k.act(lambda e, i=i, bk=bk: e.activation(out=junk[0:P, :], in_=bk.t[0:P, :], func=AF.Square,
                                                     accum_out=st[0:P, i:i + 1]),
                  R=[bk.b[0]], W=[junk.b[0], st.b[0]])
        k.dve(lambda e: e.tensor_tensor(out=st[0:P, 2:3], in0=st[0:P, 0:1], in1=st[0:P, 1:2], op=ALU.add),
              R=[st.b[0]], W=[st.b[0]])
        k.dve(lambda e: e.tensor_scalar(out=st[0:P, 3:4], in0=st[0:P, 2:3], scalar1=1.0 / D, scalar2=EPS,
                                        op0=ALU.mult, op1=ALU.add), R=[st.b[0]], W=[st.b[0]])
        rstd_from_ms(st, P, 3, 0)
        x1 = k.alloc("x1", [128, D], F32)
        for i, bk in enumerate(bks):
            k.dve(lambda e, i=i, bk=bk: e.scalar_tensor_tensor(out=x1[0:P, i * 512:(i + 1) * 512], in0=bk.t[0:P, :],
                                                               scalar=st[0:P, 0:1],
                                                               in1=gb_post[0:P, i * 512:(i + 1) * 512],
                                                               op0=ALU.mult, op1=ALU.mult),
                  R=[bk.b[0], st.b[0], gb_post.b[0]], W=[x1.b[0]])
        k.pool(lambda e: e.tensor_tensor(out=x1[0:P, :], in0=x1[0:P, :], in1=xt[0:P, :], op=ALU.add),
               R=[x1.b[0], xt.b[0]], W=[x1.b[0]])
        k.dma("pool", dst, x1[0:P, :], R=[x1.b[0]], W=([wbuf] if wbuf is not None else []))
        if dst2 is not None:
            k.dma("pool", dst2, x1[0:P, :], R=[x1.b[0]])
        k.release(st)
        k.release(junk)
        k.release(x1)


    def ssm_setup(stop=99, after_tables=None):
        TWO_PI = 6.283185307179586
        MAGIC = 12582912.0
        L = k.alloc("sL", [128, 2, 48], F32)
        LD = k.alloc("sLD", [128, 48], F32)
        B2 = k.alloc("sB2", [128, 2, 48, 16], F32)
        C2 = k.alloc("sC2", [128, 2, 48, 16], F32)
        dcol = k.alloc("sdcol", [128, 96], F32)
        mJ = k.alloc("smJ", [128, 128], F32)
        for t_, d_ in ((L, lam2_d), (LD, ldt2_d), (B2, B2_d), (C2, C2_d), (dcol, dcol_d), (mJ, maskJ_d)):
            k.dma("sp", t_[:], d_, W=[t_.b[0]])
        W1 = k.alloc("sW1", [128, 12, 48], F32)
        wb = W1.b[0]

        def tt(out, a, b, op, R=()):
            k.dve(lambda e: e.tensor_tensor(out=out, in0=a, in1=b, op=op), R=[wb] + list(R), W=[wb])

        k.act(lambda e: e.activation(out=W1[:, 0, :], in_=LD[:, :], func=AF.Exp), R=[LD.b[0]], W=[wb])
        tt(W1[:, 1, :], L[:, 0, :], W1[:, 0, :], ALU.mult, R=[L.b[0]])
        tt(W1[:, 2, :], L[:, 1, :], W1[:, 0, :], ALU.mult, R=[L.b[0]])
        k.act(lambda e: e.activation(out=W1[:, 3, :], in_=W1[:, 1, :], func=AF.Exp), R=[wb], W=[wb])

        def sin_of(dst, src, shift):
            k.dve(lambda e: e.tensor_scalar(out=W1[:, 8, :], in0=src, scalar1=shift, scalar2=None, op0=ALU.add),
                  R=[wb], W=[wb])
            k.dve(lambda e: e.tensor_scalar(out=W1[:, 9, :], in0=W1[:, 8, :], scalar1=1.0 / TWO_PI, scalar2=MAGIC,
                                            op0=ALU.mult, op1=ALU.add), R=[wb], W=[wb])
            k.dve(lambda e: e.tensor_scalar(out=W1[:, 10, :], in0=W1[:, 9, :], scalar1=-MAGIC, scalar2=None,
                                            op0=ALU.add), R=[wb], W=[wb])
            k.dve(lambda e: e.scalar_tensor_tensor(out=W1[:, 11, :], in0=W1[:, 10, :], scalar=-TWO_PI,
                                                   in1=W1[:, 8, :], op0=ALU.mult, op1=ALU.add), R=[wb], W=[wb])
            k.dve(lambda e: e.tensor_scalar(out=W1[:, 11, :], in0=W1[:, 11, :], scalar1=3.1415925, scalar2=-3.1415925,
                                            op0=ALU.min, op1=ALU.max), R=[wb], W=[wb])
            k.act(lambda e: e.activation(out=dst, in_=W1[:, 11, :], func=AF.Sin), R=[wb], W=[wb])

        sin_of(W1[:, 4, :], W1[:, 2, :], 0.0)
        sin_of(W1[:, 5, :], W1[:, 2, :], 1.5707963267948966)
        SC = k.alloc("SC", [128, 2, 48, 26], F32)
        P8 = k.alloc("P8", [128, 2, 48, 32], F32)
        scb = SC.b[0]

        def stt(out, a, b, op):
            k.dve(lambda e: e.tensor_tensor(out=out, in0=a, in1=b, op=op), R=[wb, scb, P8.b[0]], W=[wb, scb, P8.b[0]])

        k.dve(lambda e: e.memset(SC[:, 0, :, 0], 1.0), W=[scb])
        k.dve(lambda e: e.memset(SC[:, 1, :, 0], 0.0), W=[scb])
        stt(SC[:, 0, :, 1], W1[:, 3, :], W1[:, 5, :], ALU.mult)
        stt(SC[:, 1, :, 1], W1[:, 3, :], W1[:, 4, :], ALU.mult)

        def cmul(dr, di, xr, xi, yr, yi):
            stt(W1[:, 8, :], xr, yr, ALU.mult)
            stt(W1[:, 9, :], xi, yi, ALU.mult)
            stt(W1[:, 10, :], xr, yi, ALU.mult)
            stt(W1[:, 11, :], xi, yr, ALU.mult)
            stt(dr, W1[:, 8, :], W1[:, 9, :], ALU.subtract)
            stt(di, W1[:, 10, :], W1[:, 11, :], ALU.add)

        def sl(n):
            return SC[:, 0, :, n], SC[:, 1, :, n]

        W2 = k.alloc("sW2", [128, 4, 48, 8], F32)

        def vcmul(Td, d0, Ts, s0, m, Tm_, mi):
            Rr = [wb, scb, P8.b[0], W2.b[0]]
            xr, xi = Ts[:, 0, :, s0:s0 + m], Ts[:, 1, :, s0:s0 + m]
            yr = Tm_[:, 0, :, mi].unsqueeze(2).broadcast_to([128, 48, m])
            yi = Tm_[:, 1, :, mi].unsqueeze(2).broadcast_to([128, 48, m])
            tv = [W2[:, q, :, 0:m] for q in range(4)]
            for q, (a_, b_) in enumerate(((xr, yr), (xi, yi), (xr, yi), (xi, yr))):
                k.dve(lambda e, q=q, a_=a_, b_=b_: e.tensor_tensor(out=tv[q], in0=a_, in1=b_, op=ALU.mult),
                      R=Rr, W=[W2.b[0]])
            k.dve(lambda e: e.tensor_tensor(out=Td[:, 0, :, d0:d0 + m], in0=tv[0], in1=tv[1], op=ALU.subtract),
                  R=Rr, W=[wb, scb, P8.b[0]])
            k.dve(lambda e: e.tensor_tensor(out=Td[:, 1, :, d0:d0 + m], in0=tv[2], in1=tv[3], op=ALU.add),
                  R=Rr, W=[wb, scb, P8.b[0]])

        vcmul(SC, 2, SC, 1, 1, SC, 1)
        vcmul(SC, 3, SC, 1, 2, SC, 2)
        vcmul(SC, 5, SC, 1, 4, SC, 4)
        vcmul(SC, 9, SC, 1, 8, SC, 8)
        cmul(*sl(19), *sl(16), *sl(16))
        cmul(*sl(20), *sl(19), *sl(19))
        cmul(*sl(21), *sl(20), *sl(20))
        cmul(*sl(22), *sl(21), *sl(21))
        cmul(*sl(23), *sl(22), *sl(22))
        cmul(*sl(24), *sl(23), *sl(23))
        stt(W1[:, 8, :], SC[:, 0, :, 8], SC[:, 0, :, 8], ALU.mult)
        stt(W1[:, 9, :], SC[:, 1, :, 8], SC[:, 1, :, 8], ALU.mult)
        stt(W1[:, 8, :], W1[:, 8, :], W1[:, 9, :], ALU.add)
        k.pool(lambda e: e.tensor_tensor(out=W1[:, 9, :], in0=W1[:, 8, :], in1=cneg[:, 0:1].broadcast_to([128, 48]),
                                         op=ALU.pow), R=[wb, cneg.b[0]], W=[wb])
        stt(SC[:, 0, :, 25], W1[:, 8, :], W1[:, 9, :], ALU.mult)
        k.dve(lambda e: e.memset(SC[:, 1, :, 25], 0.0), W=[scb])
        k.dve(lambda e: e.memset(P8[:, 0, :, 0], 1.0), W=[P8.b[0], scb])
        k.dve(lambda e: e.memset(P8[:, 1, :, 0], 0.0), W=[P8.b[0], scb])
        k.dve(lambda e: e.memset(P8[:, 0, :, 16], 1.0), W=[P8.b[0], scb])
        k.dve(lambda e: e.memset(P8[:, 1, :, 16], 0.0), W=[P8.b[0], scb])
        stt(P8[:, 0, :, 1], SC[:, 0, :, 8], W1[:, 9, :], ALU.mult)
        stt(W1[:, 10, :], SC[:, 1, :, 8], W1[:, 9, :], ALU.mult)
        k.dve(lambda e: e.tensor_scalar(out=P8[:, 1, :, 1], in0=W1[:, 10, :], scalar1=-1.0, scalar2=None, op0=ALU.mult),
              R=[wb], W=[P8.b[0], scb])
        vcmul(P8, 2, P8, 1, 1, P8, 1)
        vcmul(P8, 3, P8, 1, 2, P8, 2)
        vcmul(P8, 5, P8, 1, 4, P8, 4)
        vcmul(P8, 9, P8, 1, 7, P8, 8)
        vcmul(P8, 17, P8, 8, 1, P8, 8)
        vcmul(P8, 18, P8, 17, 1, P8, 17)
        vcmul(P8, 19, P8, 17, 2, P8, 18)
        vcmul(P8, 21, P8, 17, 4, P8, 20)
        vcmul(P8, 25, P8, 17, 7, P8, 24)
        k.release(W2)

        def cinv(dn, sn):
            xr, xi = sl(sn)
            stt(W1[:, 8, :], xr, xr, ALU.mult)
            stt(W1[:, 9, :], xi, xi, ALU.mult)
            stt(W1[:, 8, :], W1[:, 8, :], W1[:, 9, :], ALU.add)
            k.dve(lambda e: e.reciprocal(out=W1[:, 9, :], in_=W1[:, 8, :]), R=[wb], W=[wb])
            stt(SC[:, 0, :, dn], xr, W1[:, 9, :], ALU.mult)
            stt(W1[:, 10, :], xi, W1[:, 9, :], ALU.mult)
            k.dve(lambda e: e.tensor_scalar(out=SC[:, 1, :, dn], in0=W1[:, 10, :], scalar1=-1.0, scalar2=None,
                                            op0=ALU.mult), R=[wb], W=[scb])

        cinv(17, 4)
        cinv(18, 7)
        PR = k.alloc("sPR", [128, 2, 48, 8], F32)
        for jp in range(8):
            for ri in range(2):
                k.dve(lambda e, jp=jp, ri=ri: e.tensor_copy(out=PR[:, ri, :, jp], in_=SC[:, ri, :, 7 - jp]),
                      R=[scb], W=[PR.b[0]])
        k.dve(lambda e: e.tensor_scalar(out=W1[:, 0, :], in0=SC[:, 0, :, 1], scalar1=-1.0, scalar2=None, op0=ALU.add),
              R=[scb], W=[wb])
        stt(W1[:, 8, :], L[:, 0, :], L[:, 0, :], ALU.mult)
        stt(W1[:, 9, :], L[:, 1, :], L[:, 1, :], ALU.mult)
        stt(W1[:, 8, :], W1[:, 8, :], W1[:, 9, :], ALU.add)
        k.dve(lambda e: e.reciprocal(out=W1[:, 1, :], in_=W1[:, 8, :]), R=[wb], W=[wb])
        stt(W1[:, 8, :], W1[:, 0, :], L[:, 0, :], ALU.mult)
        stt(W1[:, 9, :], SC[:, 1, :, 1], L[:, 1, :], ALU.mult)
        stt(W1[:, 8, :], W1[:, 8, :], W1[:, 9, :], ALU.add)
        stt(W1[:, 6, :], W1[:, 8, :], W1[:, 1, :], ALU.mult)
        stt(W1[:, 8, :], SC[:, 1, :, 1], L[:, 0, :], ALU.mult)
        stt(W1[:, 9, :], W1[:, 0, :], L[:, 1, :], ALU.mult)
        stt(W1[:, 8, :], W1[:, 8, :], W1[:, 9, :], ALU.subtract)
        stt(W1[:, 7, :], W1[:, 8, :], W1[:, 1, :], ALU.mult)
        if stop <= 1:
            return SC, P8

        Bb = k.alloc("sBb", [128, 2, 48, 16], F32)
        Cm = k.alloc("sCm", [128, 2, 48, 16], F32)
        T2 = k.alloc("sT2", [128, 4, 48, 16], F32)

        def bc16(ap):
            return ap.unsqueeze(2).broadcast_to([128, 48, 16])

        def cmul16(dst, X, yr, yi, neg_im=False):
            Rr = [wb, scb, X.b[0], T2.b[0], dst.b[0]]
            ops = ((0, X[:, 0], yr), (1, X[:, 1], yi), (2, X[:, 0], yi), (3, X[:, 1], yr))
            for i_, a_, b_ in ops:
                k.dve(lambda e, i_=i_, a_=a_, b_=b_: e.tensor_tensor(out=T2[:, i_], in0=a_, in1=bc16(b_), op=ALU.mult),
                      R=Rr, W=[T2.b[0]])
            k.dve(lambda e: e.tensor_tensor(out=dst[:, 0], in0=T2[:, 0], in1=T2[:, 1], op=ALU.subtract),
                  R=Rr, W=[dst.b[0]])
            k.dve(lambda e: e.tensor_tensor(out=dst[:, 1], in0=T2[:, 2], in1=T2[:, 3], op=ALU.add),
                  R=Rr, W=[dst.b[0]])

        cmul16(Bb, B2, W1[:, 6, :], W1[:, 7, :])
        cmul16(Cm, C2, SC[:, 0, :, 18], SC[:, 1, :, 18])
        k.release(T2)
        if stop <= 2:
            return SC, P8

        if after_tables is not None:
            after_tables()
        SPB = 4
        SGB = 2 * SPB
        NSB = 48 // SPB

        class SB_:
            pass

        def stage_a(bl):
            Bk = SB_()
            Bk.bl = bl
            ps_ = slice(bl * SPB, (bl + 1) * SPB)
            X = k.alloc("sX", [128, 6, SPB, 8, 16], F32, nb=3)
            T4 = k.alloc("sT4", [128, 4, SPB, 8, 16], F32)
            T4p = k.alloc("sT4p", [128, 4, SPB, 8, 16], F32)
            xb = X.b[0]
            xbs = [X.b[0], X.b[0], X.b[2], X.b[2]]
            shp = [128, SPB, 8, 16]

            def expand(dr, di, M, pw_lo, reverse, neg_im, eng=None, TT_=None, xb=None):
                eng = eng or k.dve
                TT_ = TT_ or T4
                xb = xb or X.b[0]
                if reverse:
                    pr = PR[:, 0, ps_, :]
                    pi = PR[:, 1, ps_, :]
                else:
                    pr = SC[:, 0, ps_, pw_lo:pw_lo + 8]
                    pi = SC[:, 1, ps_, pw_lo:pw_lo + 8]
                pr = pr.unsqueeze(3).broadcast_to(shp)
                pi = pi.unsqueeze(3).broadcast_to(shp)
                mr = M[:, 0, ps_, :].unsqueeze(2).broadcast_to(shp)
                mi = M[:, 1, ps_, :].unsqueeze(2).broadcast_to(shp)
                Rr = [scb, M.b[0], TT_.b[0], xb, PR.b[0]]
                for i_, a_, b_ in ((0, mr, pr), (1, mi, pi), (2, mr, pi), (3, mi, pr)):
                    eng(lambda e, i_=i_, a_=a_, b_=b_: e.tensor_tensor(out=TT_[:, i_], in0=a_, in1=b_, op=ALU.mult),
                        R=Rr, W=[TT_.b[0]])
                eng(lambda e: e.tensor_tensor(out=X[:, dr], in0=TT_[:, 0], in1=TT_[:, 1], op=ALU.subtract),
                    R=Rr, W=[xb])
                if neg_im:
                    k.dve(lambda e: e.scalar_tensor_tensor(out=X[:, di].rearrange("p a b c -> p (a b c)"),
                                                           in0=TT_[:, 2].rearrange("p a b c -> p (a b c)"), scalar=-1.0,
                                                           in1=TT_[:, 3].rearrange("p a b c -> p (a b c)"),
                                                           op0=ALU.mult, op1=ALU.subtract), R=Rr, W=[xb])
                else:
                    eng(lambda e: e.tensor_tensor(out=X[:, di], in0=TT_[:, 2], in1=TT_[:, 3], op=ALU.add),
                        R=Rr, W=[xb])

            expand(4, 5, C2, 1, False, False, eng=k.pool, TT_=T4p, xb=X.b[1])
            expand(0, 1, Bb, 0, True, False)
            expand(2, 3, Cm, 0, False, False, xb=X.b[2])
            k.release(T4p)
            cxb = k.alloc("scxb", [128, SPB, 2, 128], BF16)
            for ri in range(2):
                k.act(lambda e, ri=ri: e.activation(out=cxb[:, :, ri, :],
                                                    in_=X[:, 4 + ri].rearrange("p a b c -> p a (b c)"),
                                                    func=AF.Copy, scale=(1.0 if ri == 0 else -1.0)),
                      R=[X.b[1]], W=[cxb.b[0]])
            k.dma("pool", CX_d[bl * SPB:(bl + 1) * SPB].rearrange("q r x c -> r q x c"), cxb[:], R=[cxb.b[0]], W=[scrbuf])
            k.release(cxb)
            Xf = [X[:, i_].rearrange("p a b c -> p a (b c)") for i_ in range(6)]
            XH = k.alloc("sXH", [128, 4, SPB, 128], BF16)
            XL = k.alloc("sXL", [128, 4, SPB, 128], BF16)
            for i_ in range(4):
                sc_ = -1.0 if i_ == 3 else 1.0
                k.act(lambda e, i_=i_, sc_=sc_: e.activation(out=XH[:, i_], in_=Xf[i_], func=AF.Copy, scale=sc_),
                      R=[xbs[i_]], W=[XH.b[0]])
                k.dve(lambda e, i_=i_: e.tensor_tensor(out=T4[:, i_].rearrange("p a b c -> p a (b c)"), in0=Xf[i_],
                                                       in1=XH[:, i_], op=(ALU.add if i_ == 3 else ALU.subtract)),
                      R=[xbs[i_], XH.b[0]], W=[T4.b[0]])
                k.act(lambda e, i_=i_, sc_=sc_: e.activation(out=XL[:, i_], in_=T4[:, i_].rearrange("p a b c -> p a (b c)"),
                                                             func=AF.Copy, scale=sc_),
                      R=[T4.b[0]], W=[XL.b[0]])
            k.release(X)
            k.release(T4)
            Bk.XH, Bk.XL = XH, XL
            return Bk

        def stage_b(Bk):
            bl, XH, XL = Bk.bl, Bk.XH, Bk.XL
            ttb = k.alloc("sttb", [128, SGB, 128], BF16)
            bxb = k.alloc("sbxb", [128, SGB, 2, 64], BF16)
            tmps = []
            for pp in range(SPB):
                for g2 in range(2):
                    gl = pp * 2 + g2
                    g = bl * SGB + gl
                    rows = slice(64 * g2, 64 * g2 + 64)
                    bk = k.bank()
                    combos = ((XH, 0, XH, 2), (XH, 0, XL, 2), (XL, 0, XH, 2), (XH, 1, XH, 3), (XH, 1, XL, 3), (XL, 1, XH, 3))
                    mm_group(bk.t[:, 0:128], bk, [(A_[rows, ia, pp, :], B_[rows, ib, pp, :]) for (A_, ia, B_, ib) in combos],
                             R=[XH.b[0], XL.b[0]])
                    tmp = k.alloc("sttmp", [128, 128], F32)
                    tmps.append(tmp)
                    k.dve(lambda e, bk=bk, tmp=tmp: e.tensor_tensor(out=tmp[:, :], in0=bk.t[:, 0:128], in1=mJ[:, :], op=ALU.mult),
                          R=[bk.b[0], mJ.b[0]], W=[tmp.b[0]])
                    k.dve(lambda e, g=g, gl=gl, tmp=tmp: e.scalar_tensor_tensor(out=ttb[:, gl, :], in0=idf[:, :],
                                                                                scalar=dcol[:, g:g + 1], in1=tmp[:, :],
                                                                                op0=ALU.mult, op1=ALU.add),
                          R=[tmp.b[0], idf.b[0], dcol.b[0]], W=[ttb.b[0]])
                    bkt = k.bank()
                    pvb = bkt.t.bitcast(BF16)
                    for ri in range(2):
                        k.pe(lambda e, pvb=pvb, pp=pp, rows=rows, ri=ri: e.transpose(
                            out=pvb[:, ri * 64:(ri + 1) * 64], in_=XH[rows, ri, pp, :],
                            identity=idb[rows, rows]), R=[XH.b[0], idb.b[0]], W=[bkt.b[0]], inc=(ri == 1))
                    k.act(lambda e, pvb=pvb, bkt=bkt, gl=gl: e.copy(out=bxb[:, gl, :, :].rearrange("p x q -> p (x q)"),
                                                                    in_=pvb[:, 0:128]), R=[bkt.b[0]], W=[bxb.b[0]])
                    if len(tmps) > 2:
                        k.release(tmps.pop(0))
            for t_ in tmps:
                k.release(t_)
            k.dma("pool", TT_d[bl * SGB:(bl + 1) * SGB].rearrange("g r c -> r g c"), ttb[:], R=[ttb.b[0]], W=[scrbuf])
            k.dma("pool", BX_d[bl * SGB:(bl + 1) * SGB].rearrange("g r x p -> r g x p"), bxb[:], R=[bxb.b[0]], W=[scrbuf])
            k.release(ttb)
            k.release(bxb)
            k.release(XH)
            k.release(XL)

        if stop > 3:
            cur = stage_a(0)
            for bl in range(NSB):
                nxt = stage_a(bl + 1) if bl + 1 < NSB else None
                stage_b(cur)
                cur = nxt
        k.dma("pool", SC_d, SC[:], R=[scb], W=[scrbuf])
        k.dma("pool", P8_d, P8[:], R=[P8.b[0]], W=[scrbuf])
        for t_ in (L, LD, B2, C2, dcol, mJ, W1, Bb, Cm, PR, SC, P8):
            k.release(t_)
        return None, None

    def layer0(Wa_pre=None, kv_pre=None):
        if kv_pre:
            KT, Vb = kv_pre
        else:
            gb_mem0 = load_gb(4)
            KT, Vb = mem_kv(0, gb_mem0)
            k.release(gb_mem0)
        Wa = Wa_pre if Wa_pre is not None else load_w("Wa", w_in_a, D, INA)
        Wo = load_w("Wo", w_out[0], MIXW, D)
        gb_pre = load_gb(0)
        gb_post = load_gb(2)
        lng = k.alloc("lng", [128, BR], F32)
        lnb = k.alloc("lnb", [128, BR], F32)
        k.dma("sp", lng[:], lnv[0:1, :].broadcast_to([128, BR]), W=[lng.b[0]])
        k.dma("sp", lnb[:], lnv[1:2, :].broadcast_to([128, BR]), W=[lnb.b[0]])
        wtmp = k.alloc("wtmp", [128, 8, 128], F32)
        msk = k.alloc("msk", [128, 128], F32)
        k.dma("sp", wtmp[:], wsT_d, W=[wtmp.b[0]])
        k.dma("sp", msk[:], trilT_d, W=[msk.b[0]])
        wsT = k.alloc("wsT", [128, 8, 128], BF16)
        for g in range(8):
            k.dve(lambda e, g=g: e.tensor_tensor(out=wsT[:, g, :], in0=wtmp[:, g, :], in1=msk[:, :], op=ALU.mult),
                  R=[wtmp.b[0], msk.b[0]], W=[wsT.b[0]])
        k.release(wtmp)
        k.release(msk)
        wtmp2 = k.alloc("wtmp2", [64, 8, 64], F32)
        msk2 = k.alloc("msk2", [64, 64], F32)
        k.dma("sp", wtmp2[:], wsamp_d, W=[wtmp2.b[0]])
        k.dma("sp", msk2[:], mask_s_d, W=[msk2.b[0]])
        wsS = k.alloc("wsS", [64, 8, 64], BF16)
        for g in range(8):
            k.dve(lambda e, g=g: e.tensor_tensor(out=wsS[:, g, :], in0=wtmp2[:, g, :], in1=msk2[:, :], op=ALU.mult),
                  R=[wtmp2.b[0], msk2.b[0]], W=[wsS.b[0]])
        k.release(wtmp2)
        k.release(msk2)
        bsT = k.alloc("bsT", [128, 8], F32)
        bsS = k.alloc("bsS", [64, 8], F32)
        k.dma("sp", bsT[:], bsT_d, W=[bsT.b[0]])
        k.dma("sp", bsS[:], bs_s_d, W=[bsS.b[0]])

        fsets = FrontSets(2)

        def front(xsrc, P):
            return fsets.front(xsrc, P, gb_pre)

        def tile(fr, P, sample, dst, next_front):
            xt, hT = fr
            RW = [hT.b[0], Wa.b[0]]
            u = k.alloc("u", [128, BR], F32)
            v = k.alloc("v", [128, BR], F32)
            gate = k.alloc("gate", [128, MIXW if not sample else BR], F32)
            qT = k.alloc("qT", [128, 4, 128], BF16)
            gT = k.alloc("gT", [128, 4, 128], F32) if sample else None

            def tm_blocks(dstt, c0, fn, nblk=3):
                for cb in range(nblk):
                    bk = k.bank()
                    mm_group(bk.t[0:P, :], bk, [(hT[:, kc, 0:P], Wa[:, kc, c0 + cb * 512:c0 + (cb + 1) * 512])
                                                for kc in range(8)], R=RW)
                    k.act(lambda e, bk=bk, cb=cb: e.activation(out=dstt[0:P, cb * 512:(cb + 1) * 512],
                                                               in_=bk.t[0:P, :], func=fn),
                          R=[bk.b[0]], W=[dstt.b[0]])

            def fm_block(dstt, c0, fn):
                bk = k.bank()
                for hd in range(4):
                    mm_group(bk.t[:, hd * 128:hd * 128 + P], bk,
                             [(Wa[:, kc, c0 + hd * 128:c0 + (hd + 1) * 128], hT[:, kc, 0:P]) for kc in range(8)],
                             R=RW, last_inc=(hd == 3))
                k.act(lambda e, bk=bk: e.activation(out=dstt[:, :, 0:P],
                                                    in_=bk.t.rearrange("p (h t) -> p h t", h=4)[:, :, 0:P], func=fn),
                      R=[bk.b[0]], W=[dstt.b[0]])

            tm_blocks(v, BR, AF.Gelu_apprx_tanh)
            st = k.alloc("lnst", [128, 32], F32)
            for cb in range(3):
                k.dve(lambda e, cb=cb: e.bn_stats(out=st[0:P, cb * 6:(cb + 1) * 6], in_=v[0:P, cb * 512:(cb + 1) * 512]),
                      R=[v.b[0]], W=[st.b[0]])
            k.dve(lambda e: e.bn_aggr(out=st[0:P, 18:20], in_=st[0:P, 0:18]), R=[st.b[0]], W=[st.b[0]])
            k.dve(lambda e: e.tensor_scalar(out=st[0:P, 20:21], in0=st[0:P, 19:20], scalar1=EPS, scalar2=None,
                                            op0=ALU.add), R=[st.b[0]], W=[st.b[0]])
            rstd_from_ms(st, P, 20, 21)
            k.dve(lambda e: e.scalar_tensor_tensor(out=st[0:P, 22:23], in0=st[0:P, 18:19], scalar=-1.0,
                                                   in1=st[0:P, 21:22], op0=ALU.mult, op1=ALU.mult),
                  R=[st.b[0]], W=[st.b[0]])
            tm_blocks(u, 0, AF.Gelu_apprx_tanh)
            fm_block(qT, 2 * BR, AF.Copy)
            k.act(lambda e: e.activation(out=v[0:P, :], in_=v[0:P, :], func=AF.Identity, scale=st[0:P, 21:22],
                                         bias=st[0:P, 22:23]), R=[v.b[0], st.b[0]], W=[v.b[0]])
            k.dve(lambda e: e.tensor_tensor(out=v[0:P, :], in0=v[0:P, :], in1=lng[0:P, :], op=ALU.mult),
                  R=[v.b[0], lng.b[0]], W=[v.b[0]])
            vnb = k.alloc("vnb", [128, BR], BF16)
            if sample:
                k.dve(lambda e: e.tensor_tensor(out=v[0:P, :], in0=v[0:P, :], in1=lnb[0:P, :], op=ALU.add),
                      R=[v.b[0], lnb.b[0]], W=[v.b[0]])
                k.dma("pool", o_vs, v[0:P, :], R=[v.b[0]])
                k.dve(lambda e: e.tensor_copy(out=vnb[0:P, :], in_=v[0:P, :]), R=[v.b[0]], W=[vnb.b[0]])
            else:
                k.dve(lambda e: e.tensor_tensor(out=vnb[0:P, :], in0=v[0:P, :], in1=lnb[0:P, :], op=ALU.add),
                      R=[v.b[0], lnb.b[0]], W=[vnb.b[0]])
            k.release(st)
            k.release(v)
            mixT = k.alloc("mixT", [128, 16, 128], BF16)
            mix = k.alloc("mix", [128, BR if sample else MIXW], BF16)
            if sample:
                tm_blocks(gate, 2 * BR + XATT, AF.Silu)
                fm_block(gT, 2 * BR + XATT + BR, AF.Silu)
                k.release(hT)
                attn_sample(0, qT, gT, mixT)
                k.release(gT)
            else:
                tm_blocks(gate, 2 * BR + XATT, AF.Silu, nblk=4)
                k.release(hT)
                attn_prompt(P, KT, Vb, qT, gate, BR, mix)
            k.release(qT)
            k.dve(lambda e: e.tensor_tensor(out=u[0:P, :], in0=u[0:P, :], in1=gate[0:P, 0:BR], op=ALU.mult),
                  R=[u.b[0], gate.b[0]], W=[u.b[0]])
            k.release(gate)
            ws = wsS if sample else wsT
            bs = bsS if sample else bsT
            for gp in range(4):
                bk = k.bank()
                for gg in range(2):
                    g = gp * 2 + gg
                    k.pe(lambda e, bk=bk, g=g, gg=gg: e.matmul(bk.t[0:P, gg * 192:(gg + 1) * 192], lhsT=ws[0:P, g, 0:P],
                                                               rhs=vnb[0:P, g * 192:(g + 1) * 192], start=True, stop=True),
                         R=[ws.b[0], vnb.b[0]], W=[bk.b[0]], inc=(gg == 1))
                for gg in range(2):
                    g = gp * 2 + gg
                    k.dve(lambda e, bk=bk, g=g, gg=gg: e.scalar_tensor_tensor(out=mix[0:P, g * 192:(g + 1) * 192],
                                                                              in0=bk.t[0:P, gg * 192:(gg + 1) * 192],
                                                                              scalar=bs[0:P, g:g + 1],
                                                                              in1=u[0:P, g * 192:(g + 1) * 192],
                                                                              op0=ALU.add, op1=ALU.mult),
                          R=[bk.b[0], bs.b[0], u.b[0]], W=[mix.b[0]])
            k.release(vnb)
            k.release(u)
            nf = next_front() if next_front is not None else None
            transpose_to(mixT, mixT.b[0], mix, mix.b[0], P, 12 if sample else 16)
            k.release(mix)
            bks = [k.bank(), k.bank()]
            for cb in range(2):
                mm_group(bks[cb].t[0:P, :], bks[cb], [(mixT[:, kc, 0:P], Wo[:, kc, cb * 512:(cb + 1) * 512])
                                                     for kc in range(16)], R=[mixT.b[0], Wo.b[0]])
            k.release(mixT)
            post_and_store(P, bks, xt, gb_post, dst[0], dst[1], wbuf=x1buf)
            k.release(xt)
            return nf

        srcs = [(xp[t * 128:(t + 1) * 128, :], 128, False, (x1d[t * 128:(t + 1) * 128, :], None)) for t in range(NT)]
        srcs.append((xs[:, :], NS, True, (x1d[SEQ:SEQ + NS, :], None)))
        fr = front(srcs[0][0], srcs[0][1])
        for i, (src, P, smp, dst) in enumerate(srcs):
            nxt = (lambda i=i: front(srcs[i + 1][0], srcs[i + 1][1])) if i + 1 < len(srcs) else None
            fr = tile(fr, P, smp, dst, nxt)
        fsets.free()
        for t_ in (KT, Vb, Wa, Wo, gb_pre, gb_post, lng, lnb, wsT, wsS, bsT, bsS):
            k.release(t_)


    def layer1(SC, P8):
        SC = k.alloc("SC", [128, 2, 48, 26], F32)
        P8 = k.alloc("P8", [128, 2, 48, 32], F32)
        k.dma("sp", SC[:], SC_d, R=[scrbuf], W=[SC.b[0]])
        k.dma("sp", P8[:], P8_d, R=[scrbuf], W=[P8.b[0]])
        gb_mem1 = load_gb(5)
        KT, Vb = mem_kv(1, gb_mem1)
        k.release(gb_mem1)
        gb_pre = load_gb(1)
        gb_post = load_gb(3)
        UG = k.alloc("UG", [128, 2, 96, 128], BF16, nb=2)
        USG = k.alloc("USG", [NSEQ, 96, 64], BF16)
        HL = k.alloc("HL", [128, 2, 48], F32)
        scb = SC.b[0]

        def x1_rows(kt, j):
            return x1d[kt * 1024 + j:(kt + 1) * 1024:8, :]

        fsets = FrontSets(2)

        def load_norm_T(src, P):
            return fsets.front(src, P, gb_pre, xdeps=[x1buf])

        Wu = load_w("Wu", w_in_b, D, INB, c0=0, c1=BR)

        def a0_tile(fr, P, out_fn, in_fn, dbuf, next_front):
            xt, hT = fr
            k.release(xt)
            nf = next_front() if next_front is not None else None
            for cb in range(3):
                bk = k.bank()
                mm_group(bk.t[0:P, :], bk, [(hT[:, kc, 0:P], Wu[:, kc, cb * 512:(cb + 1) * 512]) for kc in range(8)],
                         R=[hT.b[0], Wu.b[0]])
                k.act(lambda e, bk=bk, cb=cb: e.copy(out=out_fn(cb), in_=in_fn(bk)), R=[bk.b[0]], W=[dbuf])
            k.release(hT)
            return nf

        UG5 = UG.t.rearrange("p t g (j c) -> p t g j c", c=16)
        us = k.alloc("us", [NS, BR], BF16)
        a0src = []
        for kt in range(2):
            for j in range(8):
                a0src.append((x1_rows(kt, j), 128,
                              (lambda cb, kt=kt, j=j: UG5[:, kt, cb * 32:(cb + 1) * 32, j, :]),
                              (lambda bk: bk.t[:, :].rearrange("p (g c) -> p g c", c=16)), UG.b[kt]))
        a0src.append((x1d[SEQ:SEQ + NS, :], NS, (lambda cb: us[0:NS, cb * 512:(cb + 1) * 512]),
                      (lambda bk: bk.t[0:NS, :]), us.b[0]))
        fr = load_norm_T(a0src[0][0], a0src[0][1])
        for i, (src, P, ofn, ifn, dbuf) in enumerate(a0src):
            nxt = (lambda i=i: load_norm_T(a0src[i + 1][0], a0src[i + 1][1])) if i + 1 < len(a0src) else None
            fr = a0_tile(fr, P, ofn, ifn, dbuf, nxt)
        uyS = k.alloc("uyS", [NSEQ, 4, BR], BF16)
        for j in range(4):
            k.dma("sp", uyS[0:NSEQ, j, :], us[16 * j:16 * j + 16, :], R=[us.b[0]], W=[uyS.b[0]])
        k.release(us)
        USG4 = USG.t.rearrange("p g (j c) -> p g j c", c=16)
        for j in range(4):
            k.pool(lambda e, j=j: e.tensor_copy(out=USG4[:, :, j, :], in_=uyS[0:NSEQ, j, :].rearrange("p (g c) -> p g c", c=16)),
                   R=[uyS.b[0]], W=[USG.b[0]])
        k.release(uyS)
        k.release(Wu)

        NPc = 256
        NPB = 4
        NGB = 2 * NPB
        NBLK = 48 // NPB

        def bc(ap, shape, axis):
            return ap.unsqueeze(axis).broadcast_to(shape)

        def gen_E(bl):
            gps_ = slice(bl * NPB, (bl + 1) * NPB)
            Eb_ = k.alloc("Eb", [128, 2, NPB, 256], F32)
            TP = k.alloc("TP", [128, 2, NPB, 256], F32)
            Rp = [Eb_.b[0], TP.b[0], P8.b[0]]
            E4 = [Eb_[:, ri].rearrange("p a (c i) -> p a c i", i=16) for ri in range(2)]
            Tv = [TP[:, q].rearrange("p a (c i) -> p a c i", i=16) for q in range(2)]
            Gr = bc(P8[:, 0, gps_, 16:32], [128, NPB, 16, 16], 3)
            Gi = bc(P8[:, 1, gps_, 16:32], [128, NPB, 16, 16], 3)
            Fr = bc(P8[:, 0, gps_, 0:16], [128, NPB, 16, 16], 2)
            Fi = bc(P8[:, 1, gps_, 0:16], [128, NPB, 16, 16], 2)

            def pv(out, a_, b_, op, W):
                k.pool(lambda e: e.tensor_tensor(out=out, in0=a_, in1=b_, op=op), R=Rp, W=W)

            pv(Tv[0], Gr, Fr, ALU.mult, [TP.b[0]])
            pv(Tv[1], Gi, Fi, ALU.mult, [TP.b[0]])
            pv(E4[0], Tv[0], Tv[1], ALU.subtract, [Eb_.b[0]])
            pv(Tv[0], Gr, Fi, ALU.mult, [TP.b[0]])
            pv(Tv[1], Gi, Fr, ALU.mult, [TP.b[0]])
            pv(E4[1], Tv[0], Tv[1], ALU.add, [Eb_.b[0]])
            k.release(TP)
            return Eb_

        class Blk:
            pass

        def s1(bl):
            B = Blk()
            B.bl = bl
            B.TTb = k.alloc("TTb", [128, NGB, 128], BF16)
            B.BXb = k.alloc("BXb", [128, NGB, 2, 64], BF16)
            B.CXb = k.alloc("CXb", [128, NPB, 2, 128], BF16)
            k.dma("sp", B.TTb[:], TT_d[bl * NGB:(bl + 1) * NGB].rearrange("g r c -> r g c"), R=[scrbuf], W=[B.TTb.b[0]])
            k.dma("sp", B.BXb[:], BX_d[bl * NGB:(bl + 1) * NGB].rearrange("g r x p -> r g x p"), R=[scrbuf], W=[B.BXb.b[0]])
            k.dma("sp", B.CXb[:], CX_d[bl * NPB:(bl + 1) * NPB].rearrange("q r x c -> r q x c"), R=[scrbuf], W=[B.CXb.b[0]])
            stS = k.alloc("stS", [NSEQ, 2, NPB * 128], F32)
            k.dma("sp", stS[:], st_d[:, :, bl * NPB * 128:(bl + 1) * NPB * 128], W=[stS.b[0]])
            B.Ut = k.alloc("Ut", [128, NGB, 272], BF16)
            Ut = B.Ut
            k.pool(lambda e: e.memset(Ut[64:128, :, 256:272], 0.0), W=[Ut.b[0]])
            for g4 in range(NGB // 4):
                bk = k.bank()
                pv = bk.t.bitcast(BF16)
                for gg in range(4):
                    g = bl * NGB + g4 * 4 + gg
                    for kt in range(2):
                        c = gg * 2 + kt
                        k.pe(lambda e, c=c, g=g, kt=kt, pv=pv: e.transpose(out=pv[:, c * 128:(c + 1) * 128],
                                                                           in_=UG[:, kt, g, :], identity=idb[:, :]),
                             R=[UG.b[kt], idb.b[0]], W=[bk.b[0]], inc=(c == 7))
                k.act(lambda e, g4=g4, pv=pv: e.copy(out=Ut[:, g4 * 4:(g4 + 1) * 4, 0:NPc],
                                                     in_=pv.rearrange("p (g t) -> p g t", g=4)),
                      R=[bk.b[0]], W=[Ut.b[0]])
            bk = k.bank()
            pv = bk.t.bitcast(BF16)
            for gl in range(NGB):
                g = bl * NGB + gl
                k.pe(lambda e, gl=gl, g=g, pv=pv: e.transpose(out=pv[0:64, gl * 16:(gl + 1) * 16],
                                                              in_=USG[0:NSEQ, g, :], identity=idb[0:NSEQ, 0:NSEQ]),
                     R=[USG.b[0], idb.b[0]], W=[bk.b[0]], inc=(gl == NGB - 1))
            k.dve(lambda e, pv=pv: e.tensor_copy(out=Ut[0:64, :, NPc:NPc + 16],
                                                 in_=pv[0:64, 0:NGB * 16].rearrange("p (g t) -> p g t", g=NGB)),
                  R=[bk.b[0]], W=[Ut.b[0]])
            bk = k.bank()
            for ri in range(2):
                for pp in range(NPB):
                    c = ri * NPB + pp
                    k.pe(lambda e, c=c, ri=ri, pp=pp, bk=bk: e.transpose(out=bk.t[:, c * 16:(c + 1) * 16],
                                                                         in_=stS[0:NSEQ, ri, pp * 128:(pp + 1) * 128],
                                                                         identity=idf[0:NSEQ, 0:NSEQ]),
                         R=[stS.b[0], idf.b[0]], W=[bk.b[0]], inc=(c == 2 * NPB - 1))
            B.H0 = k.alloc("H0", [128, 2, NPB, 16], F32)
            H0 = B.H0
            k.dve(lambda e, bk=bk: e.tensor_copy(out=H0[:].rearrange("p a b c -> p (a b c)"), in_=bk.t[:, 0:2 * NPB * 16]),
                  R=[bk.b[0]], W=[H0.b[0]])
            k.release(stS)
            B.S = k.alloc("S", [128, 2, NPB, 272], F32)
            S = B.S
            for pp in range(NPB):
                bks = [k.bank(), k.bank()]
                for ri in range(2):
                    for g2 in range(2):
                        gl = pp * 2 + g2
                        k.pe(lambda e, ri=ri, g2=g2, gl=gl, bks=bks: e.matmul(bks[ri].t[64 * g2:64 * g2 + 64, 0:272],
                                                                              lhsT=B.BXb[:, gl, ri, :], rhs=Ut[:, gl, :],
                                                                              start=True, stop=True),
                             R=[B.BXb.b[0], Ut.b[0]], W=[bks[ri].b[0]], inc=(g2 == 1))
                    k.act(lambda e, ri=ri, pp=pp, bks=bks: e.copy(out=S[:, ri, pp, :], in_=bks[ri].t[:, 0:272]),
                          R=[bks[ri].b[0]], W=[S.b[0]])
            return B

        def s2(B, Eb):
            S = B.S
            sb_ = S.b[0]
            gps = slice(B.bl * NPB, (B.bl + 1) * NPB)
            Tq = k.alloc("Tq", [128, 3, NPB, 256], F32)
            eb, tq = Eb.b[0], Tq.b[0]
            Rr = [sb_, eb, tq, scb, P8.b[0]]

            def dv(out, a_, b_, op, W):
                k.dve(lambda e: e.tensor_tensor(out=out, in0=a_, in1=b_, op=op), R=Rr, W=W)

            Sr = S[:, 0, :, 0:NPc]
            Si = S[:, 1, :, 0:NPc]
            Er, Ei = Eb[:, 0], Eb[:, 1]
            t0, t1, t2 = Tq[:, 0], Tq[:, 1], Tq[:, 2]
            dv(t0, Sr, Er, ALU.mult, [tq])
            dv(t1, Si, Ei, ALU.mult, [tq])
            dv(t2, Sr, Ei, ALU.mult, [tq])
            dv(Sr, t0, t1, ALU.subtract, [sb_])
            dv(t0, Si, Er, ALU.mult, [tq])
            dv(Si, t2, t0, ALU.add, [sb_])
            k.dve(lambda e: e.tensor_copy(out=t1, in_=bc(SC[:, 0, gps, 25], [128, NPB, 256], 2)), R=Rr, W=[tq])
            for ri in range(2):
                for q in range(NPB):
                    k.dve(lambda e, ri=ri, q=q: e.tensor_tensor_scan(out=Tq[:, 2 * ri, q, :], data0=t1[:, q, :],
                                                                      data1=S[:, ri, q, 0:NPc],
                                                                      initial=0.0, op0=ALU.mult, op1=ALU.add),
                          R=Rr, W=[tq])
            Wr, Wi = Tq[:, 0], Tq[:, 2]
            dv(t1, Wr, Er, ALU.mult, [tq])
            dv(Sr, Wi, Ei, ALU.mult, [sb_])
            dv(Sr, Sr, t1, ALU.add, [sb_])
            dv(t1, Wi, Er, ALU.mult, [tq])
            dv(Si, Wr, Ei, ALU.mult, [sb_])
            dv(Si, t1, Si, ALU.subtract, [sb_])
            k.release(Eb)
            k.release(Tq)

        def s3prep(B):
            S, H0 = B.S, B.H0
            sb_ = S.b[0]
            bl = B.bl
            ps_ = slice(bl * NPB, (bl + 1) * NPB)
            B.Hin = k.alloc("Hin", [128, 2, NPB, 272], BF16)
            Hin = B.Hin
            for ri in range(2):
                k.pool(lambda e, ri=ri: e.memset(Hin[:, ri, :, 0:1], 0.0), W=[Hin.b[0]])
                k.act(lambda e, ri=ri: e.copy(out=Hin[:, ri, :, 1:NPc], in_=S[:, ri, :, 0:NPc - 1]),
                      R=[sb_], W=[Hin.b[0]])
                k.dve(lambda e, ri=ri: e.tensor_copy(out=Hin[:, ri, :, NPc:NPc + 16], in_=H0[:, ri]),
                      R=[H0.b[0]], W=[Hin.b[0]])
                k.dve(lambda e, ri=ri: e.tensor_copy(out=HL[:, ri, ps_], in_=S[:, ri, :, NPc - 1]),
                      R=[sb_], W=[HL.b[0]])
            Tm = k.alloc("Tm", [128, 4, NPB, 16], F32)
            tb = Tm.b[0]

            def cmac(dr, di, ar, ai, xr, xi):
                tv = [Tm[:, q] for q in range(4)]
                Rr = [sb_, tb, scb, H0.b[0]]
                for q, (a_, x_) in enumerate(((ar, xr), (ai, xi), (ai, xr), (ar, xi))):
                    k.dve(lambda e, q=q, a_=a_, x_=x_: e.tensor_tensor(out=tv[q], in0=a_, in1=x_, op=ALU.mult),
                          R=Rr, W=[tb])
                k.dve(lambda e: e.tensor_tensor(out=tv[0], in0=tv[0], in1=tv[1], op=ALU.subtract), R=Rr, W=[tb])
                k.dve(lambda e: e.tensor_tensor(out=tv[2], in0=tv[2], in1=tv[3], op=ALU.add), R=Rr, W=[tb])
                k.dve(lambda e: e.tensor_tensor(out=dr, in0=dr, in1=tv[0], op=ALU.add), R=Rr, W=[sb_, tb])
                k.dve(lambda e: e.tensor_tensor(out=di, in0=di, in1=tv[2], op=ALU.add), R=Rr, W=[sb_, tb])

            A8r = bc(SC[:, 0, ps_, 8], [128, NPB, 16], 2)
            A8i = bc(SC[:, 1, ps_, 8], [128, NPB, 16], 2)
            Ss = [S[:, ri, :, NPc:NPc + 16] for ri in range(2)]
            cmac(Ss[0], Ss[1], A8r, A8i, H0[:, 0], H0[:, 1])
            HSo = k.alloc("HSo", [128, 2, NPB, 16], F32)
            k.dve(lambda e: e.memset(HSo[:], 0.0), W=[HSo.b[0], sb_])
            cmac(HSo[:, 0], HSo[:, 1], bc(SC[:, 0, ps_, 17], [128, NPB, 16], 2), bc(SC[:, 1, ps_, 17], [128, NPB, 16], 2),
                 Ss[0], Ss[1])
            hso = k.alloc("hso", [NSEQ, 2, NPB * 128], F32)
            for ri in range(2):
                bk = k.bank()
                for q in range(NPB):
                    k.pe(lambda e, q=q, ri=ri, bk=bk: e.transpose(out=bk.t[0:NSEQ, q * 128:(q + 1) * 128],
                                                                  in_=HSo[:, ri, q, :], identity=idf[:, :]),
                         R=[sb_, HSo.b[0], idf.b[0]], W=[bk.b[0]], inc=(q == NPB - 1))
                k.act(lambda e, ri=ri, bk=bk: e.copy(out=hso[0:NSEQ, ri, :], in_=bk.t[0:NSEQ, 0:NPB * 128]),
                      R=[bk.b[0]], W=[hso.b[0]])
            k.dma("pool", o_hs[:, :, bl * NPB * 128:(bl + 1) * NPB * 128], hso[:], R=[hso.b[0]])
            k.release(hso)
            k.release(HSo)
            k.release(Tm)

        def s3y(B):
            bl = B.bl

            def y_stage1(gl):
                pp, g2 = gl // 2, gl % 2
                rows = slice(64 * g2, 64 * g2 + 64)
                bk = k.bank()
                mm_group(bk.t[:, 0:272], bk,
                         [(B.TTb[:, gl, :], B.Ut[:, gl, :]),
                          (B.CXb[rows, pp, 0, :], B.Hin[rows, 0, pp, :]),
                          (B.CXb[rows, pp, 1, :], B.Hin[rows, 1, pp, :])],
                         R=[B.TTb.b[0], B.Ut.b[0], B.CXb.b[0], B.Hin.b[0]])
                Ysb = k.alloc("Ysb", [128, 272], F32)
                k.act(lambda e: e.copy(out=Ysb[:, :], in_=bk.t[:, 0:272]), R=[bk.b[0]], W=[Ysb.b[0]])
                return Ysb

            def y_stage2(gl, Ysb):
                g = bl * NGB + gl
                bk2 = k.bank()
                for kt in range(2):
                    k.pe(lambda e, kt=kt: e.transpose(out=bk2.t[:, kt * 128:(kt + 1) * 128],
                                                      in_=Ysb[:, kt * 128:(kt + 1) * 128], identity=idf[:, :]),
                         R=[Ysb.b[0], idf.b[0]], W=[bk2.b[0]], inc=False)
                k.pe(lambda e: e.transpose(out=bk2.t[0:NSEQ, 256:384], in_=Ysb[:, 256:272], identity=idf[:, :]),
                     R=[Ysb.b[0], idf.b[0]], W=[bk2.b[0]], inc=True)
                k.release(Ysb)
                k.act(lambda e: e.activation(out=UG[:, :, g, :], in_=bk2.t[:, 0:256].rearrange("p (t c) -> p t c", t=2),
                                             func=AF.Gelu_apprx_tanh), R=[bk2.b[0]], W=[UG.b[0], UG.b[1]])
                k.act(lambda e: e.activation(out=USG[0:NSEQ, g, :], in_=bk2.t[0:NSEQ, 256:320], func=AF.Gelu_apprx_tanh),
                      R=[bk2.b[0]], W=[USG.b[0]])

            pend = [y_stage1(0), y_stage1(1)]
            for gl in range(NGB):
                if gl + 2 < NGB:
                    pend.append(y_stage1(gl + 2))
                y_stage2(gl, pend.pop(0))
            for t_ in (B.TTb, B.BXb, B.CXb, B.S, B.Hin, B.Ut, B.H0):
                k.release(t_)

        cur = s1(0)
        Ecur = gen_E(0)
        Enext = gen_E(1)
        s2(cur, Ecur)
        for bl in range(NBLK):
            nxt = s1(bl + 1) if bl + 1 < NBLK else None
            s3prep(cur)
            if nxt is not None:
                En2 = gen_E(bl + 2) if bl + 2 < NBLK else None
                s2(nxt, Enext)
                Enext = En2
            s3y(cur)
            cur = nxt
        bk = k.bank()
        for ri in range(2):
            k.pe(lambda e, ri=ri: e.transpose(out=bk.t[0:48, ri * 128:(ri + 1) * 128], in_=HL[:, ri, :], identity=idf[:, :]),
                 R=[HL.b[0], idf.b[0]], W=[bk.b[0]], inc=(ri == 1))
        hlo = k.alloc("hlo", [48, 2, 128], F32)
        k.act(lambda e: e.copy(out=hlo[:].rearrange("p a b -> p (a b)"), in_=bk.t[0:48, 0:256]), R=[bk.b[0]], W=[hlo.b[0]])
        k.dma("pool", o_hp.rearrange("r q c -> q r c"), hlo[:], R=[hlo.b[0]])
        k.release(hlo)
        k.release(HL)
        k.release(SC)
        k.release(P8)
        Wg = load_w("Wg", w_glu, BR, BR)
        uyP = k.alloc("uyP", [128, 2, 8, BR], BF16, nb=16)
        uyS = k.alloc("uyS", [NSEQ, 4, BR], BF16)
        cnt = 0
        for kt in range(2):
            for j in range(8):
                sel = cnt % 5
                cnt += 1
                if sel in (0, 2):
                    k.act(lambda e, kt=kt, j=j: e.copy(out=uyP[:, kt, j, :].rearrange("p (g c) -> p g c", c=16),
                                                       in_=UG5[:, kt, :, j, :]),
                          R=[UG.b[kt]], W=[uyP.b[kt * 8 + j]])
                else:
                    eng = k.pool if sel == 4 else k.dve
                    eng(lambda e, kt=kt, j=j: e.tensor_copy(out=uyP[:, kt, j, :].rearrange("p (g c) -> p g c", c=16),
                                                            in_=UG5[:, kt, :, j, :]),
                        R=[UG.b[kt]], W=[uyP.b[kt * 8 + j]])
        for j in range(4):
            k.pool(lambda e, j=j: e.tensor_copy(out=uyS[0:NSEQ, j, :].rearrange("p (g c) -> p g c", c=16),
                                                in_=USG4[:, :, j, :]), R=[USG.b[0]], W=[uyS.b[0]])
        k.release(UG)
        k.release(USG)

        Wqg = load_w("Wqg", w_in_b, D, INB, c0=BR, c1=INB)
        bgl = k.alloc("bgl", [1, BR], BF16)
        k.dma("pool", bgl[:], bglu_d, W=[bgl.b[0]])
        ys = k.alloc("ys", [NS, BR], BF16)
        for j in range(4):
            k.dma("sp", ys[16 * j:16 * j + 16, :], uyS[0:NSEQ, j, :], R=[uyS.b[0]], W=[ys.b[0]])

        def glu_front(y_ap, ybuf, P):
            yT = k.alloc("yT", [128, 12, 128], BF16)
            transpose_to(yT, yT.b[0], y_ap, ybuf, P, 12)
            return yT

        def glu_tile(yT, y_ap, ybuf, P, next_front):
            nf = next_front() if next_front is not None else None
            for cb in range(3):
                bk = k.bank()
                pairs = [(yT[:, kc, 0:P], Wg[:, kc, cb * 512:(cb + 1) * 512]) for kc in range(12)]
                pairs.append((ones_b[0:1, 0:P], bgl[0:1, cb * 512:(cb + 1) * 512]))
                mm_group(bk.t[0:P, :], bk, pairs, R=[yT.b[0], Wg.b[0], ones_b.b[0], bgl.b[0]])
                sig = k.alloc("sig", [128, 512], F32)
                k.act(lambda e, bk=bk, sig=sig: e.activation(out=sig[0:P, :], in_=bk.t[0:P, :], func=AF.Sigmoid),
                      R=[bk.b[0]], W=[sig.b[0]])
                k.dve(lambda e, cb=cb, sig=sig: e.tensor_tensor(out=y_ap[0:P, cb * 512:(cb + 1) * 512],
                                                                in0=y_ap[0:P, cb * 512:(cb + 1) * 512],
                                                                in1=sig[0:P, :], op=ALU.mult),
                      R=[sig.b[0], ybuf], W=[ybuf])
                k.release(sig)
            k.release(yT)
            return nf

        gsrc = [(uyP[:, kt, j, :], uyP.b[kt * 8 + j], 128) for kt in range(2) for j in range(8)]
        gsrc.append((ys, ys.b[0], NS))
        yT = glu_front(*gsrc[0])
        for i, (y_ap, ybuf, P) in enumerate(gsrc):
            nxt = (lambda i=i: glu_front(*gsrc[i + 1])) if i + 1 < len(gsrc) else None
            yT = glu_tile(yT, y_ap, ybuf, P, nxt)
        k.release(Wg)
        k.release(bgl)

        Wo = load_w("Wo1", w_out[1], MIXW, D)

        def b_tile(fr, P, sample, br_ap, brbuf, dst, next_front):
            xt, hT = fr
            RW = [hT.b[0], Wqg.b[0]]
            qT = k.alloc("qT", [128, 4, 128], BF16)
            gT = k.alloc("gT", [128, 4, 128], F32) if sample else None
            gate = k.alloc("gate", [128, BR if sample else MIXW], F32)

            def fm_block(dstt, c0, fn):
                bk = k.bank()
                for hd in range(4):
                    mm_group(bk.t[:, hd * 128:hd * 128 + P], bk,
                             [(Wqg[:, kc, c0 + hd * 128:c0 + (hd + 1) * 128], hT[:, kc, 0:P]) for kc in range(8)],
                             R=RW, last_inc=(hd == 3))
                k.act(lambda e, bk=bk: e.activation(out=dstt[:, :, 0:P],
                                                    in_=bk.t.rearrange("p (h t) -> p h t", h=4)[:, :, 0:P], func=fn),
                      R=[bk.b[0]], W=[dstt.b[0]])

            fm_block(qT, 0, AF.Copy)
            if sample:
                fm_block(gT, XATT + BR, AF.Silu)
            for cb in range(3 if sample else 4):
                bk = k.bank()
                mm_group(bk.t[0:P, :], bk, [(hT[:, kc, 0:P], Wqg[:, kc, XATT + cb * 512:XATT + (cb + 1) * 512])
                                            for kc in range(8)], R=RW)
                k.act(lambda e, bk=bk, cb=cb: e.activation(out=gate[0:P, cb * 512:(cb + 1) * 512], in_=bk.t[0:P, :],
                                                          func=AF.Silu), R=[bk.b[0]], W=[gate.b[0]])
            k.release(hT)
            mixT = k.alloc("mixT", [128, 16, 128], BF16)
            mix = k.alloc("mix", [128, BR if sample else MIXW], BF16)
            k.dve(lambda e: e.tensor_tensor(out=mix[0:P, 0:BR], in0=gate[0:P, 0:BR], in1=br_ap, op=ALU.mult),
                  R=[gate.b[0], brbuf], W=[mix.b[0]])
            if sample:
                attn_sample(1, qT, gT, mixT)
                k.release(gT)
            else:
                attn_prompt(P, KT, Vb, qT, gate, BR, mix)
            k.release(gate)
            k.release(qT)
            transpose_to(mixT, mixT.b[0], mix, mix.b[0], P, 12 if sample else 16)
            k.release(mix)
            nf = next_front() if next_front is not None else None
            bks = [k.bank(), k.bank()]
            for cb in range(2):
                mm_group(bks[cb].t[0:P, :], bks[cb], [(mixT[:, kc, 0:P], Wo[:, kc, cb * 512:(cb + 1) * 512])
                                                     for kc in range(16)], R=[mixT.b[0], Wo.b[0]])
            k.release(mixT)
            post_and_store(P, bks, xt, gb_post, dst)
            k.release(xt)
            return nf

        bsrc = []
        for kt in range(2):
            for j in range(8):
                bsrc.append((x1_rows(kt, j), 128, False, uyP[:, kt, j, :], uyP.b[kt * 8 + j],
                             o_yp[kt * 1024 + j:(kt + 1) * 1024:8, :]))
        bsrc.append((x1d[SEQ:SEQ + NS, :], NS, True, ys[0:NS, :], ys.b[0], o_ys[:, :]))
        fr = load_norm_T(bsrc[0][0], bsrc[0][1])
        for i, (src, P, smp, br_ap, brbuf, dst) in enumerate(bsrc):
            nxt = (lambda i=i: load_norm_T(bsrc[i + 1][0], bsrc[i + 1][1])) if i + 1 < len(bsrc) else None
            fr = b_tile(fr, P, smp, br_ap, brbuf, dst, nxt)
        fsets.free()
        for t_ in (KT, Vb, Wqg, Wo, gb_pre, gb_post, uyP, uyS, ys):
            k.release(t_)

    SC = P8 = None
    Wa_pre = None
    if "l0" in parts and "setup" in parts:
        Wa_pre = load_w("Wa", w_in_a, D, INA)
    kv_pre = []

    def early_kv():
        gb_mem0 = load_gb(4)
        kv_pre.extend(mem_kv(0, gb_mem0))
        k.release(gb_mem0)

    if "setup" in parts:
        SC, P8 = ssm_setup(stop, after_tables=(early_kv if "l0" in parts else None))
    if "l0" in parts:
        layer0(Wa_pre, kv_pre)
    if "l1" in parts:
        layer1(SC, P8)

    k.finish()
    return nc


_PROGRAM = None
_BUILD_ARGS = None
_LAST = None


def _get_program():
    global _PROGRAM
    if _PROGRAM is None:
        _PROGRAM = build_program()
    return _PROGRAM


def _l2(x):
    return np.ascontiguousarray(x.reshape(2, 48, 2, 64).transpose(2, 3, 0, 1).reshape(128, 2, 48))


def kernel(x_prompt, x_sample, cache_mem_k, cache_mem_v, state_ssm_re, state_ssm_im, mem_prompt,
           w_in_a, ln_v_g, ln_v_b, w_spatial, b_spatial,
           w_in_b, ssm_lambda_re, ssm_lambda_im, ssm_log_dt, ssm_b_re, ssm_b_im, ssm_c_re, ssm_c_im,
           ssm_d, w_glu, b_glu,
           mem_norm_g, w_mem_k, w_mem_v, w_out, pre_norm_g, post_norm_g):
    import ml_dtypes
    f32 = np.float32
    A = lambda a: np.ascontiguousarray(np.asarray(a), dtype=f32)
    gvec = np.concatenate([A(pre_norm_g), A(post_norm_g), A(mem_norm_g)], axis=0)
    shared = {
        "w_mem_k": A(w_mem_k), "w_mem_v": A(w_mem_v), "gvec": gvec,
        "w_in_a": A(w_in_a)[0], "w_out": A(w_out),
        "lnv": np.concatenate([A(ln_v_g), A(ln_v_b)], axis=0),
        "wsT": np.ascontiguousarray(A(w_spatial)[0].transpose(2, 0, 1)),
        "trilT": np.triu(np.ones((128, 128), f32)),
        "bsT": np.ascontiguousarray(A(b_spatial)[0].T),
        "wsamp": np.ascontiguousarray(np.broadcast_to(
            A(w_spatial)[0][:, :4, :4].transpose(2, 0, 1)[:, None, :, :, None], (4, 16, 8, 4, 16)).reshape(64, 8, 64)),
        "mask_s": np.ascontiguousarray((np.triu(np.ones((4, 4), f32))[:, None, :, None]
                                        * np.eye(16, dtype=f32)[None, :, None, :]).reshape(64, 64)),
        "bs_s": np.ascontiguousarray(np.repeat(A(b_spatial)[0][:, :4].T, 16, axis=0)),
        "w_in_b": A(w_in_b)[0], "w_glu": A(w_glu)[0], "bglu": A(b_glu),
        "lam2": _l2(np.stack([A(ssm_lambda_re)[0], A(ssm_lambda_im)[0]], 0)),
        "ldt2": np.ascontiguousarray(np.broadcast_to(A(ssm_log_dt)[0].reshape(48, 2).T[:, None, :], (2, 64, 48)).reshape(128, 48)),
        "B2": np.ascontiguousarray(np.stack([A(ssm_b_re)[0], A(ssm_b_im)[0]], 0).reshape(2, 48, 2, 64, 16)
                                   .transpose(2, 3, 0, 1, 4).reshape(128, 2, 48, 16)),
        "C2": np.ascontiguousarray(np.stack([A(ssm_c_re)[0], A(ssm_c_im)[0]], 0).reshape(2, 48, 2, 16, 64)
                                   .transpose(2, 4, 0, 1, 3).reshape(128, 2, 48, 16)),
        "dcol": np.ascontiguousarray(np.tile(A(ssm_d)[0].reshape(96, 16).T, (8, 1))),
        "maskJ": np.ascontiguousarray(np.kron(np.triu(np.ones((8, 8), f32)), np.ones((16, 16), f32))),
        "ident_f": np.eye(128, dtype=f32), "ident_b": np.eye(128, dtype=f32).astype(ml_dtypes.bfloat16),
    }
    x_prompt = A(x_prompt); x_sample = A(x_sample); mem_prompt = A(mem_prompt)
    cache_mem_k = A(cache_mem_k); cache_mem_v = A(cache_mem_v)
    state_ssm_re = A(state_ssm_re); state_ssm_im = A(state_ssm_im)
    in_maps = []
    for c in range(NCORES):
        m = dict(shared)
        m["xp"] = x_prompt[c]
        xs_c = x_sample[c * NSEQ:(c + 1) * NSEQ]
        m["xs"] = np.ascontiguousarray(xs_c.transpose(1, 0, 2).reshape(NS, D))
        m["mem"] = mem_prompt[c]
        m["st"] = np.ascontiguousarray(np.stack([state_ssm_re[0, c * NSEQ:(c + 1) * NSEQ].reshape(NSEQ, 6144),
                                                 state_ssm_im[0, c * NSEQ:(c + 1) * NSEQ].reshape(NSEQ, 6144)], axis=1))
        m["kc"] = np.ascontiguousarray(cache_mem_k[:, c * NSEQ:(c + 1) * NSEQ].reshape(2, NSEQ, NMEM, XATT))
        m["vc"] = np.ascontiguousarray(cache_mem_v[:, c * NSEQ:(c + 1) * NSEQ].reshape(2, NSEQ, NMEM, XATT))
        in_maps.append(m)
    if _BUILD_ARGS is not None:
        nc = build_program(**_BUILD_ARGS)
        res = run_bass_kernel_spmd(nc, in_maps[:1], core_ids=[0])
        global _LAST
        _LAST = res.results
        return None
    nc = _get_program()
    res = run_bass_kernel_spmd(nc, in_maps, core_ids=list(range(NCORES)))
    R = res.results
    y_prompt = np.stack([R[c]["o_yp"] for c in range(NCORES)], axis=0)
    y_sample = np.concatenate([R[c]["o_ys"].reshape(4, NSEQ, D).transpose(1, 0, 2) for c in range(NCORES)], axis=0)
    mk = np.stack([R[c]["o_mk"] for c in range(NCORES)], axis=1).reshape(2, NCORES, NMEM, 4, 128)
    mv = np.stack([R[c]["o_mv"] for c in range(NCORES)], axis=1).reshape(2, NCORES, NMEM, 4, 128)
    hp_re = np.stack([R[c]["o_hp"][0].reshape(96, 64) for c in range(NCORES)], axis=0)[None]
    hp_im = np.stack([R[c]["o_hp"][1].reshape(96, 64) for c in range(NCORES)], axis=0)[None]
    hs_re = np.concatenate([R[c]["o_hs"][:, 0].reshape(NSEQ, 96, 64) for c in range(NCORES)], axis=0)[None]
    hs_im = np.concatenate([R[c]["o_hs"][:, 1].reshape(NSEQ, 96, 64) for c in range(NCORES)], axis=0)[None]
    v_s = np.concatenate([R[c]["o_vs"].reshape(4, NSEQ, BR).transpose(1, 0, 2) for c in range(NCORES)], axis=0)[None]
    return (y_prompt, y_sample, mk, mv, hp_re, hp_im, hs_re, hs_im, v_s)
```

```python
import numpy as np
import concourse.bass as bass
import concourse.mybir as mybir
from concourse.bass_utils import run_bass_kernel_spmd

F32 = mybir.dt.float32
BF16 = mybir.dt.bfloat16
AF = mybir.ActivationFunctionType
ALU = mybir.AluOpType

D = 1024
SEQ = 2048
NT = SEQ // 128
NS = 64
NSEQ = 16
BR = 1536
XATT = 512
MIXW = 2048
NMEM = 256
INA = 2 * BR + XATT + MIXW
INB = BR + XATT + MIXW
EPS = 1e-6
NCORES = 8

SB_BASE = 16512
SB_TOP = 229344
NDS = 40


def _dsize(dt):
    return 4 if dt == F32 else 2


class Buf:
    __slots__ = ("w", "r")

    def __init__(self):
        self.w = {}
        self.r = {}


def _merge(d, key, val):
    if d.get(key, 0) < val:
        d[key] = val


class T:
    def __init__(self, handle, off, nbytes, nb):
        self.t = handle
        self.off = off
        self.nbytes = nbytes
        self.b = [Buf() for _ in range(nb)]

    def __getitem__(self, idx):
        return self.t[idx]


class KB:
    def __init__(self, nc):
        self.nc = nc
        self.eng = {"pe": nc.tensor, "act": nc.scalar, "dve": nc.vector, "pool": nc.gpsimd, "sp": nc.sync}
        self.esem = {e: nc.alloc_semaphore("sem_" + e) for e in ("pe", "act", "dve", "pool")}
        self.ecnt = {e: 0 for e in self.esem}
        self.seen = {e: {} for e in self.eng}
        self.pending = {e: [] for e in self.esem}
        self.dsem = [nc.alloc_semaphore("dsem%d" % i) for i in range(NDS)]
        self.dcnt = [0] * NDS
        self.dnext = 0
        self.free = [(SB_BASE, SB_TOP)]
        self.dead = []
        self.nalloc = 0
        self.banks = []
        for i in range(8):
            h = nc.alloc_psum_tensor("bank%d" % i, [128, 512], F32)
            self.banks.append(T(h, 0, 0, 1))
        self.bnext = 0

    def alloc(self, name, shape, dtype, nb=1):
        n = 1
        for s in shape[1:]:
            n *= s
        nbytes = (n * _dsize(dtype) + 63) // 64 * 64
        for i, (s, e) in enumerate(self.free):
            if e - s >= nbytes:
                off = s
                if e - s == nbytes:
                    self.free.pop(i)
                else:
                    self.free[i] = (s + nbytes, e)
                break
        else:
            raise RuntimeError("SBUF arena full allocating %s (%d B); free=%s" % (name, nbytes, self.free))
        self.nalloc += 1
        h = self.nc.alloc_sbuf_tensor_at("%s_%d" % (name, self.nalloc), list(shape), dtype, offset=off)
        t = T(h, off, nbytes, nb)
        keep = []
        for (s, e, bufs) in self.dead:
            if s < off + nbytes and off < e:
                for ob in bufs:
                    for nbuf in t.b:
                        for k, v in ob.w.items():
                            _merge(nbuf.r, k, v)
                        for k, v in ob.r.items():
                            _merge(nbuf.r, k, v)
                if not (s >= off and e <= off + nbytes):
                    keep.append((s, e, bufs))
            else:
                keep.append((s, e, bufs))
        self.dead = keep
        return t

    def release(self, t, force=False):
        if getattr(t, "persistent", False) and not force:
            return
        self.dead.append((t.off, t.off + t.nbytes, t.b))
        fr = self.free + [(t.off, t.off + t.nbytes)]
        fr.sort()
        out = []
        for s, e in fr:
            if out and out[-1][1] == s:
                out[-1] = (out[-1][0], e)
            else:
                out.append((s, e))
        self.free = out

    def bank(self, hold=False):
        while True:
            b = self.banks[self.bnext]
            self.bnext = (self.bnext + 1) % 8
            if not getattr(b, "held", False):
                break
        if hold:
            b.held = True
        return b

    def unhold(self, b):
        b.held = False

    def _handle(self, key):
        return self.esem[key[1]] if key[0] == "e" else self.dsem[key[1]]

    def _waits(self, E, R, W, extra=None):
        deps = {}
        for b in R:
            for k, v in b.w.items():
                _merge(deps, k, v)
        for b in W:
            for k, v in b.w.items():
                _merge(deps, k, v)
            for k, v in b.r.items():
                _merge(deps, k, v)
        if extra:
            for k, v in extra.items():
                _merge(deps, k, v)
        eng = self.eng[E]
        seen = self.seen[E]
        for k, v in deps.items():
            if E == "pe" and k == ("e", "pe"):
                continue
            if v <= 0 or seen.get(k, 0) >= v:
                continue
            eng.wait_ge(self._handle(k), v)
            seen[k] = v

    def _record(self, tok, R, W):
        for b in R:
            _merge(b.r, tok[0], tok[1])
        for b in W:
            _merge(b.w, tok[0], tok[1])

    def op(self, E, fn, R=(), W=(), inc=True):
        self._waits(E, R, W)
        ins = fn(self.eng[E])
        if inc:
            self.ecnt[E] += 1
            ins.then_inc(self.esem[E], 1)
            tok = (("e", E), self.ecnt[E])
            self._record(tok, R, W)
            for (r2, w2) in self.pending[E]:
                self._record(tok, r2, w2)
            self.pending[E] = []
        else:
            assert E == "pe"
            self.pending[E].append((list(R), list(W)))
        return ins

    def pe(self, fn, R=(), W=(), inc=True):
        return self.op("pe", fn, R, W, inc)

    def act(self, fn, R=(), W=()):
        return self.op("act", fn, R, W)

    def dve(self, fn, R=(), W=()):
        return self.op("dve", fn, R, W)

    def pool(self, fn, R=(), W=()):
        return self.op("pool", fn, R, W)

    def dma(self, q, out, in_, R=(), W=(), **kw):
        i = self.dnext
        self.dnext = (i + 1) % NDS
        self._waits(q, R, W, extra={("d", i): self.dcnt[i]})
        self.dcnt[i] += 16
        self.eng[q].dma_start(out=out, in_=in_, **kw).then_inc(self.dsem[i], 16)
        tok = (("d", i), self.dcnt[i])
        self._record(tok, R, W)

    def finish(self):
        sp = self.eng["sp"]
        for i in range(NDS):
            if self.dcnt[i] > 0:
                sp.wait_ge(self.dsem[i], self.dcnt[i])
        for e in self.esem:
            if self.ecnt[e] > 0:
                sp.wait_ge(self.esem[e], self.ecnt[e])


def build_program(parts=("l0", "setup", "l1"), debug=False, stop=99):
    nc = bass.Bass("TRN2", target_bir_lowering=False)
    k = KB(nc)

    def din(name, shape, dt=F32):
        return nc.dram_tensor(name, list(shape), dt, kind="ExternalInput").ap()

    def dout(name, shape, dt=F32):
        return nc.dram_tensor(name, list(shape), dt, kind="ExternalOutput").ap()

    xp = din("xp", [SEQ, D])
    xs = din("xs", [NS, D])
    mem = din("mem", [NMEM, D])
    w_mem_k = din("w_mem_k", [2, D, XATT])
    w_mem_v = din("w_mem_v", [2, D, XATT])
    gvec = din("gvec", [6, D])
    ident_f = din("ident_f", [128, 128])
    ident_b = din("ident_b", [128, 128], BF16)

    w_in_a = din("w_in_a", [D, INA])
    w_out = din("w_out", [2, MIXW, D])
    lnv = din("lnv", [2, BR])
    wsT_d = din("wsT", [128, 8, 128])
    trilT_d = din("trilT", [128, 128])
    bsT_d = din("bsT", [128, 8])
    wsamp_d = din("wsamp", [64, 8, 64])
    mask_s_d = din("mask_s", [64, 64])
    bs_s_d = din("bs_s", [64, 8])
    kc_d = din("kc", [2, NSEQ, NMEM, XATT])
    vc_d = din("vc", [2, NSEQ, NMEM, XATT])
    x1d = nc.dram_tensor("x1d", [SEQ + NS, D], F32, kind="Internal").ap()
    lam2_d = din("lam2", [128, 2, 48])
    ldt2_d = din("ldt2", [128, 48])
    B2_d = din("B2", [128, 2, 48, 16])
    C2_d = din("C2", [128, 2, 48, 16])
    dcol_d = din("dcol", [128, 96])
    maskJ_d = din("maskJ", [128, 128])
    dk = "ExternalOutput" if debug else "Internal"
    TT_d = nc.dram_tensor("TT_d", [96, 128, 128], BF16, kind=dk).ap()
    BX_d = nc.dram_tensor("BX_d", [96, 128, 2, 64], BF16, kind=dk).ap()
    CX_d = nc.dram_tensor("CX_d", [48, 128, 2, 128], BF16, kind=dk).ap()
    SC_d = nc.dram_tensor("SC_d", [128, 2, 48, 26], F32, kind=dk).ap()
    P8_d = nc.dram_tensor("P8_d", [128, 2, 48, 32], F32, kind="Internal").ap()

    o_yp = dout("o_yp", [SEQ, D])
    o_ys = dout("o_ys", [NS, D])
    o_mk = dout("o_mk", [2, NMEM, XATT])
    o_mv = dout("o_mv", [2, NMEM, XATT])
    o_vs = dout("o_vs", [NS, BR])
    o_hp = dout("o_hp", [2, 48, 128])
    o_hs = dout("o_hs", [NSEQ, 2, 6144])
    w_in_b = din("w_in_b", [D, INB])
    w_glu = din("w_glu", [BR, BR])
    bglu_d = din("bglu", [1, BR])
    st_d = din("st", [NSEQ, 2, 6144])

    idf = k.alloc("idf", [128, 128], F32)
    idb = k.alloc("idb", [128, 128], BF16)
    k.dma("sp", idf[:], ident_f, W=[idf.b[0]])
    k.dma("sp", idb[:], ident_b, W=[idb.b[0]])
    cneg = k.alloc("cneg", [128, 1], F32)
    k.pool(lambda e: e.memset(cneg[:], -0.5), W=[cneg.b[0]])

    def load_gb(row):
        t = k.alloc("gb", [128, D], F32)
        k.dma("sp", t[:], gvec[row:row + 1, :].broadcast_to([128, D]), W=[t.b[0]])
        return t

    def load_w(name, src, rows, cols, c0=0, c1=None, q="pool"):
        c1 = cols if c1 is None else c1
        kc = rows // 128
        t = k.alloc(name, [128, kc, c1 - c0], BF16)
        v = src.rearrange("(kc p) n -> p kc n", p=128)
        step = 1024
        for cb in range(c0, c1, step):
            ce = min(c1, cb + step)
            k.dma(q, t[:, :, cb - c0:ce - c0], v[:, :, cb:ce], W=[t.b[0]])
        return t

    def rmsnorm_bf(xt, xbuf, P, gb, bufs=None):
        if bufs is not None:
            junk, ss, h_pre = bufs
        else:
            junk = k.alloc("junk", [128, D], BF16)
            ss = k.alloc("ss", [128, 4], F32)
            h_pre = None
        k.act(lambda e: e.activation(out=junk[0:P, :], in_=xt, func=AF.Square, accum_out=ss[0:P, 0:1]),
              R=[xbuf], W=[junk.b[0], ss.b[0]])
        k.dve(lambda e: e.tensor_scalar(out=ss[0:P, 1:2], in0=ss[0:P, 0:1], scalar1=1.0 / D, scalar2=EPS,
                                        op0=ALU.mult, op1=ALU.add), R=[ss.b[0]], W=[ss.b[0]])
        k.pool(lambda e: e.tensor_tensor(out=ss[0:P, 2:3], in0=ss[0:P, 1:2], in1=cneg[0:P, :], op=ALU.pow),
               R=[ss.b[0], cneg.b[0]], W=[ss.b[0]])
        h = h_pre if h_pre is not None else k.alloc("h", [128, D], BF16)
        k.dve(lambda e: e.scalar_tensor_tensor(out=h[0:P, :], in0=xt, scalar=ss[0:P, 2:3], in1=gb[0:P, :],
                                               op0=ALU.mult, op1=ALU.mult),
              R=[xbuf, ss.b[0], gb.b[0]], W=[h.b[0]])
        k.release(junk)
        k.release(ss)
        return h

    def transpose_to(dst, dbuf, src, sbuf, P, nchunk, col0=0):
        done = 0
        while done < nchunk:
            n = min(8, nchunk - done)
            bk = k.bank()
            pv = bk.t.bitcast(BF16)
            for c in range(n):
                cc = done + c
                k.pe(lambda e, c=c, cc=cc: e.transpose(out=pv[:, c * 128:c * 128 + P],
                                                       in_=src[0:P, cc * 128:(cc + 1) * 128],
                                                       identity=idb[0:P, 0:P]),
                     R=[sbuf, idb.b[0]], W=[bk.b[0]], inc=(c == n - 1))
            pview = pv.rearrange("p (c t) -> p c t", t=128)
            k.dve(lambda e, n=n, d0=done: e.tensor_copy(out=dst[:, d0:d0 + n, col0:col0 + P],
                                                         in_=pview[:, 0:n, 0:P]),
                  R=[bk.b[0]], W=[dbuf])
            done += n

    class FrontSets:
        def __init__(self, n=2):
            self.sets = []
            for _ in range(n):
                d = dict(xt=k.alloc("fxt", [128, D], F32), junk=k.alloc("fjunk", [128, D], BF16),
                         ss=k.alloc("fss", [128, 4], F32), h=k.alloc("fh", [128, D], BF16),
                         hT=k.alloc("fhT", [128, 8, 128], BF16))
                for t_ in d.values():
                    t_.persistent = True
                self.sets.append(d)
            self.i = 0

        def front(self, xsrc, P, gb, xdeps=()):
            d = self.sets[self.i % len(self.sets)]
            self.i += 1
            xt, hT = d["xt"], d["hT"]
            k.dma("sp", xt[0:P, :], xsrc, R=list(xdeps), W=[xt.b[0]])
            h = rmsnorm_bf(xt[0:P, :], xt.b[0], P, gb, bufs=(d["junk"], d["ss"], d["h"]))
            transpose_to(hT, hT.b[0], h, h.b[0], P, 8)
            return xt, hT

        def free(self):
            for d in self.sets:
                for t_ in d.values():
                    k.release(t_, force=True)

    x1buf = Buf()
    scrbuf = Buf()

    def mem_kv(layer, gb_mem):
        wk = load_w("wk", w_mem_k[layer], D, XATT)
        wv = load_w("wv", w_mem_v[layer], D, XATT)
        mT = k.alloc("mT", [128, 8, NMEM], BF16)
        for mt in range(2):
            xt = k.alloc("memx", [128, D], F32)
            k.dma("sp", xt[:], mem[mt * 128:(mt + 1) * 128, :], W=[xt.b[0]])
            h = rmsnorm_bf(xt[:, :], xt.b[0], 128, gb_mem)
            transpose_to(mT, mT.b[0], h, h.b[0], 128, 8, col0=mt * 128)
            k.release(h)
            k.release(xt)
        KT = k.alloc("KT", [128, 4, NMEM], BF16)
        Vb = k.alloc("Vb", [128, 2, 4, 132], BF16)
        k.pool(lambda e: e.memset(Vb[:, :, :, 128:132], 1.0), W=[Vb.b[0]])
        for mt in range(2):
            for which, w, odram in ((0, wk, o_mk), (1, wv, o_mv)):
                bk = k.bank()
                for kc in range(8):
                    k.pe(lambda e, kc=kc: e.matmul(bk.t[:, :], lhsT=mT[:, kc, mt * 128:(mt + 1) * 128],
                                                   rhs=w[:, kc, :], start=(kc == 0), stop=(kc == 7)),
                         R=[mT.b[0], w.b[0]], W=[bk.b[0]], inc=(kc == 7))
                st = k.alloc("kvst", [128, XATT], F32)
                k.act(lambda e: e.copy(out=st[:, :], in_=bk.t[:, :]), R=[bk.b[0]], W=[st.b[0]])
                if which == 1:
                    k.dve(lambda e: e.tensor_copy(out=Vb[:, mt, :, 0:128], in_=st[:, :].rearrange("p (h d) -> p h d", h=4)),
                          R=[st.b[0]], W=[Vb.b[0]])
                k.dma("pool", odram[layer, mt * 128:(mt + 1) * 128, :], st[:, :], R=[st.b[0]])
                k.release(st)
        for hp in range(2):
            bk = k.bank()
            for hh in range(2):
                hd = hp * 2 + hh
                for kc in range(8):
                    k.pe(lambda e, kc=kc, hd=hd, hh=hh: e.matmul(bk.t[:, hh * 256:(hh + 1) * 256],
                                                                 lhsT=wk[:, kc, hd * 128:(hd + 1) * 128],
                                                                 rhs=mT[:, kc, :], start=(kc == 0), stop=(kc == 7)),
                         R=[mT.b[0], wk.b[0]], W=[bk.b[0]], inc=(kc == 7 and hh == 1))
            k.act(lambda e, hp=hp: e.copy(out=KT[:, hp * 2:hp * 2 + 2, :],
                                          in_=bk.t.rearrange("p (h m) -> p h m", h=2)),
                  R=[bk.b[0]], W=[KT.b[0]])
        k.release(mT)
        k.release(wk)
        k.release(wv)
        return KT, Vb

    ones_b = k.alloc("ones_b", [128, 128], BF16)
    k.pool(lambda e: e.memset(ones_b[:], 1.0), W=[ones_b.b[0]])

    def rstd_from_ms(st, P, src_col, dst_col):
        k.pool(lambda e: e.tensor_tensor(out=st[0:P, dst_col:dst_col + 1], in0=st[0:P, src_col:src_col + 1],
                                         in1=cneg[0:P, :], op=ALU.pow),
               R=[st.b[0], cneg.b[0]], W=[st.b[0]])

    def mm_group(outap, bk, pairs, R, last_inc=True):
        n = len(pairs)
        for i, (l, r) in enumerate(pairs):
            k.pe(lambda e, l=l, r=r, i=i: e.matmul(outap, lhsT=l, rhs=r, start=(i == 0), stop=(i == n - 1)),
                 R=R, W=[bk.b[0]], inc=(last_inc and i == n - 1))

    def attn_sample(layer, qT, gT, mixT):
        P = NS
        bkS = k.bank(hold=True)
        sview = bkS.t.rearrange("p (s g j) -> p s g j", s=NSEQ, g=8)
        NBUF = 4
        Kbs = [k.alloc("Kb", [128, 2, XATT], BF16) for _ in range(NBUF)]

        def load_k(b):
            k.dma("pool", Kbs[b % NBUF][:], kc_d[layer, b].rearrange("(mc m) f -> m mc f", mc=2), W=[Kbs[b % NBUF].b[0]])

        for b in range(NBUF):
            load_k(b)
        for b in range(NSEQ):
            Kb = Kbs[b % NBUF]
            bk = k.bank()
            pv = bk.t.bitcast(BF16)
            for hd in range(4):
                for mc in range(2):
                    c = hd * 2 + mc
                    k.pe(lambda e, c=c, hd=hd, mc=mc: e.transpose(out=pv[:, c * 128:(c + 1) * 128],
                                                                   in_=Kb[:, mc, hd * 128:(hd + 1) * 128],
                                                                   identity=idb[:, :]),
                         R=[Kb.b[0], idb.b[0]], W=[bk.b[0]], inc=(c == 7))
            if b + NBUF < NSEQ:
                load_k(b + NBUF)
            KTs = k.alloc("KTs", [128, 1024], BF16)
            k.dve(lambda e: e.tensor_copy(out=KTs[:, :], in_=pv[:, :]), R=[bk.b[0]], W=[KTs.b[0]])
            for hd in range(4):
                for mc in range(2):
                    c = hd * 2 + mc
                    k.pe(lambda e, c=c, hd=hd, b=b: e.matmul(sview[:, b, c, :], lhsT=KTs[:, c * 128:(c + 1) * 128],
                                                             rhs=qT[:, hd, b:NS:NSEQ], start=True, stop=True),
                         R=[KTs.b[0], qT.b[0]], W=[bkS.b[0]], inc=(c == 7))
            k.release(KTs)
        for t_ in Kbs:
            k.release(t_)
        PTs = k.alloc("PTs", [128, NSEQ, 8, 4], BF16)
        k.act(lambda e: e.activation(out=PTs[:].rearrange("p s g j -> p (s g j)"), in_=bkS.t[:, :], func=AF.Exp,
                                     scale=float(128 ** -0.5)),
              R=[bkS.b[0]], W=[PTs.b[0]])
        k.unhold(bkS)
        bkO = k.bank(hold=True)
        bkR = k.bank(hold=True)
        Vss = [k.alloc("Vs", [128, 2, XATT], BF16) for _ in range(NBUF)]

        def load_v(b):
            k.dma("pool", Vss[b % NBUF][:], vc_d[layer, b].rearrange("(mc m) f -> m mc f", mc=2), W=[Vss[b % NBUF].b[0]])

        for b in range(NBUF):
            load_v(b)
        for b in range(NSEQ):
            Vs = Vss[b % NBUF]
            for hd in range(4):
                oc = bkO.t[:, hd * 128 + b:hd * 128 + NS:NSEQ]
                rc = bkR.t[:, hd * 128 + b:hd * 128 + NS:NSEQ]
                mm_group(oc, bkO, [(Vs[:, mc, hd * 128:(hd + 1) * 128], PTs[:, b, hd * 2 + mc, :]) for mc in range(2)],
                         R=[Vs.b[0], PTs.b[0]], last_inc=False)
                mm_group(rc, bkR, [(ones_b[:, :], PTs[:, b, hd * 2 + mc, :]) for mc in range(2)],
                         R=[ones_b.b[0], PTs.b[0]], last_inc=(hd == 3))
            if b + NBUF < NSEQ:
                load_v(b + NBUF)
        for t_ in Vss:
            k.release(t_)
        k.release(PTs)
        finish_attn(P, bkO, bkR, gT, mixT)
        k.unhold(bkO)
        k.unhold(bkR)

    def finish_attn(P, bkO, bkR, gT, mixT):
        ov = bkO.t.rearrange("p (h t) -> p h t", h=4)
        rv = bkR.t.rearrange("p (h t) -> p h t", h=4)
        rrec = k.alloc("rrec", [128, 4, 128], F32)
        k.dve(lambda e: e.reciprocal(out=rrec[:, :, 0:P], in_=rv[:, :, 0:P]), R=[bkR.b[0]], W=[rrec.b[0]])
        k.dve(lambda e: e.tensor_tensor(out=rrec[:, :, 0:P], in0=rrec[:, :, 0:P], in1=gT[:, :, 0:P], op=ALU.mult),
              R=[rrec.b[0], gT.b[0]], W=[rrec.b[0]])
        k.dve(lambda e: e.tensor_tensor(out=mixT[:, 12:16, 0:P], in0=ov[:, :, 0:P], in1=rrec[:, :, 0:P], op=ALU.mult),
              R=[bkO.b[0], rrec.b[0]], W=[mixT.b[0]])
        k.release(rrec)

    def attn_prompt(P, KT, Vb, qT, gate, goff, mix):
        bkA = k.bank()
        bkB = k.bank()
        for hd in range(4):
            bk = bkA if hd < 2 else bkB
            for mc in range(2):
                c = (hd % 2) * 2 + mc
                k.pe(lambda e, bk=bk, c=c, hd=hd, mc=mc: e.matmul(bk.t[:, c * 128:c * 128 + P],
                                                                  lhsT=KT[:, hd, mc * 128:(mc + 1) * 128],
                                                                  rhs=qT[:, hd, 0:P], start=True, stop=True),
                     R=[KT.b[0], qT.b[0]], W=[bk.b[0]], inc=(c == 3))
        PT = k.alloc("PT", [128, 8, 128], BF16)
        for i, bk in enumerate((bkA, bkB)):
            k.act(lambda e, i=i, bk=bk: e.activation(out=PT[:, i * 4:(i + 1) * 4, :],
                                                     in_=bk.t.rearrange("p (c t) -> p c t", c=4),
                                                     func=AF.Exp, scale=float(128 ** -0.5)),
                  R=[bk.b[0]], W=[PT.b[0]])
        rr = k.alloc("rr", [128, 4], F32)
        obk = [k.bank(), k.bank()]
        for hp in range(2):
            bk = obk[hp]
            for hh in range(2):
                hd = hp * 2 + hh
                mm_group(bk.t[0:P, hh * 129:(hh + 1) * 129], bk,
                         [(PT[:, hd * 2 + mc, 0:P], Vb[:, mc, hd, 0:129]) for mc in range(2)],
                         R=[Vb.b[0], PT.b[0]], last_inc=(hh == 1))
            k.dve(lambda e, hp=hp, bk=bk: e.reciprocal(out=rr[0:P, hp * 2:hp * 2 + 2], in_=bk.t[0:P, 128:258:129]),
                  R=[bk.b[0]], W=[rr.b[0]])
            for hh in range(2):
                hd = hp * 2 + hh
                k.dve(lambda e, hh=hh, hd=hd, bk=bk: e.scalar_tensor_tensor(
                    out=mix[0:P, BR + hd * 128:BR + (hd + 1) * 128], in0=bk.t[0:P, hh * 129:hh * 129 + 128],
                    scalar=rr[0:P, hd:hd + 1], in1=gate[0:P, goff + hd * 128:goff + (hd + 1) * 128],
                    op0=ALU.mult, op1=ALU.mult), R=[bk.b[0], rr.b[0], gate.b[0]], W=[mix.b[0]])
        k.release(PT)
        k.release(rr)

    def post_and_store(P, bks, xt, gb_post, dst, dst2=None, wbuf=None):
        st = k.alloc("pst", [128, 4], F32)
        junk = k.alloc("pjunk", [128, 512], BF16)
        for i, bk in enumerate(bks):
            k.act(lambda e, i=i, bk=bk: e.activation(out=junk[0:P, :], in_=bk.t[0:P, :], func=AF.Square,
                                                     accum_out=st[0:P, i:i + 1]),
                  R=[bk.b[0]], W=[junk.b[0], st.b[0]])
        k.dve(lambda e: e.tensor_tensor(out=st[0:P, 2:3], in0=st[0:P, 0:1], in1=st[0:P, 1:2], op=ALU.add),
              R=[st.b[0]], W=[st.b[0]])
        k.dve(lambda e: e.tensor_scalar(out=st[0:P, 3:4], in0=st[0:P, 2:3], scalar1=1.0 / D, scalar2=EPS,
                                        op0=ALU.mult, op1=ALU.add), R=[st.b[0]], W=[st.b[0]])
        rstd_from_ms(st, P, 3, 0)
        x1 = k.alloc("x1", [128, D], F32)
        for i, bk in enumerate(bks):
            k.dve(lambda e, i=i, bk=bk: e.scalar_tensor_tensor(out=x1[0:P, i * 512:(i + 1) * 512], in0=bk.t[0:P, :],
                                                               scalar=st[0:P, 0:1],
                                                               in1=gb_post[0:P, i * 512:(i + 1) * 512],
                                                               op0=ALU.mult, op1=ALU.mult),
                  R=[bk.b[0], st.b[0], gb_post.b[0]], W=[x1.b[0]])
        k.pool(lambda e: e.tensor_tensor(out=x1[0:P, :], in0=x1[0:P, :], in1=xt[0:P, :], op=ALU.add),
               R=[x1.b[0], xt.b[0]], W=[x1.b[0]])
        k.dma("pool", dst, x1[0:P, :], R=[x1.b[0]], W=([wbuf] if wbuf is not None else []))
        if dst2 is not None:
            k.dma("pool", dst2, x1[0:P, :], R=[x1.b[0]])
        k.release(st)
        k.release(junk)
        k.release(x1)


    def ssm_setup(stop=99, after_tables=None):
        TWO_PI = 6.283185307179586
        MAGIC = 12582912.0
        L = k.alloc("sL", [128, 2, 48], F32)
        LD = k.alloc("sLD", [128, 48], F32)
        B2 = k.alloc("sB2", [128, 2, 48, 16], F32)
        C2 = k.alloc("sC2", [128, 2, 48, 16], F32)
        dcol = k.alloc("sdcol", [128, 96], F32)
        mJ = k.alloc("smJ", [128, 128], F32)
        for t_, d_ in ((L, lam2_d), (LD, ldt2_d), (B2, B2_d), (C2, C2_d), (dcol, dcol_d), (mJ, maskJ_d)):
            k.dma("sp", t_[:], d_, W=[t_.b[0]])
        W1 = k.alloc("sW1", [128, 12, 48], F32)
        wb = W1.b[0]

        def tt(out, a, b, op, R=()):
            k.dve(lambda e: e.tensor_tensor(out=out, in0=a, in1=b, op=op), R=[wb] + list(R), W=[wb])

        k.act(lambda e: e.activation(out=W1[:, 0, :], in_=LD[:, :], func=AF.Exp), R=[LD.b[0]], W=[wb])
        tt(W1[:, 1, :], L[:, 0, :], W1[:, 0, :], ALU.mult, R=[L.b[0]])
        tt(W1[:, 2, :], L[:, 1, :], W1[:, 0, :], ALU.mult, R=[L.b[0]])
        k.act(lambda e: e.activation(out=W1[:, 3, :], in_=W1[:, 1, :], func=AF.Exp), R=[wb], W=[wb])

        def sin_of(dst, src, shift):
            k.dve(lambda e: e.tensor_scalar(out=W1[:, 8, :], in0=src, scalar1=shift, scalar2=None, op0=ALU.add),
                  R=[wb], W=[wb])
            k.dve(lambda e: e.tensor_scalar(out=W1[:, 9, :], in0=W1[:, 8, :], scalar1=1.0 / TWO_PI, scalar2=MAGIC,
                                            op0=ALU.mult, op1=ALU.add), R=[wb], W=[wb])
            k.dve(lambda e: e.tensor_scalar(out=W1[:, 10, :], in0=W1[:, 9, :], scalar1=-MAGIC, scalar2=None,
                                            op0=ALU.add), R=[wb], W=[wb])
            k.dve(lambda e: e.scalar_tensor_tensor(out=W1[:, 11, :], in0=W1[:, 10, :], scalar=-TWO_PI,
                                                   in1=W1[:, 8, :], op0=ALU.mult, op1=ALU.add), R=[wb], W=[wb])
            k.dve(lambda e: e.tensor_scalar(out=W1[:, 11, :], in0=W1[:, 11, :], scalar1=3.1415925, scalar2=-3.1415925,
                                            op0=ALU.min, op1=ALU.max), R=[wb], W=[wb])
            k.act(lambda e: e.activation(out=dst, in_=W1[:, 11, :], func=AF.Sin), R=[wb], W=[wb])

        sin_of(W1[:, 4, :], W1[:, 2, :], 0.0)
        sin_of(W1[:, 5, :], W1[:, 2, :], 1.5707963267948966)
        SC = k.alloc("SC", [128, 2, 48, 26], F32)
        P8 = k.alloc("P8", [128, 2, 48, 32], F32)
        scb = SC.b[0]

        def stt(out, a, b, op):
            k.dve(lambda e: e.tensor_tensor(out=out, in0=a, in1=b, op=op), R=[wb, scb, P8.b[0]], W=[wb, scb, P8.b[0]])

        k.dve(lambda e: e.memset(SC[:, 0, :, 0], 1.0), W=[scb])
        k.dve(lambda e: e.memset(SC[:, 1, :, 0], 0.0), W=[scb])
        stt(SC[:, 0, :, 1], W1[:, 3, :], W1[:, 5, :], ALU.mult)
        stt(SC[:, 1, :, 1], W1[:, 3, :], W1[:, 4, :], ALU.mult)

        def cmul(dr, di, xr, xi, yr, yi):
            stt(W1[:, 8, :], xr, yr, ALU.mult)
            stt(W1[:, 9, :], xi, yi, ALU.mult)
            stt(W1[:, 10, :], xr, yi, ALU.mult)
            stt(W1[:, 11, :], xi, yr, ALU.mult)
            stt(dr, W1[:, 8, :], W1[:, 9, :], ALU.subtract)
            stt(di, W1[:, 10, :], W1[:, 11, :], ALU.add)

        def sl(n):
            return SC[:, 0, :, n], SC[:, 1, :, n]

        W2 = k.alloc("sW2", [128, 4, 48, 8], F32)

        def vcmul(Td, d0, Ts, s0, m, Tm_, mi):
            Rr = [wb, scb, P8.b[0], W2.b[0]]
            xr, xi = Ts[:, 0, :, s0:s0 + m], Ts[:, 1, :, s0:s0 + m]
            yr = Tm_[:, 0, :, mi].unsqueeze(2).broadcast_to([128, 48, m])
            yi = Tm_[:, 1, :, mi].unsqueeze(2).broadcast_to([128, 48, m])
            tv = [W2[:, q, :, 0:m] for q in range(4)]
            for q, (a_, b_) in enumerate(((xr, yr), (xi, yi), (xr, yi), (xi, yr))):
                k.dve(lambda e, q=q, a_=a_, b_=b_: e.tensor_tensor(out=tv[q], in0=a_, in1=b_, op=ALU.mult),
                      R=Rr, W=[W2.b[0]])
            k.dve(lambda e: e.tensor_tensor(out=Td[:, 0, :, d0:d0 + m], in0=tv[0], in1=tv[1], op=ALU.subtract),
                  R=Rr, W=[wb, scb, P8.b[0]])
            k.dve(lambda e: e.tensor_tensor(out=Td[:, 1, :, d0:d0 + m], in0=tv[2], in1=tv[3], op=ALU.add),
                  R=Rr, W=[wb, scb, P8.b[0]])

        vcmul(SC, 2, SC, 1, 1, SC, 1)
        vcmul(SC, 3, SC, 1, 2, SC, 2)
        vcmul(SC, 5, SC, 1, 4, SC, 4)
        vcmul(SC, 9, SC, 1, 8, SC, 8)
        cmul(*sl(19), *sl(16), *sl(16))
        cmul(*sl(20), *sl(19), *sl(19))
        cmul(*sl(21), *sl(20), *sl(20))
        cmul(*sl(22), *sl(21), *sl(21))
        cmul(*sl(23), *sl(22), *sl(22))
        cmul(*sl(24), *sl(23), *sl(23))
        stt(W1[:, 8, :], SC[:, 0, :, 8], SC[:, 0, :, 8], ALU.mult)
        stt(W1[:, 9, :], SC[:, 1, :, 8], SC[:, 1, :, 8], ALU.mult)
        stt(W1[:, 8, :], W1[:, 8, :], W1[:, 9, :], ALU.add)
        k.pool(lambda e: e.tensor_tensor(out=W1[:, 9, :], in0=W1[:, 8, :], in1=cneg[:, 0:1].broadcast_to([128, 48]),
                                         op=ALU.pow), R=[wb, cneg.b[0]], W=[wb])
        stt(SC[:, 0, :, 25], W1[:, 8, :], W1[:, 9, :], ALU.mult)
        k.dve(lambda e: e.memset(SC[:, 1, :, 25], 0.0), W=[scb])
        k.dve(lambda e: e.memset(P8[:, 0, :, 0], 1.0), W=[P8.b[0], scb])
        k.dve(lambda e: e.memset(P8[:, 1, :, 0], 0.0), W=[P8.b[0], scb])
        k.dve(lambda e: e.memset(P8[:, 0, :, 16], 1.0), W=[P8.b[0], scb])
        k.dve(lambda e: e.memset(P8[:, 1, :, 16], 0.0), W=[P8.b[0], scb])
        stt(P8[:, 0, :, 1], SC[:, 0, :, 8], W1[:, 9, :], ALU.mult)
        stt(W1[:, 10, :], SC[:, 1, :, 8], W1[:, 9, :], ALU.mult)
        k.dve(lambda e: e.tensor_scalar(out=P8[:, 1, :, 1], in0=W1[:, 10, :], scalar1=-1.0, scalar2=None, op0=ALU.mult),
              R=[wb], W=[P8.b[0], scb])
        vcmul(P8, 2, P8, 1, 1, P8, 1)
        vcmul(P8, 3, P8, 1, 2, P8, 2)
        vcmul(P8, 5, P8, 1, 4, P8, 4)
        vcmul(P8, 9, P8, 1, 7, P8, 8)
        vcmul(P8, 17, P8, 8, 1, P8, 8)
        vcmul(P8, 18, P8, 17, 1, P8, 17)
        vcmul(P8, 19, P8, 17, 2, P8, 18)
        vcmul(P8, 21, P8, 17, 4, P8, 20)
        vcmul(P8, 25, P8, 17, 7, P8, 24)
        k.release(W2)

        def cinv(dn, sn):
            xr, xi = sl(sn)
            stt(W1[:, 8, :], xr, xr, ALU.mult)
            stt(W1[:, 9, :], xi, xi, ALU.mult)
            stt(W1[:, 8, :], W1[:, 8, :], W1[:, 9, :], ALU.add)
            k.dve(lambda e: e.reciprocal(out=W1[:, 9, :], in_=W1[:, 8, :]), R=[wb], W=[wb])
            stt(SC[:, 0, :, dn], xr, W1[:, 9, :], ALU.mult)
            stt(W1[:, 10, :], xi, W1[:, 9, :], ALU.mult)
            k.dve(lambda e: e.tensor_scalar(out=SC[:, 1, :, dn], in0=W1[:, 10, :], scalar1=-1.0, scalar2=None,
                                            op0=ALU.mult), R=[wb], W=[scb])

        cinv(17, 4)
        cinv(18, 7)
        PR = k.alloc("sPR", [128, 2, 48, 8], F32)
        for jp in range(8):
            for ri in range(2):
                k.dve(lambda e, jp=jp, ri=ri: e.tensor_copy(out=PR[:, ri, :, jp], in_=SC[:, ri, :, 7 - jp]),
                      R=[scb], W=[PR.b[0]])
        k.dve(lambda e: e.tensor_scalar(out=W1[:, 0, :], in0=SC[:, 0, :, 1], scalar1=-1.0, scalar2=None, op0=ALU.add),
              R=[scb], W=[wb])
        stt(W1[:, 8, :], L[:, 0, :], L[:, 0, :], ALU.mult)
        stt(W1[:, 9, :], L[:, 1, :], L[:, 1, :], ALU.mult)
        stt(W1[:, 8, :], W1[:, 8, :], W1[:, 9, :], ALU.add)
        k.dve(lambda e: e.reciprocal(out=W1[:, 1, :], in_=W1[:, 8, :]), R=[wb], W=[wb])
        stt(W1[:, 8, :], W1[:, 0, :], L[:, 0, :], ALU.mult)
        stt(W1[:, 9, :], SC[:, 1, :, 1], L[:, 1, :], ALU.mult)
        stt(W1[:, 8, :], W1[:, 8, :], W1[:, 9, :], ALU.add)
        stt(W1[:, 6, :], W1[:, 8, :], W1[:, 1, :], ALU.mult)
        stt(W1[:, 8, :], SC[:, 1, :, 1], L[:, 0, :], ALU.mult)
        stt(W1[:, 9, :], W1[:, 0, :], L[:, 1, :], ALU.mult)
        stt(W1[:, 8, :], W1[:, 8, :], W1[:, 9, :], ALU.subtract)
        stt(W1[:, 7, :], W1[:, 8, :], W1[:, 1, :], ALU.mult)
        if stop <= 1:
            return SC, P8

        Bb = k.alloc("sBb", [128, 2, 48, 16], F32)
        Cm = k.alloc("sCm", [128, 2, 48, 16], F32)
        T2 = k.alloc("sT2", [128, 4, 48, 16], F32)

        def bc16(ap):
            return ap.unsqueeze(2).broadcast_to([128, 48, 16])

        def cmul16(dst, X, yr, yi, neg_im=False):
            Rr = [wb, scb, X.b[0], T2.b[0], dst.b[0]]
            ops = ((0, X[:, 0], yr), (1, X[:, 1], yi), (2, X[:, 0], yi), (3, X[:, 1], yr))
            for i_, a_, b_ in ops:
                k.dve(lambda e, i_=i_, a_=a_, b_=b_: e.tensor_tensor(out=T2[:, i_], in0=a_, in1=bc16(b_), op=ALU.mult),
                      R=Rr, W=[T2.b[0]])
            k.dve(lambda e: e.tensor_tensor(out=dst[:, 0], in0=T2[:, 0], in1=T2[:, 1], op=ALU.subtract),
                  R=Rr, W=[dst.b[0]])
            k.dve(lambda e: e.tensor_tensor(out=dst[:, 1], in0=T2[:, 2], in1=T2[:, 3], op=ALU.add),
                  R=Rr, W=[dst.b[0]])

        cmul16(Bb, B2, W1[:, 6, :], W1[:, 7, :])
        cmul16(Cm, C2, SC[:, 0, :, 18], SC[:, 1, :, 18])
        k.release(T2)
        if stop <= 2:
            return SC, P8

        if after_tables is not None:
            after_tables()
        SPB = 4
        SGB = 2 * SPB
        NSB = 48 // SPB

        class SB_:
            pass

        def stage_a(bl):
            Bk = SB_()
            Bk.bl = bl
            ps_ = slice(bl * SPB, (bl + 1) * SPB)
            X = k.alloc("sX", [128, 6, SPB, 8, 16], F32, nb=3)
            T4 = k.alloc("sT4", [128, 4, SPB, 8, 16], F32)
            T4p = k.alloc("sT4p", [128, 4, SPB, 8, 16], F32)
            xb = X.b[0]
            xbs = [X.b[0], X.b[0], X.b[2], X.b[2]]
            shp = [128, SPB, 8, 16]

            def expand(dr, di, M, pw_lo, reverse, neg_im, eng=None, TT_=None, xb=None):
                eng = eng or k.dve
                TT_ = TT_ or T4
                xb = xb or X.b[0]
                if reverse:
                    pr = PR[:, 0, ps_, :]
                    pi = PR[:, 1, ps_, :]
                else:
                    pr = SC[:, 0, ps_, pw_lo:pw_lo + 8]
                    pi = SC[:, 1, ps_, pw_lo:pw_lo + 8]
                pr = pr.unsqueeze(3).broadcast_to(shp)
                pi = pi.unsqueeze(3).broadcast_to(shp)
                mr = M[:, 0, ps_, :].unsqueeze(2).broadcast_to(shp)
                mi = M[:, 1, ps_, :].unsqueeze(2).broadcast_to(shp)
                Rr = [scb, M.b[0], TT_.b[0], xb, PR.b[0]]
                for i_, a_, b_ in ((0, mr, pr), (1, mi, pi), (2, mr, pi), (3, mi, pr)):
                    eng(lambda e, i_=i_, a_=a_, b_=b_: e.tensor_tensor(out=TT_[:, i_], in0=a_, in1=b_, op=ALU.mult),
                        R=Rr, W=[TT_.b[0]])
                eng(lambda e: e.tensor_tensor(out=X[:, dr], in0=TT_[:, 0], in1=TT_[:, 1], op=ALU.subtract),
                    R=Rr, W=[xb])
                if neg_im:
                    k.dve(lambda e: e.scalar_tensor_tensor(out=X[:, di].rearrange("p a b c -> p (a b c)"),
                                                           in0=TT_[:, 2].rearrange("p a b c -> p (a b c)"), scalar=-1.0,
                                                           in1=TT_[:, 3].rearrange("p a b c -> p (a b c)"),
                                                           op0=ALU.mult, op1=ALU.subtract), R=Rr, W=[xb])
                else:
                    eng(lambda e: e.tensor_tensor(out=X[:, di], in0=TT_[:, 2], in1=TT_[:, 3], op=ALU.add),
                        R=Rr, W=[xb])

            expand(4, 5, C2, 1, False, False, eng=k.pool, TT_=T4p, xb=X.b[1])
            expand(0, 1, Bb, 0, True, False)
            expand(2, 3, Cm, 0, False, False, xb=X.b[2])
            k.release(T4p)
            cxb = k.alloc("scxb", [128, SPB, 2, 128], BF16)
            for ri in range(2):
                k.act(lambda e, ri=ri: e.activation(out=cxb[:, :, ri, :],
                                                    in_=X[:, 4 + ri].rearrange("p a b c -> p a (b c)"),
                                                    func=AF.Copy, scale=(1.0 if ri == 0 else -1.0)),
                      R=[X.b[1]], W=[cxb.b[0]])
            k.dma("pool", CX_d[bl * SPB:(bl + 1) * SPB].rearrange("q r x c -> r q x c"), cxb[:], R=[cxb.b[0]], W=[scrbuf])
            k.release(cxb)
            Xf = [X[:, i_].rearrange("p a b c -> p a (b c)") for i_ in range(6)]
            XH = k.alloc("sXH", [128, 4, SPB, 128], BF16)
            XL = k.alloc("sXL", [128, 4, SPB, 128], BF16)
            for i_ in range(4):
                sc_ = -1.0 if i_ == 3 else 1.0
                k.act(lambda e, i_=i_, sc_=sc_: e.activation(out=XH[:, i_], in_=Xf[i_], func=AF.Copy, scale=sc_),
                      R=[xbs[i_]], W=[XH.b[0]])
                k.dve(lambda e, i_=i_: e.tensor_tensor(out=T4[:, i_].rearrange("p a b c -> p a (b c)"), in0=Xf[i_],
                                                       in1=XH[:, i_], op=(ALU.add if i_ == 3 else ALU.subtract)),
                      R=[xbs[i_], XH.b[0]], W=[T4.b[0]])
                k.act(lambda e, i_=i_, sc_=sc_: e.activation(out=XL[:, i_], in_=T4[:, i_].rearrange("p a b c -> p a (b c)"),
                                                             func=AF.Copy, scale=sc_),
                      R=[T4.b[0]], W=[XL.b[0]])
            k.release(X)
            k.release(T4)
            Bk.XH, Bk.XL = XH, XL
            return Bk

        def stage_b(Bk):
            bl, XH, XL = Bk.bl, Bk.XH, Bk.XL
            ttb = k.alloc("sttb", [128, SGB, 128], BF16)
            bxb = k.alloc("sbxb", [128, SGB, 2, 64], BF16)
            tmps = []
            for pp in range(SPB):
                for g2 in range(2):
                    gl = pp * 2 + g2
                    g = bl * SGB + gl
                    rows = slice(64 * g2, 64 * g2 + 64)
                    bk = k.bank()
                    combos = ((XH, 0, XH, 2), (XH, 0, XL, 2), (XL, 0, XH, 2), (XH, 1, XH, 3), (XH, 1, XL, 3), (XL, 1, XH, 3))
                    mm_group(bk.t[:, 0:128], bk, [(A_[rows, ia, pp, :], B_[rows, ib, pp, :]) for (A_, ia, B_, ib) in combos],
                             R=[XH.b[0], XL.b[0]])
                    tmp = k.alloc("sttmp", [128, 128], F32)
                    tmps.append(tmp)
                    k.dve(lambda e, bk=bk, tmp=tmp: e.tensor_tensor(out=tmp[:, :], in0=bk.t[:, 0:128], in1=mJ[:, :], op=ALU.mult),
                          R=[bk.b[0], mJ.b[0]], W=[tmp.b[0]])
                    k.dve(lambda e, g=g, gl=gl, tmp=tmp: e.scalar_tensor_tensor(out=ttb[:, gl, :], in0=idf[:, :],
                                                                                scalar=dcol[:, g:g + 1], in1=tmp[:, :],
                                                                                op0=ALU.mult, op1=ALU.add),
                          R=[tmp.b[0], idf.b[0], dcol.b[0]], W=[ttb.b[0]])
                    bkt = k.bank()
                    pvb = bkt.t.bitcast(BF16)
                    for ri in range(2):
                        k.pe(lambda e, pvb=pvb, pp=pp, rows=rows, ri=ri: e.transpose(
                            out=pvb[:, ri * 64:(ri + 1) * 64], in_=XH[rows, ri, pp, :],
                            identity=idb[rows, rows]), R=[XH.b[0], idb.b[0]], W=[bkt.b[0]], inc=(ri == 1))
                    k.act(lambda e, pvb=pvb, bkt=bkt, gl=gl: e.copy(out=bxb[:, gl, :, :].rearrange("p x q -> p (x q)"),
                                                                    in_=pvb[:, 0:128]), R=[bkt.b[0]], W=[bxb.b[0]])
                    if len(tmps) > 2:
                        k.release(tmps.pop(0))
            for t_ in tmps:
                k.release(t_)
            k.dma("pool", TT_d[bl * SGB:(bl + 1) * SGB].rearrange("g r c -> r g c"), ttb[:], R=[ttb.b[0]], W=[scrbuf])
            k.dma("pool", BX_d[bl * SGB:(bl + 1) * SGB].rearrange("g r x p -> r g x p"), bxb[:], R=[bxb.b[0]], W=[scrbuf])
            k.release(ttb)
            k.release(bxb)
            k.release(XH)
            k.release(XL)

        if stop > 3:
            cur = stage_a(0)
            for bl in range(NSB):
                nxt = stage_a(bl + 1) if bl + 1 < NSB else None
                stage_b(cur)
                cur = nxt
        k.dma("pool", SC_d, SC[:], R=[scb], W=[scrbuf])
        k.dma("pool", P8_d, P8[:], R=[P8.b[0]], W=[scrbuf])
        for t_ in (L, LD, B2, C2, dcol, mJ, W1, Bb, Cm, PR, SC, P8):
            k.release(t_)
        return None, None

    def layer0(Wa_pre=None, kv_pre=None):
        if kv_pre:
            KT, Vb = kv_pre
        else:
            gb_mem0 = load_gb(4)
            KT, Vb = mem_kv(0, gb_mem0)
            k.release(gb_mem0)
        Wa = Wa_pre if Wa_pre is not None else load_w("Wa", w_in_a, D, INA)
        Wo = load_w("Wo", w_out[0], MIXW, D)
        gb_pre = load_gb(0)
        gb_post = load_gb(2)
        lng = k.alloc("lng", [128, BR], F32)
        lnb = k.alloc("lnb", [128, BR], F32)
        k.dma("sp", lng[:], lnv[0:1, :].broadcast_to([128, BR]), W=[lng.b[0]])
        k.dma("sp", lnb[:], lnv[1:2, :].broadcast_to([128, BR]), W=[lnb.b[0]])
        wtmp = k.alloc("wtmp", [128, 8, 128], F32)
        msk = k.alloc("msk", [128, 128], F32)
        k.dma("sp", wtmp[:], wsT_d, W=[wtmp.b[0]])
        k.dma("sp", msk[:], trilT_d, W=[msk.b[0]])
        wsT = k.alloc("wsT", [128, 8, 128], BF16)
        for g in range(8):
            k.dve(lambda e, g=g: e.tensor_tensor(out=wsT[:, g, :], in0=wtmp[:, g, :], in1=msk[:, :], op=ALU.mult),
                  R=[wtmp.b[0], msk.b[0]], W=[wsT.b[0]])
        k.release(wtmp)
        k.release(msk)
        wtmp2 = k.alloc("wtmp2", [64, 8, 64], F32)
        msk2 = k.alloc("msk2", [64, 64], F32)
        k.dma("sp", wtmp2[:], wsamp_d, W=[wtmp2.b[0]])
        k.dma("sp", msk2[:], mask_s_d, W=[msk2.b[0]])
        wsS = k.alloc("wsS", [64, 8, 64], BF16)
        for g in range(8):
            k.dve(lambda e, g=g: e.tensor_tensor(out=wsS[:, g, :], in0=wtmp2[:, g, :], in1=msk2[:, :], op=ALU.mult),
                  R=[wtmp2.b[0], msk2.b[0]], W=[wsS.b[0]])
        k.release(wtmp2)
        k.release(msk2)
        bsT = k.alloc("bsT", [128, 8], F32)
        bsS = k.alloc("bsS", [64, 8], F32)
        k.dma("sp", bsT[:], bsT_d, W=[bsT.b[0]])
        k.dma("sp", bsS[:], bs_s_d, W=[bsS.b[0]])

        fsets = FrontSets(2)

        def front(xsrc, P):
            return fsets.front(xsrc, P, gb_pre)

        def tile(fr, P, sample, dst, next_front):
            xt, hT = fr
            RW = [hT.b[0], Wa.b[0]]
            u = k.alloc("u", [128, BR], F32)
            v = k.alloc("v", [128, BR], F32)
            gate = k.alloc("gate", [128, MIXW if not sample else BR], F32)
            qT = k.alloc("qT", [128, 4, 128], BF16)
            gT = k.alloc("gT", [128, 4, 128], F32) if sample else None

            def tm_blocks(dstt, c0, fn, nblk=3):
                for cb in range(nblk):
                    bk = k.bank()
                    mm_group(bk.t[0:P, :], bk, [(hT[:, kc, 0:P], Wa[:, kc, c0 + cb * 512:c0 + (cb + 1) * 512])
                                                for kc in range(8)], R=RW)
                    k.act(lambda e, bk=bk, cb=cb: e.activation(out=dstt[0:P, cb * 512:(cb + 1) * 512],
                                                               in_=bk.t[0:P, :], func=fn),
                          R=[bk.b[0]], W=[dstt.b[0]])

            def fm_block(dstt, c0, fn):
                bk = k.bank()
                for hd in range(4):
                    mm_group(bk.t[:, hd * 128:hd * 128 + P], bk,
                             [(Wa[:, kc, c0 + hd * 128:c0 + (hd + 1) * 128], hT[:, kc, 0:P]) for kc in range(8)],
                             R=RW, last_inc=(hd == 3))
                k.act(lambda e, bk=bk: e.activation(out=dstt[:, :, 0:P],
                                                    in_=bk.t.rearrange("p (h t) -> p h t", h=4)[:, :, 0:P], func=fn),
                      R=[bk.b[0]], W=[dstt.b[0]])

            tm_blocks(v, BR, AF.Gelu_apprx_tanh)
            st = k.alloc("lnst", [128, 32], F32)
            for cb in range(3):
                k.dve(lambda e, cb=cb: e.bn_stats(out=st[0:P, cb * 6:(cb + 1) * 6], in_=v[0:P, cb * 512:(cb + 1) * 512]),
                      R=[v.b[0]], W=[st.b[0]])
            k.dve(lambda e: e.bn_aggr(out=st[0:P, 18:20], in_=st[0:P, 0:18]), R=[st.b[0]], W=[st.b[0]])
            k.dve(lambda e: e.tensor_scalar(out=st[0:P, 20:21], in0=st[0:P, 19:20], scalar1=EPS, scalar2=None,
                                            op0=ALU.add), R=[st.b[0]], W=[st.b[0]])
            rstd_from_ms(st, P, 20, 21)
            k.dve(lambda e: e.scalar_tensor_tensor(out=st[0:P, 22:23], in0=st[0:P, 18:19], scalar=-1.0,
                                                   in1=st[0:P, 21:22], op0=ALU.mult, op1=ALU.mult),
                  R=[st.b[0]], W=[st.b[0]])
            tm_blocks(u, 0, AF.Gelu_apprx_tanh)
            fm_block(qT, 2 * BR, AF.Copy)
            k.act(lambda e: e.activation(out=v[0:P, :], in_=v[0:P, :], func=AF.Identity, scale=st[0:P, 21:22],
                                         bias=st[0:P, 22:23]), R=[v.b[0], st.b[0]], W=[v.b[0]])
            k.dve(lambda e: e.tensor_tensor(out=v[0:P, :], in0=v[0:P, :], in1=lng[0:P, :], op=ALU.mult),
                  R=[v.b[0], lng.b[0]], W=[v.b[0]])
            vnb = k.alloc("vnb", [128, BR], BF16)
            if sample:
                k.dve(lambda e: e.tensor_tensor(out=v[0:P, :], in0=v[0:P, :], in1=lnb[0:P, :], op=ALU.add),
                      R=[v.b[0], lnb.b[0]], W=[v.b[0]])
                k.dma("pool", o_vs, v[0:P, :], R=[v.b[0]])
                k.dve(lambda e: e.tensor_copy(out=vnb[0:P, :], in_=v[0:P, :]), R=[v.b[0]], W=[vnb.b[0]])
            else:
                k.dve(lambda e: e.tensor_tensor(out=vnb[0:P, :], in0=v[0:P, :], in1=lnb[0:P, :], op=ALU.add),
                      R=[v.b[0], lnb.b[0]], W=[vnb.b[0]])
            k.release(st)
            k.release(v)
            mixT = k.alloc("mixT", [128, 16, 128], BF16)
            mix = k.alloc("mix", [128, BR if sample else MIXW], BF16)
            if sample:
                tm_blocks(gate, 2 * BR + XATT, AF.Silu)
                fm_block(gT, 2 * BR + XATT + BR, AF.Silu)
                k.release(hT)
                attn_sample(0, qT, gT, mixT)
                k.release(gT)
            else:
                tm_blocks(gate, 2 * BR + XATT, AF.Silu, nblk=4)
                k.release(hT)
                attn_prompt(P, KT, Vb, qT, gate, BR, mix)
            k.release(qT)
            k.dve(lambda e: e.tensor_tensor(out=u[0:P, :], in0=u[0:P, :], in1=gate[0:P, 0:BR], op=ALU.mult),
                  R=[u.b[0], gate.b[0]], W=[u.b[0]])
            k.release(gate)
            ws = wsS if sample else wsT
            bs = bsS if sample else bsT
            for gp in range(4):
                bk = k.bank()
                for gg in range(2):
                    g = gp * 2 + gg
                    k.pe(lambda e, bk=bk, g=g, gg=gg: e.matmul(bk.t[0:P, gg * 192:(gg + 1) * 192], lhsT=ws[0:P, g, 0:P],
                                                               rhs=vnb[0:P, g * 192:(g + 1) * 192], start=True, stop=True),
                         R=[ws.b[0], vnb.b[0]], W=[bk.b[0]], inc=(gg == 1))
                for gg in range(2):
                    g = gp * 2 + gg
                    k.dve(lambda e, bk=bk, g=g, gg=gg: e.scalar_tensor_tensor(out=mix[0:P, g * 192:(g + 1) * 192],
                                                                              in0=bk.t[0:P, gg * 192:(gg + 1) * 192],
                                                                              scalar=bs[0:P, g:g + 1],
                                                                              in1=u[0:P, g * 192:(g + 1) * 192],
                                                                              op0=ALU.add, op1=ALU.mult),
                          R=[bk.b[0], bs.b[0], u.b[0]], W=[mix.b[0]])
            k.release(vnb)
            k.release(u)
            nf = next_front() if next_front is not None else None
            transpose_to(mixT, mixT.b[0], mix, mix.b[0], P, 12 if sample else 16)
            k.release(mix)
            bks = [k.bank(), k.bank()]
            for cb in range(2):
                mm_group(bks[cb].t[0:P, :], bks[cb], [(mixT[:, kc, 0:P], Wo[:, kc, cb * 512:(cb + 1) * 512])
                                                     for kc in range(16)], R=[mixT.b[0], Wo.b[0]])
            k.release(mixT)
            post_and_store(P, bks, xt, gb_post, dst[0], dst[1], wbuf=x1buf)
            k.release(xt)
            return nf

        srcs = [(xp[t * 128:(t + 1) * 128, :], 128, False, (x1d[t * 128:(t + 1) * 128, :], None)) for t in range(NT)]
        srcs.append((xs[:, :], NS, True, (x1d[SEQ:SEQ + NS, :], None)))
        fr = front(srcs[0][0], srcs[0][1])
        for i, (src, P, smp, dst) in enumerate(srcs):
            nxt = (lambda i=i: front(srcs[i + 1][0], srcs[i + 1][1])) if i + 1 < len(srcs) else None
            fr = tile(fr, P, smp, dst, nxt)
        fsets.free()
        for t_ in (KT, Vb, Wa, Wo, gb_pre, gb_post, lng, lnb, wsT, wsS, bsT, bsS):
            k.release(t_)


    def layer1(SC, P8):
        SC = k.alloc("SC", [128, 2, 48, 26], F32)
        P8 = k.alloc("P8", [128, 2, 48, 32], F32)
        k.dma("sp", SC[:], SC_d, R=[scrbuf], W=[SC.b[0]])
        k.dma("sp", P8[:], P8_d, R=[scrbuf], W=[P8.b[0]])
        gb_mem1 = load_gb(5)
        KT, Vb = mem_kv(1, gb_mem1)
        k.release(gb_mem1)
        gb_pre = load_gb(1)
        gb_post = load_gb(3)
        UG = k.alloc("UG", [128, 2, 96, 128], BF16, nb=2)
        USG = k.alloc("USG", [NSEQ, 96, 64], BF16)
        HL = k.alloc("HL", [128, 2, 48], F32)
        scb = SC.b[0]

        def x1_rows(kt, j):
            return x1d[kt * 1024 + j:(kt + 1) * 1024:8, :]

        fsets = FrontSets(2)

        def load_norm_T(src, P):
            return fsets.front(src, P, gb_pre, xdeps=[x1buf])

        Wu = load_w("Wu", w_in_b, D, INB, c0=0, c1=BR)

        def a0_tile(fr, P, out_fn, in_fn, dbuf, next_front):
            xt, hT = fr
            k.release(xt)
            nf = next_front() if next_front is not None else None
            for cb in range(3):
                bk = k.bank()
                mm_group(bk.t[0:P, :], bk, [(hT[:, kc, 0:P], Wu[:, kc, cb * 512:(cb + 1) * 512]) for kc in range(8)],
                         R=[hT.b[0], Wu.b[0]])
                k.act(lambda e, bk=bk, cb=cb: e.copy(out=out_fn(cb), in_=in_fn(bk)), R=[bk.b[0]], W=[dbuf])
            k.release(hT)
            return nf

        UG5 = UG.t.rearrange("p t g (j c) -> p t g j c", c=16)
        us = k.alloc("us", [NS, BR], BF16)
        a0src = []
        for kt in range(2):
            for j in range(8):
                a0src.append((x1_rows(kt, j), 128,
                              (lambda cb, kt=kt, j=j: UG5[:, kt, cb * 32:(cb + 1) * 32, j, :]),
                              (lambda bk: bk.t[:, :].rearrange("p (g c) -> p g c", c=16)), UG.b[kt]))
        a0src.append((x1d[SEQ:SEQ + NS, :], NS, (lambda cb: us[0:NS, cb * 512:(cb + 1) * 512]),
                      (lambda bk: bk.t[0:NS, :]), us.b[0]))
        fr = load_norm_T(a0src[0][0], a0src[0][1])
        for i, (src, P, ofn, ifn, dbuf) in enumerate(a0src):
            nxt = (lambda i=i: load_norm_T(a0src[i + 1][0], a0src[i + 1][1])) if i + 1 < len(a0src) else None
            fr = a0_tile(fr, P, ofn, ifn, dbuf, nxt)
        uyS = k.alloc("uyS", [NSEQ, 4, BR], BF16)
        for j in range(4):
            k.dma("sp", uyS[0:NSEQ, j, :], us[16 * j:16 * j + 16, :], R=[us.b[0]], W=[uyS.b[0]])
        k.release(us)
        USG4 = USG.t.rearrange("p g (j c) -> p g j c", c=16)
        for j in range(4):
            k.pool(lambda e, j=j: e.tensor_copy(out=USG4[:, :, j, :], in_=uyS[0:NSEQ, j, :].rearrange("p (g c) -> p g c", c=16)),
                   R=[uyS.b[0]], W=[USG.b[0]])
        k.release(uyS)
        k.release(Wu)

        NPc = 256
        NPB = 4
        NGB = 2 * NPB
        NBLK = 48 // NPB

        def bc(ap, shape, axis):
            return ap.unsqueeze(axis).broadcast_to(shape)

        def gen_E(bl):
            gps_ = slice(bl * NPB, (bl + 1) * NPB)
            Eb_ = k.alloc("Eb", [128, 2, NPB, 256], F32)
            TP = k.alloc("TP", [128, 2, NPB, 256], F32)
            Rp = [Eb_.b[0], TP.b[0], P8.b[0]]
            E4 = [Eb_[:, ri].rearrange("p a (c i) -> p a c i", i=16) for ri in range(2)]
            Tv = [TP[:, q].rearrange("p a (c i) -> p a c i", i=16) for q in range(2)]
            Gr = bc(P8[:, 0, gps_, 16:32], [128, NPB, 16, 16], 3)
            Gi = bc(P8[:, 1, gps_, 16:32], [128, NPB, 16, 16], 3)
            Fr = bc(P8[:, 0, gps_, 0:16], [128, NPB, 16, 16], 2)
            Fi = bc(P8[:, 1, gps_, 0:16], [128, NPB, 16, 16], 2)

            def pv(out, a_, b_, op, W):
                k.pool(lambda e: e.tensor_tensor(out=out, in0=a_, in1=b_, op=op), R=Rp, W=W)

            pv(Tv[0], Gr, Fr, ALU.mult, [TP.b[0]])
            pv(Tv[1], Gi, Fi, ALU.mult, [TP.b[0]])
            pv(E4[0], Tv[0], Tv[1], ALU.subtract, [Eb_.b[0]])
            pv(Tv[0], Gr, Fi, ALU.mult, [TP.b[0]])
            pv(Tv[1], Gi, Fr, ALU.mult, [TP.b[0]])
            pv(E4[1], Tv[0], Tv[1], ALU.add, [Eb_.b[0]])
            k.release(TP)
            return Eb_

        class Blk:
            pass

        def s1(bl):
            B = Blk()
            B.bl = bl
            B.TTb = k.alloc("TTb", [128, NGB, 128], BF16)
            B.BXb = k.alloc("BXb", [128, NGB, 2, 64], BF16)
            B.CXb = k.alloc("CXb", [128, NPB, 2, 128], BF16)
            k.dma("sp", B.TTb[:], TT_d[bl * NGB:(bl + 1) * NGB].rearrange("g r c -> r g c"), R=[scrbuf], W=[B.TTb.b[0]])
            k.dma("sp", B.BXb[:], BX_d[bl * NGB:(bl + 1) * NGB].rearrange("g r x p -> r g x p"), R=[scrbuf], W=[B.BXb.b[0]])
            k.dma("sp", B.CXb[:], CX_d[bl * NPB:(bl + 1) * NPB].rearrange("q r x c -> r q x c"), R=[scrbuf], W=[B.CXb.b[0]])
            stS = k.alloc("stS", [NSEQ, 2, NPB * 128], F32)
            k.dma("sp", stS[:], st_d[:, :, bl * NPB * 128:(bl + 1) * NPB * 128], W=[stS.b[0]])
            B.Ut = k.alloc("Ut", [128, NGB, 272], BF16)
            Ut = B.Ut
            k.pool(lambda e: e.memset(Ut[64:128, :, 256:272], 0.0), W=[Ut.b[0]])
            for g4 in range(NGB // 4):
                bk = k.bank()
                pv = bk.t.bitcast(BF16)
                for gg in range(4):
                    g = bl * NGB + g4 * 4 + gg
                    for kt in range(2):
                        c = gg * 2 + kt
                        k.pe(lambda e, c=c, g=g, kt=kt, pv=pv: e.transpose(out=pv[:, c * 128:(c + 1) * 128],
                                                                           in_=UG[:, kt, g, :], identity=idb[:, :]),
                             R=[UG.b[kt], idb.b[0]], W=[bk.b[0]], inc=(c == 7))
                k.act(lambda e, g4=g4, pv=pv: e.copy(out=Ut[:, g4 * 4:(g4 + 1) * 4, 0:NPc],
                                                     in_=pv.rearrange("p (g t) -> p g t", g=4)),
                      R=[bk.b[0]], W=[Ut.b[0]])
            bk = k.bank()
            pv = bk.t.bitcast(BF16)
            for gl in range(NGB):
                g = bl * NGB + gl
                k.pe(lambda e, gl=gl, g=g, pv=pv: e.transpose(out=pv[0:64, gl * 16:(gl + 1) * 16],
                                                              in_=USG[0:NSEQ, g, :], identity=idb[0:NSEQ, 0:NSEQ]),
                     R=[USG.b[0], idb.b[0]], W=[bk.b[0]], inc=(gl == NGB - 1))
            k.dve(lambda e, pv=pv: e.tensor_copy(out=Ut[0:64, :, NPc:NPc + 16],
                                                 in_=pv[0:64, 0:NGB * 16].rearrange("p (g t) -> p g t", g=NGB)),
                  R=[bk.b[0]], W=[Ut.b[0]])
            bk = k.bank()
            for ri in range(2):
                for pp in range(NPB):
                    c = ri * NPB + pp
                    k.pe(lambda e, c=c, ri=ri, pp=pp, bk=bk: e.transpose(out=bk.t[:, c * 16:(c + 1) * 16],
                                                                         in_=stS[0:NSEQ, ri, pp * 128:(pp + 1) * 128],
                                                                         identity=idf[0:NSEQ, 0:NSEQ]),
                         R=[stS.b[0], idf.b[0]], W=[bk.b[0]], inc=(c == 2 * NPB - 1))
            B.H0 = k.alloc("H0", [128, 2, NPB, 16], F32)
            H0 = B.H0
            k.dve(lambda e, bk=bk: e.tensor_copy(out=H0[:].rearrange("p a b c -> p (a b c)"), in_=bk.t[:, 0:2 * NPB * 16]),
                  R=[bk.b[0]], W=[H0.b[0]])
            k.release(stS)
            B.S = k.alloc("S", [128, 2, NPB, 272], F32)
            S = B.S
            for pp in range(NPB):
                bks = [k.bank(), k.bank()]
                for ri in range(2):
                    for g2 in range(2):
                        gl = pp * 2 + g2
                        k.pe(lambda e, ri=ri, g2=g2, gl=gl, bks=bks: e.matmul(bks[ri].t[64 * g2:64 * g2 + 64, 0:272],
                                                                              lhsT=B.BXb[:, gl, ri, :], rhs=Ut[:, gl, :],
                                                                              start=True, stop=True),
                             R=[B.BXb.b[0], Ut.b[0]], W=[bks[ri].b[0]], inc=(g2 == 1))
                    k.act(lambda e, ri=ri, pp=pp, bks=bks: e.copy(out=S[:, ri, pp, :], in_=bks[ri].t[:, 0:272]),
                          R=[bks[ri].b[0]], W=[S.b[0]])
            return B

        def s2(B, Eb):
            S = B.S
            sb_ = S.b[0]
            gps = slice(B.bl * NPB, (B.bl + 1) * NPB)
            Tq = k.alloc("Tq", [128, 3, NPB, 256], F32)
            eb, tq = Eb.b[0], Tq.b[0]
            Rr = [sb_, eb, tq, scb, P8.b[0]]

            def dv(out, a_, b_, op, W):
                k.dve(lambda e: e.tensor_tensor(out=out, in0=a_, in1=b_, op=op), R=Rr, W=W)

            Sr = S[:, 0, :, 0:NPc]
            Si = S[:, 1, :, 0:NPc]
            Er, Ei = Eb[:, 0], Eb[:, 1]
            t0, t1, t2 = Tq[:, 0], Tq[:, 1], Tq[:, 2]
            dv(t0, Sr, Er, ALU.mult, [tq])
            dv(t1, Si, Ei, ALU.mult, [tq])
            dv(t2, Sr, Ei, ALU.mult, [tq])
            dv(Sr, t0, t1, ALU.subtract, [sb_])
            dv(t0, Si, Er, ALU.mult, [tq])
            dv(Si, t2, t0, ALU.add, [sb_])
            k.dve(lambda e: e.tensor_copy(out=t1, in_=bc(SC[:, 0, gps, 25], [128, NPB, 256], 2)), R=Rr, W=[tq])
            for ri in range(2):
                for q in range(NPB):
                    k.dve(lambda e, ri=ri, q=q: e.tensor_tensor_scan(out=Tq[:, 2 * ri, q, :], data0=t1[:, q, :],
                                                                      data1=S[:, ri, q, 0:NPc],
                                                                      initial=0.0, op0=ALU.mult, op1=ALU.add),
                          R=Rr, W=[tq])
            Wr, Wi = Tq[:, 0], Tq[:, 2]
            dv(t1, Wr, Er, ALU.mult, [tq])
            dv(Sr, Wi, Ei, ALU.mult, [sb_])
            dv(Sr, Sr, t1, ALU.add, [sb_])
            dv(t1, Wi, Er, ALU.mult, [tq])
            dv(Si, Wr, Ei, ALU.mult, [sb_])
            dv(Si, t1, Si, ALU.subtract, [sb_])
            k.release(Eb)
            k.release(Tq)

        def s3prep(B):
            S, H0 = B.S, B.H0
            sb_ = S.b[0]
            bl = B.bl
            ps_ = slice(bl * NPB, (bl + 1) * NPB)
            B.Hin = k.alloc("Hin", [128, 2, NPB, 272], BF16)
            Hin = B.Hin
            for ri in range(2):
                k.pool(lambda e, ri=ri: e.memset(Hin[:, ri, :, 0:1], 0.0), W=[Hin.b[0]])
                k.act(lambda e, ri=ri: e.copy(out=Hin[:, ri, :, 1:NPc], in_=S[:, ri, :, 0:NPc - 1]),
                      R=[sb_], W=[Hin.b[0]])
                k.dve(lambda e, ri=ri: e.tensor_copy(out=Hin[:, ri, :, NPc:NPc + 16], in_=H0[:, ri]),
                      R=[H0.b[0]], W=[Hin.b[0]])
                k.dve(lambda e, ri=ri: e.tensor_copy(out=HL[:, ri, ps_], in_=S[:, ri, :, NPc - 1]),
                      R=[sb_], W=[HL.b[0]])
            Tm = k.alloc("Tm", [128, 4, NPB, 16], F32)
            tb = Tm.b[0]

            def cmac(dr, di, ar, ai, xr, xi):
                tv = [Tm[:, q] for q in range(4)]
                Rr = [sb_, tb, scb, H0.b[0]]
                for q, (a_, x_) in enumerate(((ar, xr), (ai, xi), (ai, xr), (ar, xi))):
                    k.dve(lambda e, q=q, a_=a_, x_=x_: e.tensor_tensor(out=tv[q], in0=a_, in1=x_, op=ALU.mult),
                          R=Rr, W=[tb])
                k.dve(lambda e: e.tensor_tensor(out=tv[0], in0=tv[0], in1=tv[1], op=ALU.subtract), R=Rr, W=[tb])
                k.dve(lambda e: e.tensor_tensor(out=tv[2], in0=tv[2], in1=tv[3], op=ALU.add), R=Rr, W=[tb])
                k.dve(lambda e: e.tensor_tensor(out=dr, in0=dr, in1=tv[0], op=ALU.add), R=Rr, W=[sb_, tb])
                k.dve(lambda e: e.tensor_tensor(out=di, in0=di, in1=tv[2], op=ALU.add), R=Rr, W=[sb_, tb])

            A8r = bc(SC[:, 0, ps_, 8], [128, NPB, 16], 2)
            A8i = bc(SC[:, 1, ps_, 8], [128, NPB, 16], 2)
            Ss = [S[:, ri, :, NPc:NPc + 16] for ri in range(2)]
            cmac(Ss[0], Ss[1], A8r, A8i, H0[:, 0], H0[:, 1])
            HSo = k.alloc("HSo", [128, 2, NPB, 16], F32)
            k.dve(lambda e: e.memset(HSo[:], 0.0), W=[HSo.b[0], sb_])
            cmac(HSo[:, 0], HSo[:, 1], bc(SC[:, 0, ps_, 17], [128, NPB, 16], 2), bc(SC[:, 1, ps_, 17], [128, NPB, 16], 2),
                 Ss[0], Ss[1])
            hso = k.alloc("hso", [NSEQ, 2, NPB * 128], F32)
            for ri in range(2):
                bk = k.bank()
                for q in range(NPB):
                    k.pe(lambda e, q=q, ri=ri, bk=bk: e.transpose(out=bk.t[0:NSEQ, q * 128:(q + 1) * 128],
                                                                  in_=HSo[:, ri, q, :], identity=idf[:, :]),
                         R=[sb_, HSo.b[0], idf.b[0]], W=[bk.b[0]], inc=(q == NPB - 1))
                k.act(lambda e, ri=ri, bk=bk: e.copy(out=hso[0:NSEQ, ri, :], in_=bk.t[0:NSEQ, 0:NPB * 128]),
                      R=[bk.b[0]], W=[hso.b[0]])
            k.dma("pool", o_hs[:, :, bl * NPB * 128:(bl + 1) * NPB * 128], hso[:], R=[hso.b[0]])
            k.release(hso)
            k.release(HSo)
            k.release(Tm)

        def s3y(B):
            bl = B.bl

            def y_stage1(gl):
                pp, g2 = gl // 2, gl % 2
                rows = slice(64 * g2, 64 * g2 + 64)
                bk = k.bank()
                mm_group(bk.t[:, 0:272], bk,
                         [(B.TTb[:, gl, :], B.Ut[:, gl, :]),
                          (B.CXb[rows, pp, 0, :], B.Hin[rows, 0, pp, :]),
                          (B.CXb[rows, pp, 1, :], B.Hin[rows, 1, pp, :])],
                         R=[B.TTb.b[0], B.Ut.b[0], B.CXb.b[0], B.Hin.b[0]])
                Ysb = k.alloc("Ysb", [128, 272], F32)
                k.act(lambda e: e.copy(out=Ysb[:, :], in_=bk.t[:, 0:272]), R=[bk.b[0]], W=[Ysb.b[0]])
                return Ysb

            def y_stage2(gl, Ysb):
                g = bl * NGB + gl
                bk2 = k.bank()
                for kt in range(2):
                    k.pe(lambda e, kt=kt: e.transpose(out=bk2.t[:, kt * 128:(kt + 1) * 128],
                                                      in_=Ysb[:, kt * 128:(kt + 1) * 128], identity=idf[:, :]),
                         R=[Ysb.b[0], idf.b[0]], W=[bk2.b[0]], inc=False)
                k.pe(lambda e: e.transpose(out=bk2.t[0:NSEQ, 256:384], in_=Ysb[:, 256:272], identity=idf[:, :]),
                     R=[Ysb.b[0], idf.b[0]], W=[bk2.b[0]], inc=True)
                k.release(Ysb)
                k.act(lambda e: e.activation(out=UG[:, :, g, :], in_=bk2.t[:, 0:256].rearrange("p (t c) -> p t c", t=2),
                                             func=AF.Gelu_apprx_tanh), R=[bk2.b[0]], W=[UG.b[0], UG.b[1]])
                k.act(lambda e: e.activation(out=USG[0:NSEQ, g, :], in_=bk2.t[0:NSEQ, 256:320], func=AF.Gelu_apprx_tanh),
                      R=[bk2.b[0]], W=[USG.b[0]])

            pend = [y_stage1(0), y_stage1(1)]
            for gl in range(NGB):
                if gl + 2 < NGB:
                    pend.append(y_stage1(gl + 2))
                y_stage2(gl, pend.pop(0))
            for t_ in (B.TTb, B.BXb, B.CXb, B.S, B.Hin, B.Ut, B.H0):
                k.release(t_)

        cur = s1(0)
        Ecur = gen_E(0)
        Enext = gen_E(1)
        s2(cur, Ecur)
        for bl in range(NBLK):
            nxt = s1(bl + 1) if bl + 1 < NBLK else None
            s3prep(cur)
            if nxt is not None:
                En2 = gen_E(bl + 2) if bl + 2 < NBLK else None
                s2(nxt, Enext)
                Enext = En2
            s3y(cur)
            cur = nxt
        bk = k.bank()
        for ri in range(2):
            k.pe(lambda e, ri=ri: e.transpose(out=bk.t[0:48, ri * 128:(ri + 1) * 128], in_=HL[:, ri, :], identity=idf[:, :]),
                 R=[HL.b[0], idf.b[0]], W=[bk.b[0]], inc=(ri == 1))
        hlo = k.alloc("hlo", [48, 2, 128], F32)
        k.act(lambda e: e.copy(out=hlo[:].rearrange("p a b -> p (a b)"), in_=bk.t[0:48, 0:256]), R=[bk.b[0]], W=[hlo.b[0]])
        k.dma("pool", o_hp.rearrange("r q c -> q r c"), hlo[:], R=[hlo.b[0]])
        k.release(hlo)
        k.release(HL)
        k.release(SC)
        k.release(P8)
        Wg = load_w("Wg", w_glu, BR, BR)
        uyP = k.alloc("uyP", [128, 2, 8, BR], BF16, nb=16)
        uyS = k.alloc("uyS", [NSEQ, 4, BR], BF16)
        cnt = 0
        for kt in range(2):
            for j in range(8):
                sel = cnt % 5
                cnt += 1
                if sel in (0, 2):
                    k.act(lambda e, kt=kt, j=j: e.copy(out=uyP[:, kt, j, :].rearrange("p (g c) -> p g c", c=16),
                                                       in_=UG5[:, kt, :, j, :]),
                          R=[UG.b[kt]], W=[uyP.b[kt * 8 + j]])
                else:
                    eng = k.pool if sel == 4 else k.dve
                    eng(lambda e, kt=kt, j=j: e.tensor_copy(out=uyP[:, kt, j, :].rearrange("p (g c) -> p g c", c=16),
                                                            in_=UG5[:, kt, :, j, :]),
                        R=[UG.b[kt]], W=[uyP.b[kt * 8 + j]])
        for j in range(4):
            k.pool(lambda e, j=j: e.tensor_copy(out=uyS[0:NSEQ, j, :].rearrange("p (g c) -> p g c", c=16),
                                                in_=USG4[:, :, j, :]), R=[USG.b[0]], W=[uyS.b[0]])
        k.release(UG)
        k.release(USG)

        Wqg = load_w("Wqg", w_in_b, D, INB, c0=BR, c1=INB)
        bgl = k.alloc("bgl", [1, BR], BF16)
        k.dma("pool", bgl[:], bglu_d, W=[bgl.b[0]])
        ys = k.alloc("ys", [NS, BR], BF16)
        for j in range(4):
            k.dma("sp", ys[16 * j:16 * j + 16, :], uyS[0:NSEQ, j, :], R=[uyS.b[0]], W=[ys.b[0]])

        def glu_front(y_ap, ybuf, P):
            yT = k.alloc("yT", [128, 12, 128], BF16)
            transpose_to(yT, yT.b[0], y_ap, ybuf, P, 12)
            return yT

        def glu_tile(yT, y_ap, ybuf, P, next_front):
            nf = next_front() if next_front is not None else None
            for cb in range(3):
                bk = k.bank()
                pairs = [(yT[:, kc, 0:P], Wg[:, kc, cb * 512:(cb + 1) * 512]) for kc in range(12)]
                pairs.append((ones_b[0:1, 0:P], bgl[0:1, cb * 512:(cb + 1) * 512]))
                mm_group(bk.t[0:P, :], bk, pairs, R=[yT.b[0], Wg.b[0], ones_b.b[0], bgl.b[0]])
                sig = k.alloc("sig", [128, 512], F32)
                k.act(lambda e, bk=bk, sig=sig: e.activation(out=sig[0:P, :], in_=bk.t[0:P, :], func=AF.Sigmoid),
                      R=[bk.b[0]], W=[sig.b[0]])
                k.dve(lambda e, cb=cb, sig=sig: e.tensor_tensor(out=y_ap[0:P, cb * 512:(cb + 1) * 512],
                                                                in0=y_ap[0:P, cb * 512:(cb + 1) * 512],
                                                                in1=sig[0:P, :], op=ALU.mult),
                      R=[sig.b[0], ybuf], W=[ybuf])
                k.release(sig)
            k.release(yT)
            return nf

        gsrc = [(uyP[:, kt, j, :], uyP.b[kt * 8 + j], 128) for kt in range(2) for j in range(8)]
        gsrc.append((ys, ys.b[0], NS))
        yT = glu_front(*gsrc[0])
        for i, (y_ap, ybuf, P) in enumerate(gsrc):
            nxt = (lambda i=i: glu_front(*gsrc[i + 1])) if i + 1 < len(gsrc) else None
            yT = glu_tile(yT, y_ap, ybuf, P, nxt)
        k.release(Wg)
        k.release(bgl)

        Wo = load_w("Wo1", w_out[1], MIXW, D)

        def b_tile(fr, P, sample, br_ap, brbuf, dst, next_front):
            xt, hT = fr
            RW = [hT.b[0], Wqg.b[0]]
            qT = k.alloc("qT", [128, 4, 128], BF16)
            gT = k.alloc("gT", [128, 4, 128], F32) if sample else None
            gate = k.alloc("gate", [128, BR if sample else MIXW], F32)

            def fm_block(dstt, c0, fn):
                bk = k.bank()
                for hd in range(4):
                    mm_group(bk.t[:, hd * 128:hd * 128 + P], bk,
                             [(Wqg[:, kc, c0 + hd * 128:c0 + (hd + 1) * 128], hT[:, kc, 0:P]) for kc in range(8)],
                             R=RW, last_inc=(hd == 3))
                k.act(lambda e, bk=bk: e.activation(out=dstt[:, :, 0:P],
                                                    in_=bk.t.rearrange("p (h t) -> p h t", h=4)[:, :, 0:P], func=fn),
                      R=[bk.b[0]], W=[dstt.b[0]])

            fm_block(qT, 0, AF.Copy)
            if sample:
                fm_block(gT, XATT + BR, AF.Silu)
            for cb in range(3 if sample else 4):
                bk = k.bank()
                mm_group(bk.t[0:P, :], bk, [(hT[:, kc, 0:P], Wqg[:, kc, XATT + cb * 512:XATT + (cb + 1) * 512])
                                            for kc in range(8)], R=RW)
                k.act(lambda e, bk=bk, cb=cb: e.activation(out=gate[0:P, cb * 512:(cb + 1) * 512], in_=bk.t[0:P, :],
                                                          func=AF.Silu), R=[bk.b[0]], W=[gate.b[0]])
            k.release(hT)
            mixT = k.alloc("mixT", [128, 16, 128], BF16)
            mix = k.alloc("mix", [128, BR if sample else MIXW], BF16)
            k.dve(lambda e: e.tensor_tensor(out=mix[0:P, 0:BR], in0=gate[0:P, 0:BR], in1=br_ap, op=ALU.mult),
                  R=[gate.b[0], brbuf], W=[mix.b[0]])
            if sample:
                attn_sample(1, qT, gT, mixT)
                k.release(gT)
            else:
                attn_prompt(P, KT, Vb, qT, gate, BR, mix)
            k.release(gate)
            k.release(qT)
            transpose_to(mixT, mixT.b[0], mix, mix.b[0], P, 12 if sample else 16)
            k.release(mix)
            nf = next_front() if next_front is not None else None
            bks = [k.bank(), k.bank()]
            for cb in range(2):
                mm_group(bks[cb].t[0:P, :], bks[cb], [(mixT[:, kc, 0:P], Wo[:, kc, cb * 512:(cb + 1) * 512])
                                                     for kc in range(16)], R=[mixT.b[0], Wo.b[0]])
            k.release(mixT)
            post_and_store(P, bks, xt, gb_post, dst)
            k.release(xt)
            return nf

        bsrc = []
        for kt in range(2):
            for j in range(8):
                bsrc.append((x1_rows(kt, j), 128, False, uyP[:, kt, j, :], uyP.b[kt * 8 + j],
                             o_yp[kt * 1024 + j:(kt + 1) * 1024:8, :]))
        bsrc.append((x1d[SEQ:SEQ + NS, :], NS, True, ys[0:NS, :], ys.b[0], o_ys[:, :]))
        fr = load_norm_T(bsrc[0][0], bsrc[0][1])
        for i, (src, P, smp, br_ap, brbuf, dst) in enumerate(bsrc):
            nxt = (lambda i=i: load_norm_T(bsrc[i + 1][0], bsrc[i + 1][1])) if i + 1 < len(bsrc) else None
            fr = b_tile(fr, P, smp, br_ap, brbuf, dst, nxt)
        fsets.free()
        for t_ in (KT, Vb, Wqg, Wo, gb_pre, gb_post, uyP, uyS, ys):
            k.release(t_)

    SC = P8 = None
    Wa_pre = None
    if "l0" in parts and "setup" in parts:
        Wa_pre = load_w("Wa", w_in_a, D, INA)
    kv_pre = []

    def early_kv():
        gb_mem0 = load_gb(4)
        kv_pre.extend(mem_kv(0, gb_mem0))
        k.release(gb_mem0)

    if "setup" in parts:
        SC, P8 = ssm_setup(stop, after_tables=(early_kv if "l0" in parts else None))
    if "l0" in parts:
        layer0(Wa_pre, kv_pre)
    if "l1" in parts:
        layer1(SC, P8)

    k.finish()
    return nc


_PROGRAM = None
_BUILD_ARGS = None
_LAST = None


def _get_program():
    global _PROGRAM
    if _PROGRAM is None:
        _PROGRAM = build_program()
    return _PROGRAM


def _l2(x):
    return np.ascontiguousarray(x.reshape(2, 48, 2, 64).transpose(2, 3, 0, 1).reshape(128, 2, 48))


def kernel(x_prompt, x_sample, cache_mem_k, cache_mem_v, state_ssm_re, state_ssm_im, mem_prompt,
           w_in_a, ln_v_g, ln_v_b, w_spatial, b_spatial,
           w_in_b, ssm_lambda_re, ssm_lambda_im, ssm_log_dt, ssm_b_re, ssm_b_im, ssm_c_re, ssm_c_im,
           ssm_d, w_glu, b_glu,
           mem_norm_g, w_mem_k, w_mem_v, w_out, pre_norm_g, post_norm_g):
    import ml_dtypes
    f32 = np.float32
    A = lambda a: np.ascontiguousarray(np.asarray(a), dtype=f32)
    gvec = np.concatenate([A(pre_norm_g), A(post_norm_g), A(mem_norm_g)], axis=0)
    shared = {
        "w_mem_k": A(w_mem_k), "w_mem_v": A(w_mem_v), "gvec": gvec,
        "w_in_a": A(w_in_a)[0], "w_out": A(w_out),
        "lnv": np.concatenate([A(ln_v_g), A(ln_v_b)], axis=0),
        "wsT": np.ascontiguousarray(A(w_spatial)[0].transpose(2, 0, 1)),
        "trilT": np.triu(np.ones((128, 128), f32)),
        "bsT": np.ascontiguousarray(A(b_spatial)[0].T),
        "wsamp": np.ascontiguousarray(np.broadcast_to(
            A(w_spatial)[0][:, :4, :4].transpose(2, 0, 1)[:, None, :, :, None], (4, 16, 8, 4, 16)).reshape(64, 8, 64)),
        "mask_s": np.ascontiguousarray((np.triu(np.ones((4, 4), f32))[:, None, :, None]
                                        * np.eye(16, dtype=f32)[None, :, None, :]).reshape(64, 64)),
        "bs_s": np.ascontiguousarray(np.repeat(A(b_spatial)[0][:, :4].T, 16, axis=0)),
        "w_in_b": A(w_in_b)[0], "w_glu": A(w_glu)[0], "bglu": A(b_glu),
        "lam2": _l2(np.stack([A(ssm_lambda_re)[0], A(ssm_lambda_im)[0]], 0)),
        "ldt2": np.ascontiguousarray(np.broadcast_to(A(ssm_log_dt)[0].reshape(48, 2).T[:, None, :], (2, 64, 48)).reshape(128, 48)),
        "B2": np.ascontiguousarray(np.stack([A(ssm_b_re)[0], A(ssm_b_im)[0]], 0).reshape(2, 48, 2, 64, 16)
                                   .transpose(2, 3, 0, 1, 4).reshape(128, 2, 48, 16)),
        "C2": np.ascontiguousarray(np.stack([A(ssm_c_re)[0], A(ssm_c_im)[0]], 0).reshape(2, 48, 2, 16, 64)
                                   .transpose(2, 4, 0, 1, 3).reshape(128, 2, 48, 16)),
        "dcol": np.ascontiguousarray(np.tile(A(ssm_d)[0].reshape(96, 16).T, (8, 1))),
        "maskJ": np.ascontiguousarray(np.kron(np.triu(np.ones((8, 8), f32)), np.ones((16, 16), f32))),
        "ident_f": np.eye(128, dtype=f32), "ident_b": np.eye(128, dtype=f32).astype(ml_dtypes.bfloat16),
    }
    x_prompt = A(x_prompt); x_sample = A(x_sample); mem_prompt = A(mem_prompt)
    cache_mem_k = A(cache_mem_k); cache_mem_v = A(cache_mem_v)
    state_ssm_re = A(state_ssm_re); state_ssm_im = A(state_ssm_im)
    in_maps = []
    for c in range(NCORES):
        m = dict(shared)
        m["xp"] = x_prompt[c]
        xs_c = x_sample[c * NSEQ:(c + 1) * NSEQ]
        m["xs"] = np.ascontiguousarray(xs_c.transpose(1, 0, 2).reshape(NS, D))
        m["mem"] = mem_prompt[c]
        m["st"] = np.ascontiguousarray(np.stack([state_ssm_re[0, c * NSEQ:(c + 1) * NSEQ].reshape(NSEQ, 6144),
                                                 state_ssm_im[0, c * NSEQ:(c + 1) * NSEQ].reshape(NSEQ, 6144)], axis=1))
        m["kc"] = np.ascontiguousarray(cache_mem_k[:, c * NSEQ:(c + 1) * NSEQ].reshape(2, NSEQ, NMEM, XATT))
        m["vc"] = np.ascontiguousarray(cache_mem_v[:, c * NSEQ:(c + 1) * NSEQ].reshape(2, NSEQ, NMEM, XATT))
        in_maps.append(m)
    if _BUILD_ARGS is not None:
        nc = build_program(**_BUILD_ARGS)
        res = run_bass_kernel_spmd(nc, in_maps[:1], core_ids=[0])
        global _LAST
        _LAST = res.results
        return None
    nc = _get_program()
    res = run_bass_kernel_spmd(nc, in_maps, core_ids=list(range(NCORES)))
    R = res.results
    y_prompt = np.stack([R[c]["o_yp"] for c in range(NCORES)], axis=0)
    y_sample = np.concatenate([R[c]["o_ys"].reshape(4, NSEQ, D).transpose(1, 0, 2) for c in range(NCORES)], axis=0)
    mk = np.stack([R[c]["o_mk"] for c in range(NCORES)], axis=1).reshape(2, NCORES, NMEM, 4, 128)
    mv = np.stack([R[c]["o_mv"] for c in range(NCORES)], axis=1).reshape(2, NCORES, NMEM, 4, 128)
    hp_re = np.stack([R[c]["o_hp"][0].reshape(96, 64) for c in range(NCORES)], axis=0)[None]
    hp_im = np.stack([R[c]["o_hp"][1].reshape(96, 64) for c in range(NCORES)], axis=0)[None]
    hs_re = np.concatenate([R[c]["o_hs"][:, 0].reshape(NSEQ, 96, 64) for c in range(NCORES)], axis=0)[None]
    hs_im = np.concatenate([R[c]["o_hs"][:, 1].reshape(NSEQ, 96, 64) for c in range(NCORES)], axis=0)[None]
    v_s = np.concatenate([R[c]["o_vs"].reshape(4, NSEQ, BR).transpose(1, 0, 2) for c in range(NCORES)], axis=0)[None]
    return (y_prompt, y_sample, mk, mv, hp_re, hp_im, hs_re, hs_im, v_s)
```

```python
import numpy as np
import concourse.bass as bass
import concourse.mybir as mybir
from concourse.bass_utils import run_bass_kernel_spmd

F32 = mybir.dt.float32
BF16 = mybir.dt.bfloat16
AF = mybir.ActivationFunctionType
ALU = mybir.AluOpType

D = 1024
SEQ = 2048
NT = SEQ // 128
NS = 64
NSEQ = 16
BR = 1536
XATT = 512
MIXW = 2048
NMEM = 256
INA = 2 * BR + XATT + MIXW
INB = BR + XATT + MIXW
EPS = 1e-6
NCORES = 8

SB_BASE = 16512
SB_TOP = 229344
NDS = 40


def _dsize(dt):
    return 4 if dt == F32 else 2


class Buf:
    __slots__ = ("w", "r")

    def __init__(self):
        self.w = {}
        self.r = {}


def _merge(d, key, val):
    if d.get(key, 0) < val:
        d[key] = val


class T:
    def __init__(self, handle, off, nbytes, nb):
        self.t = handle
        self.off = off
        self.nbytes = nbytes
        self.b = [Buf() for _ in range(nb)]

    def __getitem__(self, idx):
        return self.t[idx]


class KB:
    def __init__(self, nc):
        self.nc = nc
        self.eng = {"pe": nc.tensor, "act": nc.scalar, "dve": nc.vector, "pool": nc.gpsimd, "sp": nc.sync}
        self.esem = {e: nc.alloc_semaphore("sem_" + e) for e in ("pe", "act", "dve", "pool")}
        self.ecnt = {e: 0 for e in self.esem}
        self.seen = {e: {} for e in self.eng}
        self.pending = {e: [] for e in self.esem}
        self.dsem = [nc.alloc_semaphore("dsem%d" % i) for i in range(NDS)]
        self.dcnt = [0] * NDS
        self.dnext = 0
        self.free = [(SB_BASE, SB_TOP)]
        self.dead = []
        self.nalloc = 0
        self.banks = []
        for i in range(8):
            h = nc.alloc_psum_tensor("bank%d" % i, [128, 512], F32)
            self.banks.append(T(h, 0, 0, 1))
        self.bnext = 0

    def alloc(self, name, shape, dtype, nb=1):
        n = 1
        for s in shape[1:]:
            n *= s
        nbytes = (n * _dsize(dtype) + 63) // 64 * 64
        for i, (s, e) in enumerate(self.free):
            if e - s >= nbytes:
                off = s
                if e - s == nbytes:
                    self.free.pop(i)
                else:
                    self.free[i] = (s + nbytes, e)
                break
        else:
            raise RuntimeError("SBUF arena full allocating %s (%d B); free=%s" % (name, nbytes, self.free))
        self.nalloc += 1
        h = self.nc.alloc_sbuf_tensor_at("%s_%d" % (name, self.nalloc), list(shape), dtype, offset=off)
        t = T(h, off, nbytes, nb)
        keep = []
        for (s, e, bufs) in self.dead:
            if s < off + nbytes and off < e:
                for ob in bufs:
                    for nbuf in t.b:
                        for k, v in ob.w.items():
                            _merge(nbuf.r, k, v)
                        for k, v in ob.r.items():
                            _merge(nbuf.r, k, v)
                if not (s >= off and e <= off + nbytes):
                    keep.append((s, e, bufs))
            else:
                keep.append((s, e, bufs))
        self.dead = keep
        return t

    def release(self, t, force=False):
        if getattr(t, "persistent", False) and not force:
            return
        self.dead.append((t.off, t.off + t.nbytes, t.b))
        fr = self.free + [(t.off, t.off + t.nbytes)]
        fr.sort()
        out = []
        for s, e in fr:
            if out and out[-1][1] == s:
                out[-1] = (out[-1][0], e)
            else:
                out.append((s, e))
        self.free = out

    def bank(self, hold=False):
        while True:
            b = self.banks[self.bnext]
            self.bnext = (self.bnext + 1) % 8
            if not getattr(b, "held", False):
                break
        if hold:
            b.held = True
        return b

    def unhold(self, b):
        b.held = False

    def _handle(self, key):
        return self.esem[key[1]] if key[0] == "e" else self.dsem[key[1]]

    def _waits(self, E, R, W, extra=None):
        deps = {}
        for b in R:
            for k, v in b.w.items():
                _merge(deps, k, v)
        for b in W:
            for k, v in b.w.items():
                _merge(deps, k, v)
            for k, v in b.r.items():
                _merge(deps, k, v)
        if extra:
            for k, v in extra.items():
                _merge(deps, k, v)
        eng = self.eng[E]
        seen = self.seen[E]
        for k, v in deps.items():
            if E == "pe" and k == ("e", "pe"):
                continue
            if v <= 0 or seen.get(k, 0) >= v:
                continue
            eng.wait_ge(self._handle(k), v)
            seen[k] = v

    def _record(self, tok, R, W):
        for b in R:
            _merge(b.r, tok[0], tok[1])
        for b in W:
            _merge(b.w, tok[0], tok[1])

    def op(self, E, fn, R=(), W=(), inc=True):
        self._waits(E, R, W)
        ins = fn(self.eng[E])
        if inc:
            self.ecnt[E] += 1
            ins.then_inc(self.esem[E], 1)
            tok = (("e", E), self.ecnt[E])
            self._record(tok, R, W)
            for (r2, w2) in self.pending[E]:
                self._record(tok, r2, w2)
            self.pending[E] = []
        else:
            assert E == "pe"
            self.pending[E].append((list(R), list(W)))
        return ins

    def pe(self, fn, R=(), W=(), inc=True):
        return self.op("pe", fn, R, W, inc)

    def act(self, fn, R=(), W=()):
        return self.op("act", fn, R, W)

    def dve(self, fn, R=(), W=()):
        return self.op("dve", fn, R, W)

    def pool(self, fn, R=(), W=()):
        return self.op("pool", fn, R, W)

    def dma(self, q, out, in_, R=(), W=(), **kw):
        i = self.dnext
        self.dnext = (i + 1) % NDS
        self._waits(q, R, W, extra={("d", i): self.dcnt[i]})
        self.dcnt[i] += 16
        self.eng[q].dma_start(out=out, in_=in_, **kw).then_inc(self.dsem[i], 16)
        tok = (("d", i), self.dcnt[i])
        self._record(tok, R, W)

    def finish(self):
        sp = self.eng["sp"]
        for i in range(NDS):
            if self.dcnt[i] > 0:
                sp.wait_ge(self.dsem[i], self.dcnt[i])
        for e in self.esem:
            if self.ecnt[e] > 0:
                sp.wait_ge(self.esem[e], self.ecnt[e])


def build_program(parts=("l0", "setup", "l1"), debug=False, stop=99):
    nc = bass.Bass("TRN2", target_bir_lowering=False)
    k = KB(nc)

    def din(name, shape, dt=F32):
        return nc.dram_tensor(name, list(shape), dt, kind="ExternalInput").ap()

    def dout(name, shape, dt=F32):
        return nc.dram_tensor(name, list(shape), dt, kind="ExternalOutput").ap()

    xp = din("xp", [SEQ, D])
    xs = din("xs", [NS, D])
    mem = din("mem", [NMEM, D])
    w_mem_k = din("w_mem_k", [2, D, XATT])
    w_mem_v = din("w_mem_v", [2, D, XATT])
    gvec = din("gvec", [6, D])
    ident_f = din("ident_f", [128, 128])
    ident_b = din("ident_b", [128, 128], BF16)

    w_in_a = din("w_in_a", [D, INA])
    w_out = din("w_out", [2, MIXW, D])
    lnv = din("lnv", [2, BR])
    wsT_d = din("wsT", [128, 8, 128])
    trilT_d = din("trilT", [128, 128])
    bsT_d = din("bsT", [128, 8])
    wsamp_d = din("wsamp", [64, 8, 64])
    mask_s_d = din("mask_s", [64, 64])
    bs_s_d = din("bs_s", [64, 8])
    kc_d = din("kc", [2, NSEQ, NMEM, XATT])
    vc_d = din("vc", [2, NSEQ, NMEM, XATT])
    x1d = nc.dram_tensor("x1d", [SEQ + NS, D], F32, kind="Internal").ap()
    lam2_d = din("lam2", [128, 2, 48])
    ldt2_d = din("ldt2", [128, 48])
    B2_d = din("B2", [128, 2, 48, 16])
    C2_d = din("C2", [128, 2, 48, 16])
    dcol_d = din("dcol", [128, 96])
    maskJ_d = din("maskJ", [128, 128])
    dk = "ExternalOutput" if debug else "Internal"
    TT_d = nc.dram_tensor("TT_d", [96, 128, 128], BF16, kind=dk).ap()
    BX_d = nc.dram_tensor("BX_d", [96, 128, 2, 64], BF16, kind=dk).ap()
    CX_d = nc.dram_tensor("CX_d", [48, 128, 2, 128], BF16, kind=dk).ap()
    SC_d = nc.dram_tensor("SC_d", [128, 2, 48, 26], F32, kind=dk).ap()
    P8_d = nc.dram_tensor("P8_d", [128, 2, 48, 32], F32, kind="Internal").ap()

    o_yp = dout("o_yp", [SEQ, D])
    o_ys = dout("o_ys", [NS, D])
    o_mk = dout("o_mk", [2, NMEM, XATT])
    o_mv = dout("o_mv", [2, NMEM, XATT])
    o_vs = dout("o_vs", [NS, BR])
    o_hp = dout("o_hp", [2, 48, 128])
    o_hs = dout("o_hs", [NSEQ, 2, 6144])
    w_in_b = din("w_in_b", [D, INB])
    w_glu = din("w_glu", [BR, BR])
    bglu_d = din("bglu", [1, BR])
    st_d = din("st", [NSEQ, 2, 6144])

    idf = k.alloc("idf", [128, 128], F32)
    idb = k.alloc("idb", [128, 128], BF16)
    k.dma("sp", idf[:], ident_f, W=[idf.b[0]])
    k.dma("sp", idb[:], ident_b, W=[idb.b[0]])
    cneg = k.alloc("cneg", [128, 1], F32)
    k.pool(lambda e: e.memset(cneg[:], -0.5), W=[cneg.b[0]])

    def load_gb(row):
        t = k.alloc("gb", [128, D], F32)
        k.dma("sp", t[:], gvec[row:row + 1, :].broadcast_to([128, D]), W=[t.b[0]])
        return t

    def load_w(name, src, rows, cols, c0=0, c1=None, q="pool"):
        c1 = cols if c1 is None else c1
        kc = rows // 128
        t = k.alloc(name, [128, kc, c1 - c0], BF16)
        v = src.rearrange("(kc p) n -> p kc n", p=128)
        step = 1024
        for cb in range(c0, c1, step):
            ce = min(c1, cb + step)
            k.dma(q, t[:, :, cb - c0:ce - c0], v[:, :, cb:ce], W=[t.b[0]])
        return t

    def rmsnorm_bf(xt, xbuf, P, gb, bufs=None):
        if bufs is not None:
            junk, ss, h_pre = bufs
        else:
            junk = k.alloc("junk", [128, D], BF16)
            ss = k.alloc("ss", [128, 4], F32)
            h_pre = None
        k.act(lambda e: e.activation(out=junk[0:P, :], in_=xt, func=AF.Square, accum_out=ss[0:P, 0:1]),
              R=[xbuf], W=[junk.b[0], ss.b[0]])
        k.dve(lambda e: e.tensor_scalar(out=ss[0:P, 1:2], in0=ss[0:P, 0:1], scalar1=1.0 / D, scalar2=EPS,
                                        op0=ALU.mult, op1=ALU.add), R=[ss.b[0]], W=[ss.b[0]])
        k.pool(lambda e: e.tensor_tensor(out=ss[0:P, 2:3], in0=ss[0:P, 1:2], in1=cneg[0:P, :], op=ALU.pow),
               R=[ss.b[0], cneg.b[0]], W=[ss.b[0]])
        h = h_pre if h_pre is not None else k.alloc("h", [128, D], BF16)
        k.dve(lambda e: e.scalar_tensor_tensor(out=h[0:P, :], in0=xt, scalar=ss[0:P, 2:3], in1=gb[0:P, :],
                                               op0=ALU.mult, op1=ALU.mult),
              R=[xbuf, ss.b[0], gb.b[0]], W=[h.b[0]])
        k.release(junk)
        k.release(ss)
        return h

    def transpose_to(dst, dbuf, src, sbuf, P, nchunk, col0=0):
        done = 0
        while done < nchunk:
            n = min(8, nchunk - done)
            bk = k.bank()
            pv = bk.t.bitcast(BF16)
            for c in range(n):
                cc = done + c
                k.pe(lambda e, c=c, cc=cc: e.transpose(out=pv[:, c * 128:c * 128 + P],
                                                       in_=src[0:P, cc * 128:(cc + 1) * 128],
                                                       identity=idb[0:P, 0:P]),
                     R=[sbuf, idb.b[0]], W=[bk.b[0]], inc=(c == n - 1))
            pview = pv.rearrange("p (c t) -> p c t", t=128)
            k.dve(lambda e, n=n, d0=done: e.tensor_copy(out=dst[:, d0:d0 + n, col0:col0 + P],
                                                         in_=pview[:, 0:n, 0:P]),
                  R=[bk.b[0]], W=[dbuf])
            done += n

    class FrontSets:
        def __init__(self, n=2):
            self.sets = []
            for _ in range(n):
                d = dict(xt=k.alloc("fxt", [128, D], F32), junk=k.alloc("fjunk", [128, D], BF16),
                         ss=k.alloc("fss", [128, 4], F32), h=k.alloc("fh", [128, D], BF16),
                         hT=k.alloc("fhT", [128, 8, 128], BF16))
                for t_ in d.values():
                    t_.persistent = True
                self.sets.append(d)
            self.i = 0

        def front(self, xsrc, P, gb, xdeps=()):
            d = self.sets[self.i % len(self.sets)]
            self.i += 1
            xt, hT = d["xt"], d["hT"]
            k.dma("sp", xt[0:P, :], xsrc, R=list(xdeps), W=[xt.b[0]])
            h = rmsnorm_bf(xt[0:P, :], xt.b[0], P, gb, bufs=(d["junk"], d["ss"], d["h"]))
            transpose_to(hT, hT.b[0], h, h.b[0], P, 8)
            return xt, hT

        def free(self):
            for d in self.sets:
                for t_ in d.values():
                    k.release(t_, force=True)

    x1buf = Buf()
    scrbuf = Buf()

    def mem_kv(layer, gb_mem):
        wk = load_w("wk", w_mem_k[layer], D, XATT)
        wv = load_w("wv", w_mem_v[layer], D, XATT)
        mT = k.alloc("mT", [128, 8, NMEM], BF16)
        for mt in range(2):
            xt = k.alloc("memx", [128, D], F32)
            k.dma("sp", xt[:], mem[mt * 128:(mt + 1) * 128, :], W=[xt.b[0]])
            h = rmsnorm_bf(xt[:, :], xt.b[0], 128, gb_mem)
            transpose_to(mT, mT.b[0], h, h.b[0], 128, 8, col0=mt * 128)
            k.release(h)
            k.release(xt)
        KT = k.alloc("KT", [128, 4, NMEM], BF16)
        Vb = k.alloc("Vb", [128, 2, 4, 132], BF16)
        k.pool(lambda e: e.memset(Vb[:, :, :, 128:132], 1.0), W=[Vb.b[0]])
        for mt in range(2):
            for which, w, odram in ((0, wk, o_mk), (1, wv, o_mv)):
                bk = k.bank()
                for kc in range(8):
                    k.pe(lambda e, kc=kc: e.matmul(bk.t[:, :], lhsT=mT[:, kc, mt * 128:(mt + 1) * 128],
                                                   rhs=w[:, kc, :], start=(kc == 0), stop=(kc == 7)),
                         R=[mT.b[0], w.b[0]], W=[bk.b[0]], inc=(kc == 7))
                st = k.alloc("kvst", [128, XATT], F32)
                k.act(lambda e: e.copy(out=st[:, :], in_=bk.t[:, :]), R=[bk.b[0]], W=[st.b[0]])
                if which == 1:
                    k.dve(lambda e: e.tensor_copy(out=Vb[:, mt, :, 0:128], in_=st[:, :].rearrange("p (h d) -> p h d", h=4)),
                          R=[st.b[0]], W=[Vb.b[0]])
                k.dma("pool", odram[layer, mt * 128:(mt + 1) * 128, :], st[:, :], R=[st.b[0]])
                k.release(st)
        for hp in range(2):
            bk = k.bank()
            for hh in range(2):
                hd = hp * 2 + hh
                for kc in range(8):
                    k.pe(lambda e, kc=kc, hd=hd, hh=hh: e.matmul(bk.t[:, hh * 256:(hh + 1) * 256],
                                                                 lhsT=wk[:, kc, hd * 128:(hd + 1) * 128],
                                                                 rhs=mT[:, kc, :], start=(kc == 0), stop=(kc == 7)),
                         R=[mT.b[0], wk.b[0]], W=[bk.b[0]], inc=(kc == 7 and hh == 1))
            k.act(lambda e, hp=hp: e.copy(out=KT[:, hp * 2:hp * 2 + 2, :],
                                          in_=bk.t.rearrange("p (h m) -> p h m", h=2)),
                  R=[bk.b[0]], W=[KT.b[0]])
        k.release(mT)
        k.release(wk)
        k.release(wv)
        return KT, Vb

    ones_b = k.alloc("ones_b", [128, 128], BF16)
    k.pool(lambda e: e.memset(ones_b[:], 1.0), W=[ones_b.b[0]])

    def rstd_from_ms(st, P, src_col, dst_col):
        k.pool(lambda e: e.tensor_tensor(out=st[0:P, dst_col:dst_col + 1], in0=st[0:P, src_col:src_col + 1],
                                         in1=cneg[0:P, :], op=ALU.pow),
               R=[st.b[0], cneg.b[0]], W=[st.b[0]])

    def mm_group(outap, bk, pairs, R, last_inc=True):
        n = len(pairs)
        for i, (l, r) in enumerate(pairs):
            k.pe(lambda e, l=l, r=r, i=i: e.matmul(outap, lhsT=l, rhs=r, start=(i == 0), stop=(i == n - 1)),
                 R=R, W=[bk.b[0]], inc=(last_inc and i == n - 1))

    def attn_sample(layer, qT, gT, mixT):
        P = NS
        bkS = k.bank(hold=True)
        sview = bkS.t.rearrange("p (s g j) -> p s g j", s=NSEQ, g=8)
        NBUF = 4
        Kbs = [k.alloc("Kb", [128, 2, XATT], BF16) for _ in range(NBUF)]

        def load_k(b):
            k.dma("pool", Kbs[b % NBUF][:], kc_d[layer, b].rearrange("(mc m) f -> m mc f", mc=2), W=[Kbs[b % NBUF].b[0]])

        for b in range(NBUF):
            load_k(b)
        for b in range(NSEQ):
            Kb = Kbs[b % NBUF]
            bk = k.bank()
            pv = bk.t.bitcast(BF16)
            for hd in range(4):
                for mc in range(2):
                    c = hd * 2 + mc
                    k.pe(lambda e, c=c, hd=hd, mc=mc: e.transpose(out=pv[:, c * 128:(c + 1) * 128],
                                                                   in_=Kb[:, mc, hd * 128:(hd + 1) * 128],
                                                                   identity=idb[:, :]),
                         R=[Kb.b[0], idb.b[0]], W=[bk.b[0]], inc=(c == 7))
            if b + NBUF < NSEQ:
                load_k(b + NBUF)
            KTs = k.alloc("KTs", [128, 1024], BF16)
            k.dve(lambda e: e.tensor_copy(out=KTs[:, :], in_=pv[:, :]), R=[bk.b[0]], W=[KTs.b[0]])
            for hd in range(4):
                for mc in range(2):
                    c = hd * 2 + mc
                    k.pe(lambda e, c=c, hd=hd, b=b: e.matmul(sview[:, b, c, :], lhsT=KTs[:, c * 128:(c + 1) * 128],
                                                             rhs=qT[:, hd, b:NS:NSEQ], start=True, stop=True),
                         R=[KTs.b[0], qT.b[0]], W=[bkS.b[0]], inc=(c == 7))
            k.release(KTs)
        for t_ in Kbs:
            k.release(t_)
        PTs = k.alloc("PTs", [128, NSEQ, 8, 4], BF16)
        k.act(lambda e: e.activation(out=PTs[:].rearrange("p s g j -> p (s g j)"), in_=bkS.t[:, :], func=AF.Exp,
                                     scale=float(128 ** -0.5)),
              R=[bkS.b[0]], W=[PTs.b[0]])
        k.unhold(bkS)
        bkO = k.bank(hold=True)
        bkR = k.bank(hold=True)
        Vss = [k.alloc("Vs", [128, 2, XATT], BF16) for _ in range(NBUF)]

        def load_v(b):
            k.dma("pool", Vss[b % NBUF][:], vc_d[layer, b].rearrange("(mc m) f -> m mc f", mc=2), W=[Vss[b % NBUF].b[0]])

        for b in range(NBUF):
            load_v(b)
        for b in range(NSEQ):
            Vs = Vss[b % NBUF]
            for hd in range(4):
                oc = bkO.t[:, hd * 128 + b:hd * 128 + NS:NSEQ]
                rc = bkR.t[:, hd * 128 + b:hd * 128 + NS:NSEQ]
                mm_group(oc, bkO, [(Vs[:, mc, hd * 128:(hd + 1) * 128], PTs[:, b, hd * 2 + mc, :]) for mc in range(2)],
                         R=[Vs.b[0], PTs.b[0]], last_inc=False)
                mm_group(rc, bkR, [(ones_b[:, :], PTs[:, b, hd * 2 + mc, :]) for mc in range(2)],
                         R=[ones_b.b[0], PTs.b[0]], last_inc=(hd == 3))
            if b + NBUF < NSEQ:
                load_v(b + NBUF)
        for t_ in Vss:
            k.release(t_)
        k.release(PTs)
        finish_attn(P, bkO, bkR, gT, mixT)
        k.unhold(bkO)
        k.unhold(bkR)

    def finish_attn(P, bkO, bkR, gT, mixT):
        ov = bkO.t.rearrange("p (h t) -> p h t", h=4)
        rv = bkR.t.rearrange("p (h t) -> p h t", h=4)
        rrec = k.alloc("rrec", [128, 4, 128], F32)
        k.dve(lambda e: e.reciprocal(out=rrec[:, :, 0:P], in_=rv[:, :, 0:P]), R=[bkR.b[0]], W=[rrec.b[0]])
        k.dve(lambda e: e.tensor_tensor(out=rrec[:, :, 0:P], in0=rrec[:, :, 0:P], in1=gT[:, :, 0:P], op=ALU.mult),
              R=[rrec.b[0], gT.b[0]], W=[rrec.b[0]])
        k.dve(lambda e: e.tensor_tensor(out=mixT[:, 12:16, 0:P], in0=ov[:, :, 0:P], in1=rrec[:, :, 0:P], op=ALU.mult),
              R=[bkO.b[0], rrec.b[0]], W=[mixT.b[0]])
        k.release(rrec)

    def attn_prompt(P, KT, Vb, qT, gate, goff, mix):
        bkA = k.bank()
        bkB = k.bank()
        for hd in range(4):
            bk = bkA if hd < 2 else bkB
            for mc in range(2):
                c = (hd % 2) * 2 + mc
                k.pe(lambda e, bk=bk, c=c, hd=hd, mc=mc: e.matmul(bk.t[:, c * 128:c * 128 + P],
                                                                  lhsT=KT[:, hd, mc * 128:(mc + 1) * 128],
                                                                  rhs=qT[:, hd, 0:P], start=True, stop=True),
                     R=[KT.b[0], qT.b[0]], W=[bk.b[0]], inc=(c == 3))
        PT = k.alloc("PT", [128, 8, 128], BF16)
        for i, bk in enumerate((bkA, bkB)):
            k.act(lambda e, i=i, bk=bk: e.activation(out=PT[:, i * 4:(i + 1) * 4, :],
                                                     in_=bk.t.rearrange("p (c t) -> p c t", c=4),
                                                     func=AF.Exp, scale=float(128 ** -0.5)),
                  R=[bk.b[0]], W=[PT.b[0]])
        rr = k.alloc("rr", [128, 4], F32)
        obk = [k.bank(), k.bank()]
        for hp in range(2):
            bk = obk[hp]
            for hh in range(2):
                hd = hp * 2 + hh
                mm_group(bk.t[0:P, hh * 129:(hh + 1) * 129], bk,
                         [(PT[:, hd * 2 + mc, 0:P], Vb[:, mc, hd, 0:129]) for mc in range(2)],
                         R=[Vb.b[0], PT.b[0]], last_inc=(hh == 1))
            k.dve(lambda e, hp=hp, bk=bk: e.reciprocal(out=rr[0:P, hp * 2:hp * 2 + 2], in_=bk.t[0:P, 128:258:129]),
                  R=[bk.b[0]], W=[rr.b[0]])
            for hh in range(2):
                hd = hp * 2 + hh
                k.dve(lambda e, hh=hh, hd=hd, bk=bk: e.scalar_tensor_tensor(
                    out=mix[0:P, BR + hd * 128:BR + (hd + 1) * 128], in0=bk.t[0:P, hh * 129:hh * 129 + 128],
                    scalar=rr[0:P, hd:hd + 1], in1=gate[0:P, goff + hd * 128:goff + (hd + 1) * 128],
                    op0=ALU.mult, op1=ALU.mult), R=[bk.b[0], rr.b[0], gate.b[0]], W=[mix.b[0]])
        k.release(PT)
        k.release(rr)

    def post_and_store(P, bks, xt, gb_post, dst, dst2=None, wbuf=None):
        st = k.alloc("pst", [128, 4], F32)
        junk = k.alloc("pjunk", [128, 512], BF16)
        for i, bk in enumerate(bks):
            k.act(lambda e, i=i, bk=bk: e.activation(out=junk[0:P, :], in_=bk.t[0:P, :], func=AF.Square,
                                                     accum_out=st[0:P, i:i + 1]),
                  R=[bk.b[0]], W=[junk.b[0], st.b[0]])
        k.dve(lambda e: e.tensor_tensor(out=st[0:P, 2:3], in0=st[0:P, 0:1], in1=st[0:P, 1:2], op=ALU.add),
              R=[st.b[0]], W=[st.b[0]])
        k.dve(lambda e: e.tensor_scalar(out=st[0:P, 3:4], in0=st[0:P, 2:3], scalar1=1.0 / D, scalar2=EPS,
                                        op0=ALU.mult, op1=ALU.add), R=[st.b[0]], W=[st.b[0]])
        rstd_from_ms(st, P, 3, 0)
        x1 = k.alloc("x1", [128, D], F32)
        for i, bk in enumerate(bks):
            k.dve(lambda e, i=i, bk=bk: e.scalar_tensor_tensor(out=x1[0:P, i * 512:(i + 1) * 512], in0=bk.t[0:P, :],
                                                               scalar=st[0:P, 0:1],
                                                               in1=gb_post[0:P, i * 512:(i + 1) * 512],
                                                               op0=ALU.mult, op1=ALU.mult),
                  R=[bk.b[0], st.b[0], gb_post.b[0]], W=[x1.b[0]])
        k.dve(lambda e: e.tensor_tensor(out=x1[0:P, :], in0=x1[0:P, :], in1=xt[0:P, :], op=ALU.add),
              R=[x1.b[0], xt.b[0]], W=[x1.b[0]])
        k.dma("pool", dst, x1[0:P, :], R=[x1.b[0]], W=([wbuf] if wbuf is not None else []))
        if dst2 is not None:
            k.dma("pool", dst2, x1[0:P, :], R=[x1.b[0]])
        k.release(st)
        k.release(junk)
        k.release(x1)


    def ssm_setup(stop=99, after_tables=None):
        TWO_PI = 6.283185307179586
        MAGIC = 12582912.0
        L = k.alloc("sL", [128, 2, 48], F32)
        LD = k.alloc("sLD", [128, 48], F32)
        B2 = k.alloc("sB2", [128, 2, 48, 16], F32)
        C2 = k.alloc("sC2", [128, 2, 48, 16], F32)
        dcol = k.alloc("sdcol", [128, 96], F32)
        mJ = k.alloc("smJ", [128, 128], F32)
        for t_, d_ in ((L, lam2_d), (LD, ldt2_d), (B2, B2_d), (C2, C2_d), (dcol, dcol_d), (mJ, maskJ_d)):
            k.dma("sp", t_[:], d_, W=[t_.b[0]])
        W1 = k.alloc("sW1", [128, 12, 48], F32)
        wb = W1.b[0]

        def tt(out, a, b, op, R=()):
            k.dve(lambda e: e.tensor_tensor(out=out, in0=a, in1=b, op=op), R=[wb] + list(R), W=[wb])

        k.act(lambda e: e.activation(out=W1[:, 0, :], in_=LD[:, :], func=AF.Exp), R=[LD.b[0]], W=[wb])
        tt(W1[:, 1, :], L[:, 0, :], W1[:, 0, :], ALU.mult, R=[L.b[0]])
        tt(W1[:, 2, :], L[:, 1, :], W1[:, 0, :], ALU.mult, R=[L.b[0]])
        k.act(lambda e: e.activation(out=W1[:, 3, :], in_=W1[:, 1, :], func=AF.Exp), R=[wb], W=[wb])

        def sin_of(dst, src, shift):
            k.dve(lambda e: e.tensor_scalar(out=W1[:, 8, :], in0=src, scalar1=shift, scalar2=None, op0=ALU.add),
                  R=[wb], W=[wb])
            k.dve(lambda e: e.tensor_scalar(out=W1[:, 9, :], in0=W1[:, 8, :], scalar1=1.0 / TWO_PI, scalar2=MAGIC,
                                            op0=ALU.mult, op1=ALU.add), R=[wb], W=[wb])
            k.dve(lambda e: e.tensor_scalar(out=W1[:, 10, :], in0=W1[:, 9, :], scalar1=-MAGIC, scalar2=None,
                                            op0=ALU.add), R=[wb], W=[wb])
            k.dve(lambda e: e.scalar_tensor_tensor(out=W1[:, 11, :], in0=W1[:, 10, :], scalar=-TWO_PI,
                                                   in1=W1[:, 8, :], op0=ALU.mult, op1=ALU.add), R=[wb], W=[wb])
            k.dve(lambda e: e.tensor_scalar(out=W1[:, 11, :], in0=W1[:, 11, :], scalar1=3.1415925, scalar2=-3.1415925,
                                            op0=ALU.min, op1=ALU.max), R=[wb], W=[wb])
            k.act(lambda e: e.activation(out=dst, in_=W1[:, 11, :], func=AF.Sin), R=[wb], W=[wb])

        sin_of(W1[:, 4, :], W1[:, 2, :], 0.0)
        sin_of(W1[:, 5, :], W1[:, 2, :], 1.5707963267948966)
        SC = k.alloc("SC", [128, 2, 48, 26], F32)
        P8 = k.alloc("P8", [128, 2, 48, 32], F32)
        scb = SC.b[0]

        def stt(out, a, b, op):
            k.dve(lambda e: e.tensor_tensor(out=out, in0=a, in1=b, op=op), R=[wb, scb, P8.b[0]], W=[wb, scb, P8.b[0]])

        k.dve(lambda e: e.memset(SC[:, 0, :, 0], 1.0), W=[scb])
        k.dve(lambda e: e.memset(SC[:, 1, :, 0], 0.0), W=[scb])
        stt(SC[:, 0, :, 1], W1[:, 3, :], W1[:, 5, :], ALU.mult)
        stt(SC[:, 1, :, 1], W1[:, 3, :], W1[:, 4, :], ALU.mult)

        def cmul(dr, di, xr, xi, yr, yi):
            stt(W1[:, 8, :], xr, yr, ALU.mult)
            stt(W1[:, 9, :], xi, yi, ALU.mult)
            stt(W1[:, 10, :], xr, yi, ALU.mult)
            stt(W1[:, 11, :], xi, yr, ALU.mult)
            stt(dr, W1[:, 8, :], W1[:, 9, :], ALU.subtract)
            stt(di, W1[:, 10, :], W1[:, 11, :], ALU.add)

        def sl(n):
            return SC[:, 0, :, n], SC[:, 1, :, n]

        W2 = k.alloc("sW2", [128, 4, 48, 8], F32)

        def vcmul(Td, d0, Ts, s0, m, Tm_, mi):
            Rr = [wb, scb, P8.b[0], W2.b[0]]
            xr, xi = Ts[:, 0, :, s0:s0 + m], Ts[:, 1, :, s0:s0 + m]
            yr = Tm_[:, 0, :, mi].unsqueeze(2).broadcast_to([128, 48, m])
            yi = Tm_[:, 1, :, mi].unsqueeze(2).broadcast_to([128, 48, m])
            tv = [W2[:, q, :, 0:m] for q in range(4)]
            for q, (a_, b_) in enumerate(((xr, yr), (xi, yi), (xr, yi), (xi, yr))):
                k.dve(lambda e, q=q, a_=a_, b_=b_: e.tensor_tensor(out=tv[q], in0=a_, in1=b_, op=ALU.mult),
                      R=Rr, W=[W2.b[0]])
            k.dve(lambda e: e.tensor_tensor(out=Td[:, 0, :, d0:d0 + m], in0=tv[0], in1=tv[1], op=ALU.subtract),
                  R=Rr, W=[wb, scb, P8.b[0]])
            k.dve(lambda e: e.tensor_tensor(out=Td[:, 1, :, d0:d0 + m], in0=tv[2], in1=tv[3], op=ALU.add),
                  R=Rr, W=[wb, scb, P8.b[0]])

        vcmul(SC, 2, SC, 1, 1, SC, 1)
        vcmul(SC, 3, SC, 1, 2, SC, 2)
        vcmul(SC, 5, SC, 1, 4, SC, 4)
        vcmul(SC, 9, SC, 1, 8, SC, 8)
        cmul(*sl(19), *sl(16), *sl(16))
        cmul(*sl(20), *sl(19), *sl(19))
        cmul(*sl(21), *sl(20), *sl(20))
        cmul(*sl(22), *sl(21), *sl(21))
        cmul(*sl(23), *sl(22), *sl(22))
        cmul(*sl(24), *sl(23), *sl(23))
        stt(W1[:, 8, :], SC[:, 0, :, 8], SC[:, 0, :, 8], ALU.mult)
        stt(W1[:, 9, :], SC[:, 1, :, 8], SC[:, 1, :, 8], ALU.mult)
        stt(W1[:, 8, :], W1[:, 8, :], W1[:, 9, :], ALU.add)
        k.pool(lambda e: e.tensor_tensor(out=W1[:, 9, :], in0=W1[:, 8, :], in1=cneg[:, 0:1].broadcast_to([128, 48]),
                                         op=ALU.pow), R=[wb, cneg.b[0]], W=[wb])
        stt(SC[:, 0, :, 25], W1[:, 8, :], W1[:, 9, :], ALU.mult)
        k.dve(lambda e: e.memset(SC[:, 1, :, 25], 0.0), W=[scb])
        k.dve(lambda e: e.memset(P8[:, 0, :, 0], 1.0), W=[P8.b[0], scb])
        k.dve(lambda e: e.memset(P8[:, 1, :, 0], 0.0), W=[P8.b[0], scb])
        k.dve(lambda e: e.memset(P8[:, 0, :, 16], 1.0), W=[P8.b[0], scb])
        k.dve(lambda e: e.memset(P8[:, 1, :, 16], 0.0), W=[P8.b[0], scb])
        stt(P8[:, 0, :, 1], SC[:, 0, :, 8], W1[:, 9, :], ALU.mult)
        stt(W1[:, 10, :], SC[:, 1, :, 8], W1[:, 9, :], ALU.mult)
        k.dve(lambda e: e.tensor_scalar(out=P8[:, 1, :, 1], in0=W1[:, 10, :], scalar1=-1.0, scalar2=None, op0=ALU.mult),
              R=[wb], W=[P8.b[0], scb])
        vcmul(P8, 2, P8, 1, 1, P8, 1)
        vcmul(P8, 3, P8, 1, 2, P8, 2)
        vcmul(P8, 5, P8, 1, 4, P8, 4)
        vcmul(P8, 9, P8, 1, 7, P8, 8)
        vcmul(P8, 17, P8, 8, 1, P8, 8)
        vcmul(P8, 18, P8, 17, 1, P8, 17)
        vcmul(P8, 19, P8, 17, 2, P8, 18)
        vcmul(P8, 21, P8, 17, 4, P8, 20)
        vcmul(P8, 25, P8, 17, 7, P8, 24)
        k.release(W2)

        def cinv(dn, sn):
            xr, xi = sl(sn)
            stt(W1[:, 8, :], xr, xr, ALU.mult)
            stt(W1[:, 9, :], xi, xi, ALU.mult)
            stt(W1[:, 8, :], W1[:, 8, :], W1[:, 9, :], ALU.add)
            k.dve(lambda e: e.reciprocal(out=W1[:, 9, :], in_=W1[:, 8, :]), R=[wb], W=[wb])
            stt(SC[:, 0, :, dn], xr, W1[:, 9, :], ALU.mult)
            stt(W1[:, 10, :], xi, W1[:, 9, :], ALU.mult)
            k.dve(lambda e: e.tensor_scalar(out=SC[:, 1, :, dn], in0=W1[:, 10, :], scalar1=-1.0, scalar2=None,
                                            op0=ALU.mult), R=[wb], W=[scb])

        cinv(17, 4)
        cinv(18, 7)
        PR = k.alloc("sPR", [128, 2, 48, 8], F32)
        for jp in range(8):
            for ri in range(2):
                k.dve(lambda e, jp=jp, ri=ri: e.tensor_copy(out=PR[:, ri, :, jp], in_=SC[:, ri, :, 7 - jp]),
                      R=[scb], W=[PR.b[0]])
        k.dve(lambda e: e.tensor_scalar(out=W1[:, 0, :], in0=SC[:, 0, :, 1], scalar1=-1.0, scalar2=None, op0=ALU.add),
              R=[scb], W=[wb])
        stt(W1[:, 8, :], L[:, 0, :], L[:, 0, :], ALU.mult)
        stt(W1[:, 9, :], L[:, 1, :], L[:, 1, :], ALU.mult)
        stt(W1[:, 8, :], W1[:, 8, :], W1[:, 9, :], ALU.add)
        k.dve(lambda e: e.reciprocal(out=W1[:, 1, :], in_=W1[:, 8, :]), R=[wb], W=[wb])
        stt(W1[:, 8, :], W1[:, 0, :], L[:, 0, :], ALU.mult)
        stt(W1[:, 9, :], SC[:, 1, :, 1], L[:, 1, :], ALU.mult)
        stt(W1[:, 8, :], W1[:, 8, :], W1[:, 9, :], ALU.add)
        stt(W1[:, 6, :], W1[:, 8, :], W1[:, 1, :], ALU.mult)
        stt(W1[:, 8, :], SC[:, 1, :, 1], L[:, 0, :], ALU.mult)
        stt(W1[:, 9, :], W1[:, 0, :], L[:, 1, :], ALU.mult)
        stt(W1[:, 8, :], W1[:, 8, :], W1[:, 9, :], ALU.subtract)
        stt(W1[:, 7, :], W1[:, 8, :], W1[:, 1, :], ALU.mult)
        if stop <= 1:
            return SC, P8

        Bb = k.alloc("sBb", [128, 2, 48, 16], F32)
        Cm = k.alloc("sCm", [128, 2, 48, 16], F32)
        T2 = k.alloc("sT2", [128, 4, 48, 16], F32)

        def bc16(ap):
            return ap.unsqueeze(2).broadcast_to([128, 48, 16])

        def cmul16(dst, X, yr, yi, neg_im=False):
            Rr = [wb, scb, X.b[0], T2.b[0], dst.b[0]]
            ops = ((0, X[:, 0], yr), (1, X[:, 1], yi), (2, X[:, 0], yi), (3, X[:, 1], yr))
            for i_, a_, b_ in ops:
                k.dve(lambda e, i_=i_, a_=a_, b_=b_: e.tensor_tensor(out=T2[:, i_], in0=a_, in1=bc16(b_), op=ALU.mult),
                      R=Rr, W=[T2.b[0]])
            k.dve(lambda e: e.tensor_tensor(out=dst[:, 0], in0=T2[:, 0], in1=T2[:, 1], op=ALU.subtract),
                  R=Rr, W=[dst.b[0]])
            k.dve(lambda e: e.tensor_tensor(out=dst[:, 1], in0=T2[:, 2], in1=T2[:, 3], op=ALU.add),
                  R=Rr, W=[dst.b[0]])

        cmul16(Bb, B2, W1[:, 6, :], W1[:, 7, :])
        cmul16(Cm, C2, SC[:, 0, :, 18], SC[:, 1, :, 18])
        k.release(T2)
        if stop <= 2:
            return SC, P8

        if after_tables is not None:
            after_tables()
        SPB = 4
        SGB = 2 * SPB
        NSB = 48 // SPB

        class SB_:
            pass

        def stage_a(bl):
            Bk = SB_()
            Bk.bl = bl
            ps_ = slice(bl * SPB, (bl + 1) * SPB)
            X = k.alloc("sX", [128, 6, SPB, 8, 16], F32, nb=3)
            T4 = k.alloc("sT4", [128, 4, SPB, 8, 16], F32)
            T4p = k.alloc("sT4p", [128, 4, SPB, 8, 16], F32)
            xb = X.b[0]
            xbs = [X.b[0], X.b[0], X.b[2], X.b[2]]
            shp = [128, SPB, 8, 16]

            def expand(dr, di, M, pw_lo, reverse, neg_im, eng=None, TT_=None, xb=None):
                eng = eng or k.dve
                TT_ = TT_ or T4
                xb = xb or X.b[0]
                if reverse:
                    pr = PR[:, 0, ps_, :]
                    pi = PR[:, 1, ps_, :]
                else:
                    pr = SC[:, 0, ps_, pw_lo:pw_lo + 8]
                    pi = SC[:, 1, ps_, pw_lo:pw_lo + 8]
                pr = pr.unsqueeze(3).broadcast_to(shp)
                pi = pi.unsqueeze(3).broadcast_to(shp)
                mr = M[:, 0, ps_, :].unsqueeze(2).broadcast_to(shp)
                mi = M[:, 1, ps_, :].unsqueeze(2).broadcast_to(shp)
                Rr = [scb, M.b[0], TT_.b[0], xb, PR.b[0]]
                for i_, a_, b_ in ((0, mr, pr), (1, mi, pi), (2, mr, pi), (3, mi, pr)):
                    eng(lambda e, i_=i_, a_=a_, b_=b_: e.tensor_tensor(out=TT_[:, i_], in0=a_, in1=b_, op=ALU.mult),
                        R=Rr, W=[TT_.b[0]])
                eng(lambda e: e.tensor_tensor(out=X[:, dr], in0=TT_[:, 0], in1=TT_[:, 1], op=ALU.subtract),
                    R=Rr, W=[xb])
                if neg_im:
                    k.dve(lambda e: e.scalar_tensor_tensor(out=X[:, di].rearrange("p a b c -> p (a b c)"),
                                                           in0=TT_[:, 2].rearrange("p a b c -> p (a b c)"), scalar=-1.0,
                                                           in1=TT_[:, 3].rearrange("p a b c -> p (a b c)"),
                                                           op0=ALU.mult, op1=ALU.subtract), R=Rr, W=[xb])
                else:
                    eng(lambda e: e.tensor_tensor(out=X[:, di], in0=TT_[:, 2], in1=TT_[:, 3], op=ALU.add),
                        R=Rr, W=[xb])

            expand(4, 5, C2, 1, False, False, eng=k.pool, TT_=T4p, xb=X.b[1])
            expand(0, 1, Bb, 0, True, False)
            expand(2, 3, Cm, 0, False, False, xb=X.b[2])
            k.release(T4p)
            cxb = k.alloc("scxb", [128, SPB, 2, 128], BF16)
            for ri in range(2):
                k.act(lambda e, ri=ri: e.activation(out=cxb[:, :, ri, :],
                                                    in_=X[:, 4 + ri].rearrange("p a b c -> p a (b c)"),
                                                    func=AF.Copy, scale=(1.0 if ri == 0 else -1.0)),
                      R=[X.b[1]], W=[cxb.b[0]])
            k.dma("pool", CX_d[bl * SPB:(bl + 1) * SPB].rearrange("q r x c -> r q x c"), cxb[:], R=[cxb.b[0]], W=[scrbuf])
            k.release(cxb)
            Xf = [X[:, i_].rearrange("p a b c -> p a (b c)") for i_ in range(6)]
            XH = k.alloc("sXH", [128, 4, SPB, 128], BF16)
            XL = k.alloc("sXL", [128, 4, SPB, 128], BF16)
            for i_ in range(4):
                sc_ = -1.0 if i_ == 3 else 1.0
                k.act(lambda e, i_=i_, sc_=sc_: e.activation(out=XH[:, i_], in_=Xf[i_], func=AF.Copy, scale=sc_),
                      R=[xbs[i_]], W=[XH.b[0]])
                k.dve(lambda e, i_=i_: e.tensor_tensor(out=T4[:, i_].rearrange("p a b c -> p a (b c)"), in0=Xf[i_],
                                                       in1=XH[:, i_], op=(ALU.add if i_ == 3 else ALU.subtract)),
                      R=[xbs[i_], XH.b[0]], W=[T4.b[0]])
                k.act(lambda e, i_=i_, sc_=sc_: e.activation(out=XL[:, i_], in_=T4[:, i_].rearrange("p a b c -> p a (b c)"),
                                                             func=AF.Copy, scale=sc_),
                      R=[T4.b[0]], W=[XL.b[0]])
            k.release(X)
            k.release(T4)
            Bk.XH, Bk.XL = XH, XL
            return Bk

        def stage_b(Bk):
            bl, XH, XL = Bk.bl, Bk.XH, Bk.XL
            ttb = k.alloc("sttb", [128, SGB, 128], BF16)
            bxb = k.alloc("sbxb", [128, SGB, 2, 64], BF16)
            tmps = []
            for pp in range(SPB):
                for g2 in range(2):
                    gl = pp * 2 + g2
                    g = bl * SGB + gl
                    rows = slice(64 * g2, 64 * g2 + 64)
                    bk = k.bank()
                    combos = ((XH, 0, XH, 2), (XH, 0, XL, 2), (XL, 0, XH, 2), (XH, 1, XH, 3), (XH, 1, XL, 3), (XL, 1, XH, 3))
                    mm_group(bk.t[:, 0:128], bk, [(A_[rows, ia, pp, :], B_[rows, ib, pp, :]) for (A_, ia, B_, ib) in combos],
                             R=[XH.b[0], XL.b[0]])
                    tmp = k.alloc("sttmp", [128, 128], F32)
                    tmps.append(tmp)
                    k.dve(lambda e, bk=bk, tmp=tmp: e.tensor_tensor(out=tmp[:, :], in0=bk.t[:, 0:128], in1=mJ[:, :], op=ALU.mult),
                          R=[bk.b[0], mJ.b[0]], W=[tmp.b[0]])
                    k.dve(lambda e, g=g, gl=gl, tmp=tmp: e.scalar_tensor_tensor(out=ttb[:, gl, :], in0=idf[:, :],
                                                                                scalar=dcol[:, g:g + 1], in1=tmp[:, :],
                                                                                op0=ALU.mult, op1=ALU.add),
                          R=[tmp.b[0], idf.b[0], dcol.b[0]], W=[ttb.b[0]])
                    bkt = k.bank()
                    pvb = bkt.t.bitcast(BF16)
                    for ri in range(2):
                        k.pe(lambda e, pvb=pvb, pp=pp, rows=rows, ri=ri: e.transpose(
                            out=pvb[:, ri * 64:(ri + 1) * 64], in_=XH[rows, ri, pp, :],
                            identity=idb[rows, rows]), R=[XH.b[0], idb.b[0]], W=[bkt.b[0]], inc=(ri == 1))
                    k.act(lambda e, pvb=pvb, bkt=bkt, gl=gl: e.copy(out=bxb[:, gl, :, :].rearrange("p x q -> p (x q)"),
                                                                    in_=pvb[:, 0:128]), R=[bkt.b[0]], W=[bxb.b[0]])
                    if len(tmps) > 2:
                        k.release(tmps.pop(0))
            for t_ in tmps:
                k.release(t_)
            k.dma("pool", TT_d[bl * SGB:(bl + 1) * SGB].rearrange("g r c -> r g c"), ttb[:], R=[ttb.b[0]], W=[scrbuf])
            k.dma("pool", BX_d[bl * SGB:(bl + 1) * SGB].rearrange("g r x p -> r g x p"), bxb[:], R=[bxb.b[0]], W=[scrbuf])
            k.release(ttb)
            k.release(bxb)
            k.release(XH)
            k.release(XL)

        if stop > 3:
            cur = stage_a(0)
            for bl in range(NSB):
                nxt = stage_a(bl + 1) if bl + 1 < NSB else None
                stage_b(cur)
                cur = nxt
        k.dma("pool", SC_d, SC[:], R=[scb], W=[scrbuf])
        k.dma("pool", P8_d, P8[:], R=[P8.b[0]], W=[scrbuf])
        for t_ in (L, LD, B2, C2, dcol, mJ, W1, Bb, Cm, PR, SC, P8):
            k.release(t_)
        return None, None

    def layer0(Wa_pre=None, kv_pre=None):
        if kv_pre:
            KT, Vb = kv_pre
        else:
            gb_mem0 = load_gb(4)
            KT, Vb = mem_kv(0, gb_mem0)
            k.release(gb_mem0)
        Wa = Wa_pre if Wa_pre is not None else load_w("Wa", w_in_a, D, INA)
        Wo = load_w("Wo", w_out[0], MIXW, D)
        gb_pre = load_gb(0)
        gb_post = load_gb(2)
        lng = k.alloc("lng", [128, BR], F32)
        lnb = k.alloc("lnb", [128, BR], F32)
        k.dma("sp", lng[:], lnv[0:1, :].broadcast_to([128, BR]), W=[lng.b[0]])
        k.dma("sp", lnb[:], lnv[1:2, :].broadcast_to([128, BR]), W=[lnb.b[0]])
        wtmp = k.alloc("wtmp", [128, 8, 128], F32)
        msk = k.alloc("msk", [128, 128], F32)
        k.dma("sp", wtmp[:], wsT_d, W=[wtmp.b[0]])
        k.dma("sp", msk[:], trilT_d, W=[msk.b[0]])
        wsT = k.alloc("wsT", [128, 8, 128], BF16)
        for g in range(8):
            k.dve(lambda e, g=g: e.tensor_tensor(out=wsT[:, g, :], in0=wtmp[:, g, :], in1=msk[:, :], op=ALU.mult),
                  R=[wtmp.b[0], msk.b[0]], W=[wsT.b[0]])
        k.release(wtmp)
        k.release(msk)
        wtmp2 = k.alloc("wtmp2", [64, 8, 64], F32)
        msk2 = k.alloc("msk2", [64, 64], F32)
        k.dma("sp", wtmp2[:], wsamp_d, W=[wtmp2.b[0]])
        k.dma("sp", msk2[:], mask_s_d, W=[msk2.b[0]])
        wsS = k.alloc("wsS", [64, 8, 64], BF16)
        for g in range(8):
            k.dve(lambda e, g=g: e.tensor_tensor(out=wsS[:, g, :], in0=wtmp2[:, g, :], in1=msk2[:, :], op=ALU.mult),
                  R=[wtmp2.b[0], msk2.b[0]], W=[wsS.b[0]])
        k.release(wtmp2)
        k.release(msk2)
        bsT = k.alloc("bsT", [128, 8], F32)
        bsS = k.alloc("bsS", [64, 8], F32)
        k.dma("sp", bsT[:], bsT_d, W=[bsT.b[0]])
        k.dma("sp", bsS[:], bs_s_d, W=[bsS.b[0]])

        fsets = FrontSets(2)

        def front(xsrc, P):
            return fsets.front(xsrc, P, gb_pre)

        def tile(fr, P, sample, dst, next_front):
            xt, hT = fr
            RW = [hT.b[0], Wa.b[0]]
            u = k.alloc("u", [128, BR], F32)
            v = k.alloc("v", [128, BR], F32)
            gate = k.alloc("gate", [128, MIXW if not sample else BR], F32)
            qT = k.alloc("qT", [128, 4, 128], BF16)
            gT = k.alloc("gT", [128, 4, 128], F32) if sample else None

            def tm_blocks(dstt, c0, fn, nblk=3):
                for cb in range(nblk):
                    bk = k.bank()
                    mm_group(bk.t[0:P, :], bk, [(hT[:, kc, 0:P], Wa[:, kc, c0 + cb * 512:c0 + (cb + 1) * 512])
                                                for kc in range(8)], R=RW)
                    k.act(lambda e, bk=bk, cb=cb: e.activation(out=dstt[0:P, cb * 512:(cb + 1) * 512],
                                                               in_=bk.t[0:P, :], func=fn),
                          R=[bk.b[0]], W=[dstt.b[0]])

            def fm_block(dstt, c0, fn):
                bk = k.bank()
                for hd in range(4):
                    mm_group(bk.t[:, hd * 128:hd * 128 + P], bk,
                             [(Wa[:, kc, c0 + hd * 128:c0 + (hd + 1) * 128], hT[:, kc, 0:P]) for kc in range(8)],
                             R=RW, last_inc=(hd == 3))
                k.act(lambda e, bk=bk: e.activation(out=dstt[:, :, 0:P],
                                                    in_=bk.t.rearrange("p (h t) -> p h t", h=4)[:, :, 0:P], func=fn),
                      R=[bk.b[0]], W=[dstt.b[0]])

            tm_blocks(v, BR, AF.Gelu_apprx_tanh)
            st = k.alloc("lnst", [128, 32], F32)
            for cb in range(3):
                k.dve(lambda e, cb=cb: e.bn_stats(out=st[0:P, cb * 6:(cb + 1) * 6], in_=v[0:P, cb * 512:(cb + 1) * 512]),
                      R=[v.b[0]], W=[st.b[0]])
            k.dve(lambda e: e.bn_aggr(out=st[0:P, 18:20], in_=st[0:P, 0:18]), R=[st.b[0]], W=[st.b[0]])
            k.dve(lambda e: e.tensor_scalar(out=st[0:P, 20:21], in0=st[0:P, 19:20], scalar1=EPS, scalar2=None,
                                            op0=ALU.add), R=[st.b[0]], W=[st.b[0]])
            rstd_from_ms(st, P, 20, 21)
            k.dve(lambda e: e.scalar_tensor_tensor(out=st[0:P, 22:23], in0=st[0:P, 18:19], scalar=-1.0,
                                                   in1=st[0:P, 21:22], op0=ALU.mult, op1=ALU.mult),
                  R=[st.b[0]], W=[st.b[0]])
            tm_blocks(u, 0, AF.Gelu_apprx_tanh)
            fm_block(qT, 2 * BR, AF.Copy)
            k.act(lambda e: e.activation(out=v[0:P, :], in_=v[0:P, :], func=AF.Identity, scale=st[0:P, 21:22],
                                         bias=st[0:P, 22:23]), R=[v.b[0], st.b[0]], W=[v.b[0]])
            k.dve(lambda e: e.tensor_tensor(out=v[0:P, :], in0=v[0:P, :], in1=lng[0:P, :], op=ALU.mult),
                  R=[v.b[0], lng.b[0]], W=[v.b[0]])
            vnb = k.alloc("vnb", [128, BR], BF16)
            if sample:
                k.dve(lambda e: e.tensor_tensor(out=v[0:P, :], in0=v[0:P, :], in1=lnb[0:P, :], op=ALU.add),
                      R=[v.b[0], lnb.b[0]], W=[v.b[0]])
                k.dma("pool", o_vs, v[0:P, :], R=[v.b[0]])
                k.dve(lambda e: e.tensor_copy(out=vnb[0:P, :], in_=v[0:P, :]), R=[v.b[0]], W=[vnb.b[0]])
            else:
                k.dve(lambda e: e.tensor_tensor(out=vnb[0:P, :], in0=v[0:P, :], in1=lnb[0:P, :], op=ALU.add),
                      R=[v.b[0], lnb.b[0]], W=[vnb.b[0]])
            k.release(st)
            k.release(v)
            mixT = k.alloc("mixT", [128, 16, 128], BF16)
            mix = k.alloc("mix", [128, BR if sample else MIXW], BF16)
            if sample:
                tm_blocks(gate, 2 * BR + XATT, AF.Silu)
                fm_block(gT, 2 * BR + XATT + BR, AF.Silu)
                k.release(hT)
                attn_sample(0, qT, gT, mixT)
                k.release(gT)
            else:
                tm_blocks(gate, 2 * BR + XATT, AF.Silu, nblk=4)
                k.release(hT)
                attn_prompt(P, KT, Vb, qT, gate, BR, mix)
            k.release(qT)
            k.dve(lambda e: e.tensor_tensor(out=u[0:P, :], in0=u[0:P, :], in1=gate[0:P, 0:BR], op=ALU.mult),
                  R=[u.b[0], gate.b[0]], W=[u.b[0]])
            k.release(gate)
            ws = wsS if sample else wsT
            bs = bsS if sample else bsT
            for gp in range(4):
                bk = k.bank()
                for gg in range(2):
                    g = gp * 2 + gg
                    k.pe(lambda e, bk=bk, g=g, gg=gg: e.matmul(bk.t[0:P, gg * 192:(gg + 1) * 192], lhsT=ws[0:P, g, 0:P],
                                                               rhs=vnb[0:P, g * 192:(g + 1) * 192], start=True, stop=True),
                         R=[ws.b[0], vnb.b[0]], W=[bk.b[0]], inc=(gg == 1))
                for gg in range(2):
                    g = gp * 2 + gg
                    k.dve(lambda e, bk=bk, g=g, gg=gg: e.scalar_tensor_tensor(out=mix[0:P, g * 192:(g + 1) * 192],
                                                                              in0=bk.t[0:P, gg * 192:(gg + 1) * 192],
                                                                              scalar=bs[0:P, g:g + 1],
                                                                              in1=u[0:P, g * 192:(g + 1) * 192],
                                                                              op0=ALU.add, op1=ALU.mult),
                          R=[bk.b[0], bs.b[0], u.b[0]], W=[mix.b[0]])
            k.release(vnb)
            k.release(u)
            nf = next_front() if next_front is not None else None
            transpose_to(mixT, mixT.b[0], mix, mix.b[0], P, 12 if sample else 16)
            k.release(mix)
            bks = [k.bank(), k.bank()]
            for cb in range(2):
                mm_group(bks[cb].t[0:P, :], bks[cb], [(mixT[:, kc, 0:P], Wo[:, kc, cb * 512:(cb + 1) * 512])
                                                     for kc in range(16)], R=[mixT.b[0], Wo.b[0]])
            k.release(mixT)
            post_and_store(P, bks, xt, gb_post, dst[0], dst[1], wbuf=x1buf)
            k.release(xt)
            return nf

        srcs = [(xp[t * 128:(t + 1) * 128, :], 128, False, (x1d[t * 128:(t + 1) * 128, :], None)) for t in range(NT)]
        srcs.append((xs[:, :], NS, True, (x1d[SEQ:SEQ + NS, :], None)))
        fr = front(srcs[0][0], srcs[0][1])
        for i, (src, P, smp, dst) in enumerate(srcs):
            nxt = (lambda i=i: front(srcs[i + 1][0], srcs[i + 1][1])) if i + 1 < len(srcs) else None
            fr = tile(fr, P, smp, dst, nxt)
        fsets.free()
        for t_ in (KT, Vb, Wa, Wo, gb_pre, gb_post, lng, lnb, wsT, wsS, bsT, bsS):
            k.release(t_)


    def layer1(SC, P8):
        SC = k.alloc("SC", [128, 2, 48, 26], F32)
        P8 = k.alloc("P8", [128, 2, 48, 32], F32)
        k.dma("sp", SC[:], SC_d, R=[scrbuf], W=[SC.b[0]])
        k.dma("sp", P8[:], P8_d, R=[scrbuf], W=[P8.b[0]])
        gb_mem1 = load_gb(5)
        KT, Vb = mem_kv(1, gb_mem1)
        k.release(gb_mem1)
        gb_pre = load_gb(1)
        gb_post = load_gb(3)
        UG = k.alloc("UG", [128, 2, 96, 128], BF16, nb=2)
        USG = k.alloc("USG", [NSEQ, 96, 64], BF16)
        HL = k.alloc("HL", [128, 2, 48], F32)
        scb = SC.b[0]

        def x1_rows(kt, j):
            return x1d[kt * 1024 + j:(kt + 1) * 1024:8, :]

        fsets = FrontSets(2)

        def load_norm_T(src, P):
            return fsets.front(src, P, gb_pre, xdeps=[x1buf])

        Wu = load_w("Wu", w_in_b, D, INB, c0=0, c1=BR)

        def a0_tile(fr, P, out_fn, in_fn, dbuf, next_front):
            xt, hT = fr
            k.release(xt)
            nf = next_front() if next_front is not None else None
            for cb in range(3):
                bk = k.bank()
                mm_group(bk.t[0:P, :], bk, [(hT[:, kc, 0:P], Wu[:, kc, cb * 512:(cb + 1) * 512]) for kc in range(8)],
                         R=[hT.b[0], Wu.b[0]])
                k.act(lambda e, bk=bk, cb=cb: e.copy(out=out_fn(cb), in_=in_fn(bk)), R=[bk.b[0]], W=[dbuf])
            k.release(hT)
            return nf

        UG5 = UG.t.rearrange("p t g (j c) -> p t g j c", c=16)
        us = k.alloc("us", [NS, BR], BF16)
        a0src = []
        for kt in range(2):
            for j in range(8):
                a0src.append((x1_rows(kt, j), 128,
                              (lambda cb, kt=kt, j=j: UG5[:, kt, cb * 32:(cb + 1) * 32, j, :]),
                              (lambda bk: bk.t[:, :].rearrange("p (g c) -> p g c", c=16)), UG.b[kt]))
        a0src.append((x1d[SEQ:SEQ + NS, :], NS, (lambda cb: us[0:NS, cb * 512:(cb + 1) * 512]),
                      (lambda bk: bk.t[0:NS, :]), us.b[0]))
        fr = load_norm_T(a0src[0][0], a0src[0][1])
        for i, (src, P, ofn, ifn, dbuf) in enumerate(a0src):
            nxt = (lambda i=i: load_norm_T(a0src[i + 1][0], a0src[i + 1][1])) if i + 1 < len(a0src) else None
            fr = a0_tile(fr, P, ofn, ifn, dbuf, nxt)
        uyS = k.alloc("uyS", [NSEQ, 4, BR], BF16)
        for j in range(4):
            k.dma("sp", uyS[0:NSEQ, j, :], us[16 * j:16 * j + 16, :], R=[us.b[0]], W=[uyS.b[0]])
        k.release(us)
        USG4 = USG.t.rearrange("p g (j c) -> p g j c", c=16)
        for j in range(4):
            k.pool(lambda e, j=j: e.tensor_copy(out=USG4[:, :, j, :], in_=uyS[0:NSEQ, j, :].rearrange("p (g c) -> p g c", c=16)),
                   R=[uyS.b[0]], W=[USG.b[0]])
        k.release(uyS)
        k.release(Wu)

        NPc = 256
        NPB = 4
        NGB = 2 * NPB
        NBLK = 48 // NPB

        def bc(ap, shape, axis):
            return ap.unsqueeze(axis).broadcast_to(shape)

        def gen_E(bl):
            gps_ = slice(bl * NPB, (bl + 1) * NPB)
            Eb_ = k.alloc("Eb", [128, 2, NPB, 256], F32)
            TP = k.alloc("TP", [128, 2, NPB, 256], F32)
            Rp = [Eb_.b[0], TP.b[0], P8.b[0]]
            E4 = [Eb_[:, ri].rearrange("p a (c i) -> p a c i", i=16) for ri in range(2)]
            Tv = [TP[:, q].rearrange("p a (c i) -> p a c i", i=16) for q in range(2)]
            Gr = bc(P8[:, 0, gps_, 16:32], [128, NPB, 16, 16], 3)
            Gi = bc(P8[:, 1, gps_, 16:32], [128, NPB, 16, 16], 3)
            Fr = bc(P8[:, 0, gps_, 0:16], [128, NPB, 16, 16], 2)
            Fi = bc(P8[:, 1, gps_, 0:16], [128, NPB, 16, 16], 2)

            def pv(out, a_, b_, op, W):
                k.pool(lambda e: e.tensor_tensor(out=out, in0=a_, in1=b_, op=op), R=Rp, W=W)

            pv(Tv[0], Gr, Fr, ALU.mult, [TP.b[0]])
            pv(Tv[1], Gi, Fi, ALU.mult, [TP.b[0]])
            pv(E4[0], Tv[0], Tv[1], ALU.subtract, [Eb_.b[0]])
            pv(Tv[0], Gr, Fi, ALU.mult, [TP.b[0]])
            pv(Tv[1], Gi, Fr, ALU.mult, [TP.b[0]])
            pv(E4[1], Tv[0], Tv[1], ALU.add, [Eb_.b[0]])
            k.release(TP)
            return Eb_

        class Blk:
            pass

        def s1(bl):
            B = Blk()
            B.bl = bl
            B.TTb = k.alloc("TTb", [128, NGB, 128], BF16)
            B.BXb = k.alloc("BXb", [128, NGB, 2, 64], BF16)
            B.CXb = k.alloc("CXb", [128, NPB, 2, 128], BF16)
            k.dma("sp", B.TTb[:], TT_d[bl * NGB:(bl + 1) * NGB].rearrange("g r c -> r g c"), R=[scrbuf], W=[B.TTb.b[0]])
            k.dma("sp", B.BXb[:], BX_d[bl * NGB:(bl + 1) * NGB].rearrange("g r x p -> r g x p"), R=[scrbuf], W=[B.BXb.b[0]])
            k.dma("sp", B.CXb[:], CX_d[bl * NPB:(bl + 1) * NPB].rearrange("q r x c -> r q x c"), R=[scrbuf], W=[B.CXb.b[0]])
            stS = k.alloc("stS", [NSEQ, 2, NPB * 128], F32)
            k.dma("sp", stS[:], st_d[:, :, bl * NPB * 128:(bl + 1) * NPB * 128], W=[stS.b[0]])
            B.Ut = k.alloc("Ut", [128, NGB, 272], BF16)
            Ut = B.Ut
            k.pool(lambda e: e.memset(Ut[64:128, :, 256:272], 0.0), W=[Ut.b[0]])
            for g4 in range(NGB // 4):
                bk = k.bank()
                pv = bk.t.bitcast(BF16)
                for gg in range(4):
                    g = bl * NGB + g4 * 4 + gg
                    for kt in range(2):
                        c = gg * 2 + kt
                        k.pe(lambda e, c=c, g=g, kt=kt, pv=pv: e.transpose(out=pv[:, c * 128:(c + 1) * 128],
                                                                           in_=UG[:, kt, g, :], identity=idb[:, :]),
                             R=[UG.b[kt], idb.b[0]], W=[bk.b[0]], inc=(c == 7))
                k.act(lambda e, g4=g4, pv=pv: e.copy(out=Ut[:, g4 * 4:(g4 + 1) * 4, 0:NPc],
                                                     in_=pv.rearrange("p (g t) -> p g t", g=4)),
                      R=[bk.b[0]], W=[Ut.b[0]])
            bk = k.bank()
            pv = bk.t.bitcast(BF16)
            for gl in range(NGB):
                g = bl * NGB + gl
                k.pe(lambda e, gl=gl, g=g, pv=pv: e.transpose(out=pv[0:64, gl * 16:(gl + 1) * 16],
                                                              in_=USG[0:NSEQ, g, :], identity=idb[0:NSEQ, 0:NSEQ]),
                     R=[USG.b[0], idb.b[0]], W=[bk.b[0]], inc=(gl == NGB - 1))
            k.dve(lambda e, pv=pv: e.tensor_copy(out=Ut[0:64, :, NPc:NPc + 16],
                                                 in_=pv[0:64, 0:NGB * 16].rearrange("p (g t) -> p g t", g=NGB)),
                  R=[bk.b[0]], W=[Ut.b[0]])
            bk = k.bank()
            for ri in range(2):
                for pp in range(NPB):
                    c = ri * NPB + pp
                    k.pe(lambda e, c=c, ri=ri, pp=pp, bk=bk: e.transpose(out=bk.t[:, c * 16:(c + 1) * 16],
                                                                         in_=stS[0:NSEQ, ri, pp * 128:(pp + 1) * 128],
                                                                         identity=idf[0:NSEQ, 0:NSEQ]),
                         R=[stS.b[0], idf.b[0]], W=[bk.b[0]], inc=(c == 2 * NPB - 1))
            B.H0 = k.alloc("H0", [128, 2, NPB, 16], F32)
            H0 = B.H0
            k.dve(lambda e, bk=bk: e.tensor_copy(out=H0[:].rearrange("p a b c -> p (a b c)"), in_=bk.t[:, 0:2 * NPB * 16]),
                  R=[bk.b[0]], W=[H0.b[0]])
            k.release(stS)
            B.S = k.alloc("S", [128, 2, NPB, 272], F32)
            S = B.S
            for pp in range(NPB):
                bks = [k.bank(), k.bank()]
                for ri in range(2):
                    for g2 in range(2):
                        gl = pp * 2 + g2
                        k.pe(lambda e, ri=ri, g2=g2, gl=gl, bks=bks: e.matmul(bks[ri].t[64 * g2:64 * g2 + 64, 0:272],
                                                                              lhsT=B.BXb[:, gl, ri, :], rhs=Ut[:, gl, :],
                                                                              start=True, stop=True),
                             R=[B.BXb.b[0], Ut.b[0]], W=[bks[ri].b[0]], inc=(g2 == 1))
                    k.act(lambda e, ri=ri, pp=pp, bks=bks: e.copy(out=S[:, ri, pp, :], in_=bks[ri].t[:, 0:272]),
                          R=[bks[ri].b[0]], W=[S.b[0]])
            return B

        def s2(B, Eb):
            S = B.S
            sb_ = S.b[0]
            gps = slice(B.bl * NPB, (B.bl + 1) * NPB)
            Tq = k.alloc("Tq", [128, 3, NPB, 256], F32)
            eb, tq = Eb.b[0], Tq.b[0]
            Rr = [sb_, eb, tq, scb, P8.b[0]]

            def dv(out, a_, b_, op, W):
                k.dve(lambda e: e.tensor_tensor(out=out, in0=a_, in1=b_, op=op), R=Rr, W=W)

            Sr = S[:, 0, :, 0:NPc]
            Si = S[:, 1, :, 0:NPc]
            Er, Ei = Eb[:, 0], Eb[:, 1]
            t0, t1, t2 = Tq[:, 0], Tq[:, 1], Tq[:, 2]
            dv(t0, Sr, Er, ALU.mult, [tq])
            dv(t1, Si, Ei, ALU.mult, [tq])
            dv(t2, Sr, Ei, ALU.mult, [tq])
            dv(Sr, t0, t1, ALU.subtract, [sb_])
            dv(t0, Si, Er, ALU.mult, [tq])
            dv(Si, t2, t0, ALU.add, [sb_])
            k.dve(lambda e: e.tensor_copy(out=t1, in_=bc(SC[:, 0, gps, 25], [128, NPB, 256], 2)), R=Rr, W=[tq])
            for ri in range(2):
                for q in range(NPB):
                    k.dve(lambda e, ri=ri, q=q: e.tensor_tensor_scan(out=Tq[:, 2 * ri, q, :], data0=t1[:, q, :],
                                                                      data1=S[:, ri, q, 0:NPc],
                                                                      initial=0.0, op0=ALU.mult, op1=ALU.add),
                          R=Rr, W=[tq])
            Wr, Wi = Tq[:, 0], Tq[:, 2]
            dv(t1, Wr, Er, ALU.mult, [tq])
            dv(Sr, Wi, Ei, ALU.mult, [sb_])
            dv(Sr, Sr, t1, ALU.add, [sb_])
            dv(t1, Wi, Er, ALU.mult, [tq])
            dv(Si, Wr, Ei, ALU.mult, [sb_])
            dv(Si, t1, Si, ALU.subtract, [sb_])
            k.release(Eb)
            k.release(Tq)

        def s3prep(B):
            S, H0 = B.S, B.H0
            sb_ = S.b[0]
            bl = B.bl
            ps_ = slice(bl * NPB, (bl + 1) * NPB)
            B.Hin = k.alloc("Hin", [128, 2, NPB, 272], BF16)
            Hin = B.Hin
            for ri in range(2):
                k.pool(lambda e, ri=ri: e.memset(Hin[:, ri, :, 0:1], 0.0), W=[Hin.b[0]])
                k.act(lambda e, ri=ri: e.copy(out=Hin[:, ri, :, 1:NPc], in_=S[:, ri, :, 0:NPc - 1]),
                      R=[sb_], W=[Hin.b[0]])
                k.dve(lambda e, ri=ri: e.tensor_copy(out=Hin[:, ri, :, NPc:NPc + 16], in_=H0[:, ri]),
                      R=[H0.b[0]], W=[Hin.b[0]])
                k.dve(lambda e, ri=ri: e.tensor_copy(out=HL[:, ri, ps_], in_=S[:, ri, :, NPc - 1]),
                      R=[sb_], W=[HL.b[0]])
            Tm = k.alloc("Tm", [128, 4, NPB, 16], F32)
            tb = Tm.b[0]

            def cmac(dr, di, ar, ai, xr, xi):
                tv = [Tm[:, q] for q in range(4)]
                Rr = [sb_, tb, scb, H0.b[0]]
                for q, (a_, x_) in enumerate(((ar, xr), (ai, xi), (ai, xr), (ar, xi))):
                    k.dve(lambda e, q=q, a_=a_, x_=x_: e.tensor_tensor(out=tv[q], in0=a_, in1=x_, op=ALU.mult),
                          R=Rr, W=[tb])
                k.dve(lambda e: e.tensor_tensor(out=tv[0], in0=tv[0], in1=tv[1], op=ALU.subtract), R=Rr, W=[tb])
                k.dve(lambda e: e.tensor_tensor(out=tv[2], in0=tv[2], in1=tv[3], op=ALU.add), R=Rr, W=[tb])
                k.dve(lambda e: e.tensor_tensor(out=dr, in0=dr, in1=tv[0], op=ALU.add), R=Rr, W=[sb_, tb])
                k.dve(lambda e: e.tensor_tensor(out=di, in0=di, in1=tv[2], op=ALU.add), R=Rr, W=[sb_, tb])

            A8r = bc(SC[:, 0, ps_, 8], [128, NPB, 16], 2)
            A8i = bc(SC[:, 1, ps_, 8], [128, NPB, 16], 2)
            Ss = [S[:, ri, :, NPc:NPc + 16] for ri in range(2)]
            cmac(Ss[0], Ss[1], A8r, A8i, H0[:, 0], H0[:, 1])
            HSo = k.alloc("HSo", [128, 2, NPB, 16], F32)
            k.dve(lambda e: e.memset(HSo[:], 0.0), W=[HSo.b[0], sb_])
            cmac(HSo[:, 0], HSo[:, 1], bc(SC[:, 0, ps_, 17], [128, NPB, 16], 2), bc(SC[:, 1, ps_, 17], [128, NPB, 16], 2),
                 Ss[0], Ss[1])
            hso = k.alloc("hso", [NSEQ, 2, NPB * 128], F32)
            for ri in range(2):
                bk = k.bank()
                for q in range(NPB):
                    k.pe(lambda e, q=q, ri=ri, bk=bk: e.transpose(out=bk.t[0:NSEQ, q * 128:(q + 1) * 128],
                                                                  in_=HSo[:, ri, q, :], identity=idf[:, :]),
                         R=[sb_, HSo.b[0], idf.b[0]], W=[bk.b[0]], inc=(q == NPB - 1))
                k.act(lambda e, ri=ri, bk=bk: e.copy(out=hso[0:NSEQ, ri, :], in_=bk.t[0:NSEQ, 0:NPB * 128]),
                      R=[bk.b[0]], W=[hso.b[0]])
            k.dma("pool", o_hs[:, :, bl * NPB * 128:(bl + 1) * NPB * 128], hso[:], R=[hso.b[0]])
            k.release(hso)
            k.release(HSo)
            k.release(Tm)

        def s3y(B):
            bl = B.bl

            def y_stage1(gl):
                pp, g2 = gl // 2, gl % 2
                rows = slice(64 * g2, 64 * g2 + 64)
                bk = k.bank()
                mm_group(bk.t[:, 0:272], bk,
                         [(B.TTb[:, gl, :], B.Ut[:, gl, :]),
                          (B.CXb[rows, pp, 0, :], B.Hin[rows, 0, pp, :]),
                          (B.CXb[rows, pp, 1, :], B.Hin[rows, 1, pp, :])],
                         R=[B.TTb.b[0], B.Ut.b[0], B.CXb.b[0], B.Hin.b[0]])
                Ysb = k.alloc("Ysb", [128, 272], F32)
                k.act(lambda e: e.copy(out=Ysb[:, :], in_=bk.t[:, 0:272]), R=[bk.b[0]], W=[Ysb.b[0]])
                return Ysb

            def y_stage2(gl, Ysb):
                g = bl * NGB + gl
                bk2 = k.bank()
                for kt in range(2):
                    k.pe(lambda e, kt=kt: e.transpose(out=bk2.t[:, kt * 128:(kt + 1) * 128],
                                                      in_=Ysb[:, kt * 128:(kt + 1) * 128], identity=idf[:, :]),
                         R=[Ysb.b[0], idf.b[0]], W=[bk2.b[0]], inc=False)
                k.pe(lambda e: e.transpose(out=bk2.t[0:NSEQ, 256:384], in_=Ysb[:, 256:272], identity=idf[:, :]),
                     R=[Ysb.b[0], idf.b[0]], W=[bk2.b[0]], inc=True)
                k.release(Ysb)
                k.act(lambda e: e.activation(out=UG[:, :, g, :], in_=bk2.t[:, 0:256].rearrange("p (t c) -> p t c", t=2),
                                             func=AF.Gelu_apprx_tanh), R=[bk2.b[0]], W=[UG.b[0], UG.b[1]])
                k.act(lambda e: e.activation(out=USG[0:NSEQ, g, :], in_=bk2.t[0:NSEQ, 256:320], func=AF.Gelu_apprx_tanh),
                      R=[bk2.b[0]], W=[USG.b[0]])

            pend = [y_stage1(0), y_stage1(1)]
            for gl in range(NGB):
                if gl + 2 < NGB:
                    pend.append(y_stage1(gl + 2))
                y_stage2(gl, pend.pop(0))
            for t_ in (B.TTb, B.BXb, B.CXb, B.S, B.Hin, B.Ut, B.H0):
                k.release(t_)

        cur = s1(0)
        Ecur = gen_E(0)
        Enext = gen_E(1)
        s2(cur, Ecur)
        for bl in range(NBLK):
            nxt = s1(bl + 1) if bl + 1 < NBLK else None
            s3prep(cur)
            if nxt is not None:
                En2 = gen_E(bl + 2) if bl + 2 < NBLK else None
                s2(nxt, Enext)
                Enext = En2
            s3y(cur)
            cur = nxt
        bk = k.bank()
        for ri in range(2):
            k.pe(lambda e, ri=ri: e.transpose(out=bk.t[0:48, ri * 128:(ri + 1) * 128], in_=HL[:, ri, :], identity=idf[:, :]),
                 R=[HL.b[0], idf.b[0]], W=[bk.b[0]], inc=(ri == 1))
        hlo = k.alloc("hlo", [48, 2, 128], F32)
        k.act(lambda e: e.copy(out=hlo[:].rearrange("p a b -> p (a b)"), in_=bk.t[0:48, 0:256]), R=[bk.b[0]], W=[hlo.b[0]])
        k.dma("pool", o_hp.rearrange("r q c -> q r c"), hlo[:], R=[hlo.b[0]])
        k.release(hlo)
        k.release(HL)
        k.release(SC)
        k.release(P8)
        Wg = load_w("Wg", w_glu, BR, BR)
        uyP = k.alloc("uyP", [128, 2, 8, BR], BF16, nb=16)
        uyS = k.alloc("uyS", [NSEQ, 4, BR], BF16)
        cnt = 0
        for kt in range(2):
            for j in range(8):
                sel = cnt % 5
                cnt += 1
                if sel in (0, 2):
                    k.act(lambda e, kt=kt, j=j: e.copy(out=uyP[:, kt, j, :].rearrange("p (g c) -> p g c", c=16),
                                                       in_=UG5[:, kt, :, j, :]),
                          R=[UG.b[kt]], W=[uyP.b[kt * 8 + j]])
                else:
                    eng = k.pool if sel == 4 else k.dve
                    eng(lambda e, kt=kt, j=j: e.tensor_copy(out=uyP[:, kt, j, :].rearrange("p (g c) -> p g c", c=16),
                                                            in_=UG5[:, kt, :, j, :]),
                        R=[UG.b[kt]], W=[uyP.b[kt * 8 + j]])
        for j in range(4):
            k.pool(lambda e, j=j: e.tensor_copy(out=uyS[0:NSEQ, j, :].rearrange("p (g c) -> p g c", c=16),
                                                in_=USG4[:, :, j, :]), R=[USG.b[0]], W=[uyS.b[0]])
        k.release(UG)
        k.release(USG)

        Wqg = load_w("Wqg", w_in_b, D, INB, c0=BR, c1=INB)
        bgl = k.alloc("bgl", [1, BR], BF16)
        k.dma("pool", bgl[:], bglu_d, W=[bgl.b[0]])
        ys = k.alloc("ys", [NS, BR], BF16)
        for j in range(4):
            k.dma("sp", ys[16 * j:16 * j + 16, :], uyS[0:NSEQ, j, :], R=[uyS.b[0]], W=[ys.b[0]])

        def glu_front(y_ap, ybuf, P):
            yT = k.alloc("yT", [128, 12, 128], BF16)
            transpose_to(yT, yT.b[0], y_ap, ybuf, P, 12)
            return yT

        def glu_tile(yT, y_ap, ybuf, P, next_front):
            nf = next_front() if next_front is not None else None
            for cb in range(3):
                bk = k.bank()
                pairs = [(yT[:, kc, 0:P], Wg[:, kc, cb * 512:(cb + 1) * 512]) for kc in range(12)]
                pairs.append((ones_b[0:1, 0:P], bgl[0:1, cb * 512:(cb + 1) * 512]))
                mm_group(bk.t[0:P, :], bk, pairs, R=[yT.b[0], Wg.b[0], ones_b.b[0], bgl.b[0]])
                sig = k.alloc("sig", [128, 512], F32)
                k.act(lambda e, bk=bk, sig=sig: e.activation(out=sig[0:P, :], in_=bk.t[0:P, :], func=AF.Sigmoid),
                      R=[bk.b[0]], W=[sig.b[0]])
                k.dve(lambda e, cb=cb, sig=sig: e.tensor_tensor(out=y_ap[0:P, cb * 512:(cb + 1) * 512],
                                                                in0=y_ap[0:P, cb * 512:(cb + 1) * 512],
                                                                in1=sig[0:P, :], op=ALU.mult),
                      R=[sig.b[0], ybuf], W=[ybuf])
                k.release(sig)
            k.release(yT)
            return nf

        gsrc = [(uyP[:, kt, j, :], uyP.b[kt * 8 + j], 128) for kt in range(2) for j in range(8)]
        gsrc.append((ys, ys.b[0], NS))
        yT = glu_front(*gsrc[0])
        for i, (y_ap, ybuf, P) in enumerate(gsrc):
            nxt = (lambda i=i: glu_front(*gsrc[i + 1])) if i + 1 < len(gsrc) else None
            yT = glu_tile(yT, y_ap, ybuf, P, nxt)
        k.release(Wg)
        k.release(bgl)

        Wo = load_w("Wo1", w_out[1], MIXW, D)

        def b_tile(fr, P, sample, br_ap, brbuf, dst, next_front):
            xt, hT = fr
            RW = [hT.b[0], Wqg.b[0]]
            qT = k.alloc("qT", [128, 4, 128], BF16)
            gT = k.alloc("gT", [128, 4, 128], F32) if sample else None
            gate = k.alloc("gate", [128, BR if sample else MIXW], F32)

            def fm_block(dstt, c0, fn):
                bk = k.bank()
                for hd in range(4):
                    mm_group(bk.t[:, hd * 128:hd * 128 + P], bk,
                             [(Wqg[:, kc, c0 + hd * 128:c0 + (hd + 1) * 128], hT[:, kc, 0:P]) for kc in range(8)],
                             R=RW, last_inc=(hd == 3))
                k.act(lambda e, bk=bk: e.activation(out=dstt[:, :, 0:P],
                                                    in_=bk.t.rearrange("p (h t) -> p h t", h=4)[:, :, 0:P], func=fn),
                      R=[bk.b[0]], W=[dstt.b[0]])

            fm_block(qT, 0, AF.Copy)
            if sample:
                fm_block(gT, XATT + BR, AF.Silu)
            for cb in range(3 if sample else 4):
                bk = k.bank()
                mm_group(bk.t[0:P, :], bk, [(hT[:, kc, 0:P], Wqg[:, kc, XATT + cb * 512:XATT + (cb + 1) * 512])
                                            for kc in range(8)], R=RW)
                k.act(lambda e, bk=bk, cb=cb: e.activation(out=gate[0:P, cb * 512:(cb + 1) * 512], in_=bk.t[0:P, :],
                                                          func=AF.Silu), R=[bk.b[0]], W=[gate.b[0]])
            k.release(hT)
            mixT = k.alloc("mixT", [128, 16, 128], BF16)
            mix = k.alloc("mix", [128, BR if sample else MIXW], BF16)
            k.dve(lambda e: e.tensor_tensor(out=mix[0:P, 0:BR], in0=gate[0:P, 0:BR], in1=br_ap, op=ALU.mult),
                  R=[gate.b[0], brbuf], W=[mix.b[0]])
            if sample:
                attn_sample(1, qT, gT, mixT)
                k.release(gT)
            else:
                attn_prompt(P, KT, Vb, qT, gate, BR, mix)
            k.release(gate)
            k.release(qT)
            transpose_to(mixT, mixT.b[0], mix, mix.b[0], P, 12 if sample else 16)
            k.release(mix)
            nf = next_front() if next_front is not None else None
            bks = [k.bank(), k.bank()]
            for cb in range(2):
                mm_group(bks[cb].t[0:P, :], bks[cb], [(mixT[:, kc, 0:P], Wo[:, kc, cb * 512:(cb + 1) * 512])
                                                     for kc in range(16)], R=[mixT.b[0], Wo.b[0]])
            k.release(mixT)
            post_and_store(P, bks, xt, gb_post, dst)
            k.release(xt)
            return nf

        bsrc = []
        for kt in range(2):
            for j in range(8):
                bsrc.append((x1_rows(kt, j), 128, False, uyP[:, kt, j, :], uyP.b[kt * 8 + j],
                             o_yp[kt * 1024 + j:(kt + 1) * 1024:8, :]))
        bsrc.append((x1d[SEQ:SEQ + NS, :], NS, True, ys[0:NS, :], ys.b[0], o_ys[:, :]))
        fr = load_norm_T(bsrc[0][0], bsrc[0][1])
        for i, (src, P, smp, br_ap, brbuf, dst) in enumerate(bsrc):
            nxt = (lambda i=i: load_norm_T(bsrc[i + 1][0], bsrc[i + 1][1])) if i + 1 < len(bsrc) else None
            fr = b_tile(fr, P, smp, br_ap, brbuf, dst, nxt)
        fsets.free()
        for t_ in (KT, Vb, Wqg, Wo, gb_pre, gb_post, uyP, uyS, ys):
            k.release(t_)

    SC = P8 = None
    Wa_pre = None
    if "l0" in parts and "setup" in parts:
        Wa_pre = load_w("Wa", w_in_a, D, INA)
    kv_pre = []

    def early_kv():
        gb_mem0 = load_gb(4)
        kv_pre.extend(mem_kv(0, gb_mem0))
        k.release(gb_mem0)

    if "setup" in parts:
        SC, P8 = ssm_setup(stop, after_tables=(early_kv if "l0" in parts else None))
    if "l0" in parts:
        layer0(Wa_pre, kv_pre)
    if "l1" in parts:
        layer1(SC, P8)

    k.finish()
    return nc


_PROGRAM = None
_BUILD_ARGS = None
_LAST = None


def _get_program():
    global _PROGRAM
    if _PROGRAM is None:
        _PROGRAM = build_program()
    return _PROGRAM


def _l2(x):
    return np.ascontiguousarray(x.reshape(2, 48, 2, 64).transpose(2, 3, 0, 1).reshape(128, 2, 48))


def kernel(x_prompt, x_sample, cache_mem_k, cache_mem_v, state_ssm_re, state_ssm_im, mem_prompt,
           w_in_a, ln_v_g, ln_v_b, w_spatial, b_spatial,
           w_in_b, ssm_lambda_re, ssm_lambda_im, ssm_log_dt, ssm_b_re, ssm_b_im, ssm_c_re, ssm_c_im,
           ssm_d, w_glu, b_glu,
           mem_norm_g, w_mem_k, w_mem_v, w_out, pre_norm_g, post_norm_g):
    import ml_dtypes
    f32 = np.float32
    A = lambda a: np.ascontiguousarray(np.asarray(a), dtype=f32)
    gvec = np.concatenate([A(pre_norm_g), A(post_norm_g), A(mem_norm_g)], axis=0)
    shared = {
        "w_mem_k": A(w_mem_k), "w_mem_v": A(w_mem_v), "gvec": gvec,
        "w_in_a": A(w_in_a)[0], "w_out": A(w_out),
        "lnv": np.concatenate([A(ln_v_g), A(ln_v_b)], axis=0),
        "wsT": np.ascontiguousarray(A(w_spatial)[0].transpose(2, 0, 1)),
        "trilT": np.triu(np.ones((128, 128), f32)),
        "bsT": np.ascontiguousarray(A(b_spatial)[0].T),
        "wsamp": np.ascontiguousarray(np.broadcast_to(
            A(w_spatial)[0][:, :4, :4].transpose(2, 0, 1)[:, None, :, :, None], (4, 16, 8, 4, 16)).reshape(64, 8, 64)),
        "mask_s": np.ascontiguousarray((np.triu(np.ones((4, 4), f32))[:, None, :, None]
                                        * np.eye(16, dtype=f32)[None, :, None, :]).reshape(64, 64)),
        "bs_s": np.ascontiguousarray(np.repeat(A(b_spatial)[0][:, :4].T, 16, axis=0)),
        "w_in_b": A(w_in_b)[0], "w_glu": A(w_glu)[0], "bglu": A(b_glu),
        "lam2": _l2(np.stack([A(ssm_lambda_re)[0], A(ssm_lambda_im)[0]], 0)),
        "ldt2": np.ascontiguousarray(np.broadcast_to(A(ssm_log_dt)[0].reshape(48, 2).T[:, None, :], (2, 64, 48)).reshape(128, 48)),
        "B2": np.ascontiguousarray(np.stack([A(ssm_b_re)[0], A(ssm_b_im)[0]], 0).reshape(2, 48, 2, 64, 16)
                                   .transpose(2, 3, 0, 1, 4).reshape(128, 2, 48, 16)),
        "C2": np.ascontiguousarray(np.stack([A(ssm_c_re)[0], A(ssm_c_im)[0]], 0).reshape(2, 48, 2, 16, 64)
                                   .transpose(2, 4, 0, 1, 3).reshape(128, 2, 48, 16)),
        "dcol": np.ascontiguousarray(np.tile(A(ssm_d)[0].reshape(96, 16).T, (8, 1))),
        "maskJ": np.ascontiguousarray(np.kron(np.triu(np.ones((8, 8), f32)), np.ones((16, 16), f32))),
        "ident_f": np.eye(128, dtype=f32), "ident_b": np.eye(128, dtype=f32).astype(ml_dtypes.bfloat16),
    }
    x_prompt = A(x_prompt); x_sample = A(x_sample); mem_prompt = A(mem_prompt)
    cache_mem_k = A(cache_mem_k); cache_mem_v = A(cache_mem_v)
    state_ssm_re = A(state_ssm_re); state_ssm_im = A(state_ssm_im)
    in_maps = []
    for c in range(NCORES):
        m = dict(shared)
        m["xp"] = x_prompt[c]
        xs_c = x_sample[c * NSEQ:(c + 1) * NSEQ]
        m["xs"] = np.ascontiguousarray(xs_c.transpose(1, 0, 2).reshape(NS, D))
        m["mem"] = mem_prompt[c]
        m["st"] = np.ascontiguousarray(np.stack([state_ssm_re[0, c * NSEQ:(c + 1) * NSEQ].reshape(NSEQ, 6144),
                                                 state_ssm_im[0, c * NSEQ:(c + 1) * NSEQ].reshape(NSEQ, 6144)], axis=1))
        m["kc"] = np.ascontiguousarray(cache_mem_k[:, c * NSEQ:(c + 1) * NSEQ].reshape(2, NSEQ, NMEM, XATT))
        m["vc"] = np.ascontiguousarray(cache_mem_v[:, c * NSEQ:(c + 1) * NSEQ].reshape(2, NSEQ, NMEM, XATT))
        in_maps.append(m)
    if _BUILD_ARGS is not None:
        nc = build_program(**_BUILD_ARGS)
        res = run_bass_kernel_spmd(nc, in_maps[:1], core_ids=[0])
        global _LAST
        _LAST = res.results
        return None
    nc = _get_program()
    res = run_bass_kernel_spmd(nc, in_maps, core_ids=list(range(NCORES)))
    R = res.results
    y_prompt = np.stack([R[c]["o_yp"] for c in range(NCORES)], axis=0)
    y_sample = np.concatenate([R[c]["o_ys"].reshape(4, NSEQ, D).transpose(1, 0, 2) for c in range(NCORES)], axis=0)
    mk = np.stack([R[c]["o_mk"] for c in range(NCORES)], axis=1).reshape(2, NCORES, NMEM, 4, 128)
    mv = np.stack([R[c]["o_mv"] for c in range(NCORES)], axis=1).reshape(2, NCORES, NMEM, 4, 128)
    hp_re = np.stack([R[c]["o_hp"][0].reshape(96, 64) for c in range(NCORES)], axis=0)[None]
    hp_im = np.stack([R[c]["o_hp"][1].reshape(96, 64) for c in range(NCORES)], axis=0)[None]
    hs_re = np.concatenate([R[c]["o_hs"][:, 0].reshape(NSEQ, 96, 64) for c in range(NCORES)], axis=0)[None]
    hs_im = np.concatenate([R[c]["o_hs"][:, 1].reshape(NSEQ, 96, 64) for c in range(NCORES)], axis=0)[None]
    v_s = np.concatenate([R[c]["o_vs"].reshape(4, NSEQ, BR).transpose(1, 0, 2) for c in range(NCORES)], axis=0)[None]
    return (y_prompt, y_sample, mk, mv, hp_re, hp_im, hs_re, hs_im, v_s)
```
